# Optimizing a Trainium2 kernel written in Bass

```python
import jax, jax.numpy as jnp
from jax import lax
import numpy as np

D_MODEL = 1024
BATCH = 8
SEQ = 2048
DEPTH = 4

HEAD_DIM = 64
ATTN_WIDTH = D_MODEL
N_Q_HEADS = ATTN_WIDTH // HEAD_DIM
N_KV_HEADS = N_Q_HEADS // 4
Q_PER_KV = N_Q_HEADS // N_KV_HEADS
KV_WIDTH = N_KV_HEADS * HEAD_DIM
WINDOW = 128
ATTN_BLOCK = 128

D_INNER = D_MODEL
SSM_HEAD_DIM = 64
N_SSM_HEADS = D_INNER // SSM_HEAD_DIM
N_SSM_GROUPS = 2
SSM_HEADS_PER_GROUP = N_SSM_HEADS // N_SSM_GROUPS
D_STATE = 128
CONV_WIDTH = 4
CHUNK = 128
CONV_CH = D_INNER + 2 * N_SSM_GROUPS * D_STATE

MIX_WIDTH = ATTN_WIDTH + D_INNER
IN_SPLITS = tuple(int(s) for s in np.cumsum([ATTN_WIDTH, KV_WIDTH, KV_WIDTH, ATTN_WIDTH, D_INNER, CONV_CH]))
IN_WIDTH = IN_SPLITS[-1] + N_SSM_HEADS

DEEPNORM_ALPHA = (2.0 * DEPTH) ** 0.25
DEEPNORM_BETA = (8.0 * DEPTH) ** -0.25
LN_EPS = 1e-5
RMS_EPS = 1e-5

kernel_name = "hybrid_swa_sink_ssd_deepnorm"


def alibi_slopes():
    h = jnp.arange(1, N_Q_HEADS + 1, dtype=jnp.float32)
    return jnp.exp2(-8.0 * h / N_Q_HEADS)


def sliding_window_attention(q, k, v, sinks):
    b, l, _ = q.shape
    nb = l // ATTN_BLOCK
    q = q.reshape(b, nb, ATTN_BLOCK, N_KV_HEADS, Q_PER_KV, HEAD_DIM)

    def band(t):
        t = t.reshape(b, l, N_KV_HEADS, HEAD_DIM)
        t = jnp.pad(t, ((0, 0), (ATTN_BLOCK, 0), (0, 0), (0, 0)))
        t = t.reshape(b, nb + 1, ATTN_BLOCK, N_KV_HEADS, HEAD_DIM)
        return jnp.concatenate([t[:, :-1], t[:, 1:]], axis=2)

    kb, vb = band(k), band(v)
    scores = jnp.einsum("bnqkgd,bnskd->bnkgqs", q, kb).astype(jnp.float32) * (HEAD_DIM ** -0.5)
    qi = jnp.arange(ATTN_BLOCK)[:, None]
    sj = jnp.arange(2 * ATTN_BLOCK)[None, :]
    dist = qi - sj + ATTN_BLOCK
    blk = jnp.arange(nb)[:, None, None]
    valid = (dist >= 0) & (dist < WINDOW) & (blk * ATTN_BLOCK + sj - ATTN_BLOCK >= 0)
    slopes = alibi_slopes().reshape(N_KV_HEADS, Q_PER_KV)[:, :, None, None]
    scores = scores - slopes * dist.astype(jnp.float32)
    scores = jnp.where(valid[None, :, None, None], scores, -jnp.inf)
    sink = sinks.astype(jnp.float32).reshape(N_KV_HEADS, Q_PER_KV)[:, :, None, None]
    m = jnp.maximum(scores.max(axis=-1, keepdims=True), sink)
    p = jnp.exp(scores - m)
    probs = (p / (p.sum(axis=-1, keepdims=True) + jnp.exp(sink - m))).astype(v.dtype)
    out = jnp.einsum("bnkgqs,bnskd->bnqkgd", probs, vb)
    return out.reshape(b, l, ATTN_WIDTH)


def causal_depthwise_conv(u, w, bias):
    out = lax.conv_general_dilated(u, w[:, None, :], window_strides=(1,),
                                   padding=[(CONV_WIDTH - 1, 0)],
                                   dimension_numbers=("NWC", "WIO", "NWC"),
                                   feature_group_count=u.shape[-1])
    return out + bias


def exp_segsum(a_cs):
    n = a_cs.shape[-1]
    diff = a_cs[..., :, None] - a_cs[..., None, :]
    mask = jnp.tril(jnp.ones((n, n), dtype=bool))
    return jnp.exp(jnp.where(mask, diff, -jnp.inf))


def ssd_chunked(xs, dt, a, bm, cm):
    b, l, _ = xs.shape
    nc = l // CHUNK
    G, HG, P, N = N_SSM_GROUPS, SSM_HEADS_PER_GROUP, SSM_HEAD_DIM, D_STATE
    xc = xs.reshape(b, nc, CHUNK, G, HG, P)
    dtc = dt.reshape(b, nc, CHUNK, G, HG)
    bc = bm.reshape(b, nc, CHUNK, G, N)
    cc = cm.reshape(b, nc, CHUNK, G, N)
    a_cs = jnp.moveaxis(jnp.cumsum(dtc * a.reshape(G, HG), axis=2), 2, -1)
    xdt = xc * dtc[..., None]
    cb = jnp.einsum("bclgn,bcsgn->bcgls", cc, bc)
    y_diag = jnp.einsum("bcgls,bcghls,bcsghp->bclghp", cb, exp_segsum(a_cs), xdt)
    decay_to_end = jnp.exp(a_cs[..., -1:] - a_cs)
    states = jnp.einsum("bclgn,bcghl,bclghp->bcghpn", bc, decay_to_end, xdt)
    chunk_decay = jnp.exp(a_cs[..., -1])

    def step(carry, inp):
        st, dec = inp
        return carry * dec[..., None, None] + st, carry

    init = jnp.zeros(states.shape[:1] + states.shape[2:], states.dtype)
    _, prev = lax.scan(step, init, (jnp.moveaxis(states, 1, 0), jnp.moveaxis(chunk_decay, 1, 0)))
    prev = jnp.moveaxis(prev, 0, 1)
    y_off = jnp.einsum("bclgn,bcghpn,bcghl->bclghp", cc, prev, jnp.exp(a_cs))
    return (y_diag + y_off).astype(xs.dtype).reshape(b, l, D_INNER)


def gated_rmsnorm(y, z, w):
    b, l, _ = y.shape
    g = (y * jax.nn.silu(z)).astype(jnp.float32).reshape(b, l, N_SSM_GROUPS, D_INNER // N_SSM_GROUPS)
    g = g * lax.rsqrt(jnp.mean(g * g, axis=-1, keepdims=True) + RMS_EPS)
    return (g.reshape(b, l, D_INNER) * w.astype(jnp.float32)).astype(y.dtype)


def layer_norm(x, g, bias):
    xf = x.astype(jnp.float32)
    mu = jnp.mean(xf, axis=-1, keepdims=True)
    var = jnp.mean(jnp.square(xf - mu), axis=-1, keepdims=True)
    return ((xf - mu) * lax.rsqrt(var + LN_EPS) * g + bias).astype(x.dtype)


def hybrid_layer(x, w_in, conv_w, conv_b, dt_bias, a_log, d_skip, ssm_norm_w, sinks, w_out, ln_g, ln_b):
    b, l, _ = x.shape
    proj = jnp.einsum("bld,de->ble", x, w_in)
    q, k, v, z_attn, z_ssm, xbc, dt_raw = jnp.split(proj, IN_SPLITS, axis=-1)
    attn = sliding_window_attention(q, k, v, sinks) * jax.nn.silu(z_attn)
    xbc = jax.nn.silu(causal_depthwise_conv(xbc, conv_w, conv_b))
    xs, bm, cm = jnp.split(xbc, [D_INNER, D_INNER + N_SSM_GROUPS * D_STATE], axis=-1)
    dt = jax.nn.softplus(dt_raw.astype(jnp.float32) + dt_bias.astype(jnp.float32))
    a = -jnp.exp(a_log.astype(jnp.float32))
    y = ssd_chunked(xs, dt, a, bm, cm)
    y = y + (xs.reshape(b, l, N_SSM_HEADS, SSM_HEAD_DIM) * d_skip[:, None]).reshape(b, l, D_INNER)
    ssm = gated_rmsnorm(y, z_ssm, ssm_norm_w)
    out = jnp.einsum("ble,ed->bld", jnp.concatenate([attn, ssm], axis=-1), w_out)
    return layer_norm(DEEPNORM_ALPHA * x + out, ln_g, ln_b)


def setup_inputs(seed: int = 0) -> dict:
    key = jax.random.key(seed)
    ks = jax.random.split(key, 12)
    f32 = jnp.float32
    x = jax.random.normal(ks[0], (BATCH, SEQ, D_MODEL), f32)
    w_in = jax.random.normal(ks[1], (DEPTH, D_MODEL, IN_WIDTH), f32) * D_MODEL ** -0.5
    conv_w = jax.random.normal(ks[2], (DEPTH, CONV_WIDTH, CONV_CH), f32) * CONV_WIDTH ** -0.5
    conv_b = jax.random.normal(ks[3], (DEPTH, CONV_CH), f32) * 0.02
    dt0 = jnp.exp(jax.random.uniform(ks[4], (DEPTH, N_SSM_HEADS), f32) * (jnp.log(0.1) - jnp.log(0.001)) + jnp.log(0.001))
    dt_bias = dt0 + jnp.log(-jnp.expm1(-dt0))
    a_log = jnp.log(jax.random.uniform(ks[5], (DEPTH, N_SSM_HEADS), f32, 1.0, 16.0))
    d_skip = 1.0 + 0.1 * jax.random.normal(ks[6], (DEPTH, N_SSM_HEADS), f32)
    ssm_norm_w = 1.0 + 0.02 * jax.random.normal(ks[7], (DEPTH, D_INNER), f32)
    sinks = jax.random.normal(ks[8], (DEPTH, N_Q_HEADS), f32)
    w_out = jax.random.normal(ks[9], (DEPTH, MIX_WIDTH, D_MODEL), f32) * (MIX_WIDTH ** -0.5 * DEEPNORM_BETA)
    ln_g = 1.0 + 0.02 * jax.random.normal(ks[10], (DEPTH, D_MODEL), f32)
    ln_b = 0.02 * jax.random.normal(ks[11], (DEPTH, D_MODEL), f32)
    return {"x": x, "w_in": w_in, "conv_w": conv_w, "conv_b": conv_b, "dt_bias": dt_bias,
            "a_log": a_log, "d_skip": d_skip, "ssm_norm_w": ssm_norm_w, "sinks": sinks,
            "w_out": w_out, "ln_g": ln_g, "ln_b": ln_b}


def reference(x, w_in, conv_w, conv_b, dt_bias, a_log, d_skip, ssm_norm_w, sinks, w_out, ln_g, ln_b):
    h = x
    for i in range(DEPTH):
        h = hybrid_layer(h, w_in[i], conv_w[i], conv_b[i], dt_bias[i], a_log[i], d_skip[i],
                         ssm_norm_w[i], sinks[i], w_out[i], ln_g[i], ln_b[i])
    return h
```

```python
from contextlib import ExitStack
import numpy as np
import ml_dtypes
import concourse.bass as bass
import concourse.mybir as mybir
from concourse.bass_utils import run_bass_kernel_spmd

F32 = mybir.dt.float32
BF16 = mybir.dt.bfloat16
AF = mybir.ActivationFunctionType
ALU = mybir.AluOpType

D_MODEL = 1024
SEQ = 2048
DEPTH = 4
NT = 16
NG = 4
HEAD = 64
ALPHA = (2.0 * DEPTH) ** 0.25
LN_EPS = 1e-5
RMS_EPS = 1e-5
N_W_CHUNKS = 10


class Res:
    __slots__ = ("w", "r")

    def __init__(self):
        self.w = None
        self.r = {}


class Sched:
    ENGS = ("pe", "act", "dve", "pool", "sp")

    def __init__(self, nc, st, ndma=8):
        self.nc = nc
        self.eng = {"pe": nc.tensor, "act": nc.scalar, "dve": nc.vector, "pool": nc.gpsimd, "sp": nc.sync}
        self.cnt = {e: 0 for e in self.ENGS}
        self.waited = {e: {} for e in self.ENGS}
        self.sems = {}
        for e in self.ENGS:
            self.sems["s_" + e] = st.enter_context(nc.semaphore("s_" + e))
        self.ndma = ndma
        self.dma_names = {}
        self.dma_use = {}
        self.dma_rr = {}
        for q in ("sp", "act", "pool"):
            self.dma_names[q] = ["d_%s_%d" % (q, i) for i in range(ndma)]
            for n in self.dma_names[q]:
                self.sems[n] = st.enter_context(nc.semaphore(n))
            self.dma_use[q] = [0] * ndma
            self.dma_rr[q] = 0
        self.res = {}
        self.pending = {e: [] for e in self.ENGS}
        self.all_dma_toks = []
        self.n_ins = 0

    def R(self, name):
        r = self.res.get(name)
        if r is None:
            r = Res()
            self.res[name] = r
        return r

    def _deps(self, eng, reads, writes):
        need = {}
        for r in reads:
            t = self.R(r).w
            if t is not None and need.get(t[0], 0) < t[1]:
                need[t[0]] = t[1]
        for w in writes:
            rr = self.R(w)
            t = rr.w
            if t is not None and need.get(t[0], 0) < t[1]:
                need[t[0]] = t[1]
            for k, v in rr.r.items():
                if need.get(k, 0) < v:
                    need[k] = v
        wd = self.waited[eng]
        e = self.eng[eng]
        for k, v in need.items():
            if k == "s_pe" and eng == "pe":
                continue
            if wd.get(k, 0) >= v:
                continue
            wd[k] = v
            e.wait_ge(self.sems[k], v)
            self.n_ins += 1

    def _commit(self, tok, reads, writes):
        k, v = tok
        for r in reads:
            d = self.R(r).r
            if d.get(k, 0) < v:
                d[k] = v
        for w in writes:
            rr = self.R(w)
            rr.w = tok
            rr.r = {}

    def op(self, eng, fn, reads=(), writes=(), inc=True):
        self._deps(eng, reads, writes)
        ins = fn(self.eng[eng])
        self.n_ins += 1
        if not inc:
            self.pending[eng].append((reads, writes))
            return None
        self.cnt[eng] += 1
        tok = ("s_" + eng, self.cnt[eng])
        ins.then_inc(self.sems[tok[0]], 1)
        for (r, w) in self.pending[eng]:
            self._commit(tok, r, w)
        self.pending[eng] = []
        self._commit(tok, reads, writes)
        return tok

    def dma(self, q, fn, reads=(), writes=()):
        assert not self.pending[q]
        i = self.dma_rr[q]
        self.dma_rr[q] = (i + 1) % self.ndma
        key = self.dma_names[q][i]
        self._deps(q, reads, writes)
        prev = self.dma_use[q][i] * 16
        if prev > 0 and self.waited[q].get(key, 0) < prev:
            self.waited[q][key] = prev
            self.eng[q].wait_ge(self.sems[key], prev)
        self.dma_use[q][i] += 1
        tok = (key, self.dma_use[q][i] * 16)
        fn(self.eng[q]).then_inc(self.sems[key], 16)
        self.n_ins += 1
        self._commit(tok, reads, writes)
        return tok

    def barrier(self):
        for e in self.ENGS:
            assert not self.pending[e]
        targets = {}
        for e in self.ENGS:
            if self.cnt[e]:
                targets["s_" + e] = self.cnt[e]
        for q in self.dma_names:
            for i, n in enumerate(self.dma_names[q]):
                if self.dma_use[q][i]:
                    targets[n] = self.dma_use[q][i] * 16
        for e in self.ENGS:
            wd = self.waited[e]
            for k, v in targets.items():
                if k == "s_" + e:
                    continue
                if wd.get(k, 0) < v:
                    wd[k] = v
                    self.eng[e].wait_ge(self.sems[k], v)
                    self.n_ins += 1

    def wait_tok(self, eng, tok):
        k, v = tok
        if self.waited[eng].get(k, 0) < v:
            self.waited[eng][k] = v
            self.eng[eng].wait_ge(self.sems[k], v)


def AP(t, off, pat):
    return bass.AP(t, off, [list(p) for p in pat])


def _attn_perm(kp):
    idx = []
    for j in range(4):
        for r in range(2):
            h = (2 * kp + r) * 4 + j
            idx.extend(range(h * 64, h * 64 + 64))
    return np.array(idx)


def _in_col_chunks():
    q0, k0, v0, za0, zs0, x0 = 0, 1024, 1280, 1536, 2560, 3584
    ch = []
    ch.append(np.concatenate([np.arange(k0, k0 + 256), np.arange(v0, v0 + 256)]))
    for kp in range(2):
        p = _attn_perm(kp)
        ch.append(q0 + p)
        ch.append(za0 + p)
    ch.append(np.arange(x0 + 1024, x0 + 1536))
    for g in range(2):
        ch.append(np.arange(x0 + g * 512, x0 + (g + 1) * 512))
        ch.append(np.arange(zs0 + g * 512, zs0 + (g + 1) * 512))
    return ch


def _consts():
    s = np.arange(128)[:, None].astype(np.float64)
    q = np.arange(128)[None, :].astype(np.float64)
    slopes = np.exp2(-8.0 * np.arange(1, 17) / 16.0)
    masks = np.zeros((128, 4, 2, 4, 128), np.float32)
    for kv in range(4):
        for g in range(4):
            sl = slopes[kv * 4 + g]
            masks[:, kv, 0, g, :] = np.where(s <= q, np.exp(-sl * (q - s)), 0.0)
            masks[:, kv, 1, g, :] = np.where(s > q, np.exp(-sl * (128.0 + q - s)), 0.0)
    tri_le = (s <= q).astype(np.float32)
    tri_st = (s > q).astype(np.float32)
    ident = np.eye(128, dtype=np.float32)
    ones = np.ones((128, 128), np.float32)
    cb16 = np.concatenate([ident, tri_le, tri_st, ones], axis=1).astype(ml_dtypes.bfloat16)
    cf32 = np.concatenate([tri_le, ones], axis=1).astype(np.float32)
    return masks.reshape(128, 4096).astype(ml_dtypes.bfloat16), cb16, cf32


def prep_weights(inp, n_layers):
    L = n_layers
    w_in = np.asarray(inp["w_in"], np.float32)[:L]
    w_out = np.asarray(inp["w_out"], np.float32)[:L]
    chunks = _in_col_chunks()
    w_in_p = np.empty((L, N_W_CHUNKS, 128, 8, 512), np.float32)
    for c, cols in enumerate(chunks):
        w = w_in[:, :, cols]
        w_in_p[:, c] = w.reshape(L, 8, 128, 512).transpose(0, 2, 1, 3)
    w_dt = np.ascontiguousarray(w_in[:, :, 5120:5136].reshape(L, 8, 128, 16).transpose(0, 2, 1, 3))
    rows = np.concatenate([_attn_perm(0), _attn_perm(1), np.arange(1024, 2048)])
    w_out_p = np.ascontiguousarray(w_out[:, rows, :].reshape(L, 4, 4, 128, 1024).transpose(0, 1, 3, 2, 4))
    conv_w = np.asarray(inp["conv_w"], np.float32)[:L]
    conv_wT = np.ascontiguousarray(conv_w.reshape(L, 4, 12, 128).transpose(0, 3, 2, 1))
    conv_bT = np.ascontiguousarray(np.asarray(inp["conv_b"], np.float32)[:L].reshape(L, 12, 128).transpose(0, 2, 1))

    def rep(a):
        a = np.asarray(a, np.float32)[:L]
        return np.ascontiguousarray(np.broadcast_to(a[:, None, :], (L, 128, a.shape[1])))
    small = np.concatenate([rep(inp["dt_bias"]), rep(inp["a_log"]), rep(inp["d_skip"]), rep(inp["sinks"])], axis=2)
    masks, cb16, cf32 = _consts()
    return {
        "w_in_p": w_in_p, "w_dt": w_dt, "w_out_p": w_out_p, "conv_wT": conv_wT, "conv_bT": conv_bT,
        "small": np.ascontiguousarray(small), "ln_g": rep(inp["ln_g"]), "ln_b": rep(inp["ln_b"]),
        "normw": rep(inp["ssm_norm_w"]), "masks": masks, "cb16": cb16, "cf32": cf32,
    }


def build_program(n_layers=DEPTH, stop=99):
    L = n_layers
    nc = bass.Bass("TRN2", target_bir_lowering=False)
    x_d = nc.dram_tensor("x", [SEQ, D_MODEL], F32, kind="ExternalInput")
    w_in_d = nc.dram_tensor("w_in_p", [L, N_W_CHUNKS, 128, 8 * 512], F32, kind="ExternalInput")
    w_dt_d = nc.dram_tensor("w_dt", [L, 128, 8 * 16], F32, kind="ExternalInput")
    w_out_d = nc.dram_tensor("w_out_p", [L, 4, 128, 4 * 1024], F32, kind="ExternalInput")
    convw_d = nc.dram_tensor("conv_wT", [L, 128, 48], F32, kind="ExternalInput")
    convb_d = nc.dram_tensor("conv_bT", [L, 128, 12], F32, kind="ExternalInput")
    small_d = nc.dram_tensor("small", [L, 128, 64], F32, kind="ExternalInput")
    lng_d = nc.dram_tensor("ln_g", [L, 128, 1024], F32, kind="ExternalInput")
    lnb_d = nc.dram_tensor("ln_b", [L, 128, 1024], F32, kind="ExternalInput")
    normw_d = nc.dram_tensor("normw", [L, 128, 1024], F32, kind="ExternalInput")
    masks_d = nc.dram_tensor("masks", [128, 4096], BF16, kind="ExternalInput")
    cb16_d = nc.dram_tensor("cb16", [128, 512], BF16, kind="ExternalInput")
    cf32_d = nc.dram_tensor("cf32", [128, 256], F32, kind="ExternalInput")
    y_d = nc.dram_tensor("y", [SEQ, D_MODEL], F32, kind="ExternalOutput")

    with ExitStack() as st:
        def sb(name, cols, dt):
            return st.enter_context(nc.sbuf_tensor("sb_" + name, [128, cols], dt))

        def ps(name, cols, dt):
            return st.enter_context(nc.psum_tensor("ps_" + name, [128, cols], dt))

        S = Sched(nc, st)
        acc = sb("acc", NT * 1024, F32)
        xT = sb("xT", 8 * SEQ, BF16)
        cb16 = sb("cb16", 512, BF16)
        cf32 = sb("cf32", 256, F32)
        identb = cb16[:, 0:128]
        trileb = cb16[:, 128:256]
        tristb = cb16[:, 256:384]
        onesb = cb16[:, 384:512]
        trilef = cf32[:, 0:128]
        onesf = cf32[:, 128:256]
        wbuf = [sb("wbuf%d" % i, 8 * 512, BF16) for i in range(2)]
        wob = sb("wob", 4 * 1024, BF16)
        wdt = sb("wdt", 128, BF16)
        lng = sb("lng", 1024, F32)
        lnb = sb("lnb", 1024, F32)
        normw = sb("normw", 1024, F32)
        convw = sb("convw", 48, F32)
        convb = sb("convb", 12, F32)
        small = sb("small", 64, F32)
        mix4 = sb("mix4", 4 * 512, BF16)
        xb16 = sb("xb16", 1024, BF16)
        stats = sb("stats", 12, F32)
        mv = sb("mv", 2, F32)
        rstd = sb("rstd", 1, F32)
        PH_BYTES = 60 * 1024
        ph8 = sb("phase", PH_BYTES // 2, BF16)
        ph32 = ph8.bitcast(F32)

        class Carver:
            def __init__(self):
                self.off = 0

            def bf(self, n):
                o = self.off
                self.off += 2 * n
                assert self.off <= PH_BYTES, self.off
                return ph8[:, o // 2: o // 2 + n], o // 2

            def f32(self, n):
                self.off = (self.off + 3) // 4 * 4
                o = self.off
                self.off += 4 * n
                assert self.off <= PH_BYTES, self.off
                return ph32[:, o // 4: o // 4 + n], o // 4

        pj = [ps("pj%d" % i, 512, F32) for i in range(2)]
        sc = [ps("sc%d" % i, 512, F32) for i in range(2)]
        nd = [ps("nd%d" % i, 512, F32) for i in range(2)]
        st0 = ps("st0", 512, F32)
        trT = ps("tr", 1024, BF16)
        tr = [trT[:, 0:512], trT[:, 512:1024]]
        pj_rr = [0]

        def next_pj():
            i = pj_rr[0]
            pj_rr[0] = 1 - i
            return pj[i], "pj%d" % i

        S.dma("sp", lambda e: e.dma_start(out=cb16[:], in_=cb16_d[:]), writes=["cb16"])
        S.dma("sp", lambda e: e.dma_start(out=cf32[:], in_=cf32_d[:]), writes=["cf32"])

        xT3 = lambda kc, c0, n: xT[:, kc * SEQ + c0: kc * SEQ + c0 + n]

        def load_w_chunk(l, c, buf_i):
            S.dma("pool", lambda e: e.dma_start(out=wbuf[buf_i][:], in_=w_in_d[l, c]), writes=["wbuf%d" % buf_i])

        def load_wo(l, qi):
            S.dma("pool", lambda e: e.dma_start(out=wob[:], in_=w_out_d[l, qi]), writes=["wob"])

        def mm_group(items, reads, writes):
            n = len(items)
            for i, (o, a, b) in enumerate(items):
                S.op("pe", lambda e, o=o, a=a, b=b, i=i: e.matmul(o, lhsT=a, rhs=b, start=(i == 0), stop=(i == n - 1)),
                     reads=reads, writes=writes, inc=(i == n - 1))

        def inproj_fm(wb_i, col0, tg, bank, bank_name):
            items = [(bank[:, 0:512], wbuf[wb_i][:, kc * 512 + col0: kc * 512 + col0 + 128], xT3(kc, tg * 512, 512))
                     for kc in range(8)]
            mm_group(items, reads=["wbuf%d" % wb_i] + ["xT%d" % t for t in range(4 * tg, 4 * tg + 4)], writes=[bank_name])

        def inproj_tm(wb_i, col0, ncols, t, bank, bank_name):
            items = [(bank[:, 0:ncols], xT3(kc, t * 128, 128), wbuf[wb_i][:, kc * 512 + col0: kc * 512 + col0 + ncols])
                     for kc in range(8)]
            mm_group(items, reads=["wbuf%d" % wb_i, "xT%d" % t], writes=[bank_name])

        def to_xT(t):
            S.op("act", lambda e: e.activation(out=xb16[:], in_=acc[:, t * 1024:(t + 1) * 1024], func=AF.Copy),
                 reads=["acc%d" % t], writes=["xb16"])
            for half in range(2):
                for j in range(4):
                    kc = half * 4 + j
                    S.op("pe", lambda e, kc=kc, j=j, half=half: e.transpose(tr[half][:, j * 128:(j + 1) * 128],
                                                                           xb16[:, kc * 128:(kc + 1) * 128], identb),
                         reads=["xb16", "cb16"], writes=["tr"], inc=(j == 3))
                o = AP(xT, half * 4 * SEQ + t * 128, [[8 * SEQ, 128], [SEQ, 4], [1, 128]])
                i_ = AP(trT, half * 512, [[1024, 128], [128, 4], [1, 128]])
                S.op("dve", lambda e, o=o, i_=i_: e.tensor_copy(out=o, in_=i_), reads=["tr"], writes=["xT%d" % t])

        def outproj_partial(l, first, tg):
            for b in range(4):
                t = 4 * tg + b
                for ch in range(2):
                    bank, bn = next_pj()
                    items = [(bank[:, 0:512], mix4[:, c * 512 + b * 128: c * 512 + (b + 1) * 128],
                              wob[:, c * 1024 + ch * 512: c * 1024 + (ch + 1) * 512]) for c in range(4)]
                    mm_group(items, reads=["mix4", "wob"], writes=[bn])
                    a_ = acc[:, t * 1024 + ch * 512: t * 1024 + (ch + 1) * 512]
                    if first:
                        S.op("dve", lambda e, a_=a_, bank=bank: e.scalar_tensor_tensor(
                            out=a_, in0=a_, scalar=float(ALPHA), in1=bank[:, 0:512], op0=ALU.mult, op1=ALU.add),
                            reads=[bn, "acc%d" % t], writes=["acc%d" % t])
                    else:
                        S.op("dve", lambda e, a_=a_, bank=bank: e.tensor_tensor(out=a_, in0=a_, in1=bank[:, 0:512], op=ALU.add),
                             reads=[bn, "acc%d" % t], writes=["acc%d" % t])

        def layernorm_tile(l, t, last):
            a_ = acc[:, t * 1024:(t + 1) * 1024]
            rw = ["acc%d" % t]
            for h in range(2):
                S.op("dve", lambda e, h=h: e.bn_stats(out=stats[:, h * 6:(h + 1) * 6], in_=acc[:, t * 1024 + h * 512: t * 1024 + (h + 1) * 512]),
                     reads=rw, writes=["stats"])
            S.op("dve", lambda e: e.bn_aggr(out=mv[:], in_=stats[:]), reads=["stats"], writes=["mv"])
            S.op("act", lambda e: e.activation(out=rstd[:], in_=mv[:, 1:2], func=AF.Sqrt, bias=float(LN_EPS), scale=1.0),
                 reads=["mv"], writes=["rstd"])
            S.op("dve", lambda e: e.reciprocal(out=rstd[:], in_=rstd[:]), reads=["rstd"], writes=["rstd"])
            S.op("dve", lambda e: e.tensor_scalar(out=a_, in0=a_, scalar1=mv[:, 0:1], scalar2=rstd[:, 0:1],
                                                  op0=ALU.subtract, op1=ALU.mult), reads=rw + ["mv", "rstd"], writes=rw)
            S.op("pool", lambda e: e.tensor_tensor(out=a_, in0=a_, in1=lng[:], op=ALU.mult), reads=rw + ["lng"], writes=rw)
            S.op("pool", lambda e: e.tensor_tensor(out=a_, in0=a_, in1=lnb[:], op=ALU.add), reads=rw + ["lnb"], writes=rw)
            if last:
                S.all_dma_toks.append(S.dma("sp", lambda e: e.dma_start(out=y_d[t * 128:(t + 1) * 128, :], in_=a_), reads=rw))
            else:
                to_xT(t)

        def finalize():
            for t in range(NT):
                S.all_dma_toks.append(S.dma("sp", lambda e, t=t: e.dma_start(out=y_d[t * 128:(t + 1) * 128, :], in_=acc[:, t * 1024:(t + 1) * 1024]),
                                            reads=["acc%d" % t]))
            S.barrier()
            for tok in S.all_dma_toks:
                S.wait_tok("sp", tok)
            print("instructions emitted (stopped):", S.n_ins)
            return nc

        for t in range(NT):
            S.dma("sp", lambda e, t=t: e.dma_start(out=acc[:, t * 1024:(t + 1) * 1024], in_=x_d[t * 128:(t + 1) * 128, :]),
                  writes=["acc%d" % t])
        for t in range(NT):
            to_xT(t)

        if stop == 0:
            return finalize()
        for l in range(L):
            last = (l == L - 1)
            S.dma("sp", lambda e: e.dma_start(out=lng[:], in_=lng_d[l]), writes=["lng"])
            S.dma("sp", lambda e: e.dma_start(out=lnb[:], in_=lnb_d[l]), writes=["lnb"])
            S.dma("sp", lambda e: e.dma_start(out=normw[:], in_=normw_d[l]), writes=["normw"])
            S.dma("sp", lambda e: e.dma_start(out=convw[:], in_=convw_d[l]), writes=["convw"])
            S.dma("sp", lambda e: e.dma_start(out=convb[:], in_=convb_d[l]), writes=["convb"])
            S.dma("sp", lambda e: e.dma_start(out=small[:], in_=small_d[l]), writes=["small"])
            S.dma("pool", lambda e: e.dma_start(out=wdt[:], in_=w_dt_d[l]), writes=["wdt"])
            dtb = small[:, 0:16]
            alog = small[:, 16:32]
            dsk = small[:, 32:48]
            snk = small[:, 48:64]

            cv = Carver()
            masks, _ = cv.bf(4096)
            kT, kT_o = cv.bf(2 * SEQ)
            vt, vt_o = cv.bf(NT * 256)
            q4, q4_o = cv.bf(4 * 512)
            z4, z4_o = cv.bf(4 * 512)
            Eb = [cv.bf(512) for _ in range(4)]
            sinkexp, se_o = cv.f32(2 * 512)
            dS, dS_o = cv.f32(512)
            wgt, wgt_o = cv.f32(512)
            es, es_o = cv.f32(16)
            S.barrier()
            S.dma("sp", lambda e: e.dma_start(out=masks, in_=masks_d[:]), reads=[], writes=["masks"])
            load_w_chunk(l, 0, 0)
            load_w_chunk(l, 1, 1)
            S.op("act", lambda e: e.activation(out=es, in_=snk, func=AF.Exp), reads=["small"], writes=["es"])
            for kp in range(2):
                for r in range(2):
                    o = AP(ph32, se_o + kp * 512 + r * 64 * (PH_BYTES // 4), [[PH_BYTES // 4, 64], [128, 4], [1, 128]])
                    hh = (2 * kp + r) * 4
                    i_ = AP(ph32, es_o + hh + r * 64 * (PH_BYTES // 4), [[PH_BYTES // 4, 64], [1, 4], [0, 128]])
                    S.op("dve", lambda e, o=o, i_=i_: e.tensor_copy(out=o, in_=i_), reads=["es"], writes=["sinkexp"])
            for tg in range(NG):
                for c in range(2):
                    bank, bn = next_pj()
                    inproj_fm(0, c * 128, tg, bank, bn)
                    S.op("dve", lambda e, c=c, tg=tg, bank=bank: e.tensor_copy(
                        out=kT[:, c * SEQ + tg * 512: c * SEQ + (tg + 1) * 512], in_=bank[:, 0:512]),
                        reads=[bn], writes=["kT%d" % tg])
            for t in range(NT):
                bank, bn = next_pj()
                inproj_tm(0, 256, 256, t, bank, bn)
                S.op("act", lambda e, t=t, bank=bank: e.activation(out=vt[:, t * 256:(t + 1) * 256], in_=bank[:, 0:256], func=AF.Copy),
                     reads=[bn], writes=["v%d" % t])
            if stop == 1:
                return finalize()
            for kp in range(2):
                wq = 1
                load_w_chunk(l, 2 + 2 * kp, 0)
                load_wo(l, kp)
                for tg in range(NG):
                    for j in range(4):
                        bank, bn = next_pj()
                        inproj_fm(wq, j * 128, tg, bank, bn)
                        S.op("dve", lambda e, j=j, bank=bank: e.tensor_copy(out=q4[:, j * 512:(j + 1) * 512], in_=bank[:, 0:512]),
                             reads=[bn], writes=["q4"])
                    for j in range(4):
                        bank, bn = next_pj()
                        inproj_fm(0, j * 128, tg, bank, bn)
                        S.op("act", lambda e, j=j, bank=bank: e.activation(out=z4[:, j * 512:(j + 1) * 512], in_=bank[:, 0:512], func=AF.Silu),
                             reads=[bn], writes=["z4"])
                    for b in range(4):
                        n = 4 * tg + b
                        ei = 0
                        for r in range(2):
                            kv = 2 * kp + r
                            rows = slice(r * 64, r * 64 + 64)
                            blocks = ([n - 1] if n > 0 else []) + [n]
                            for bi, sblk in enumerate(blocks):
                                pc = 0 if sblk == n else 1
                                E, _eo = Eb[ei]
                                en = "E%d" % ei
                                ei += 1
                                scb = sc[(ei) % 2]
                                scn = "sc%d" % (ei % 2)
                                lhsT = kT[rows, kp * SEQ + sblk * 128: kp * SEQ + (sblk + 1) * 128]
                                rhs = AP(ph8, q4_o + b * 128 + r * 64 * (PH_BYTES // 2), [[PH_BYTES // 2, 64], [512, 4], [1, 128]])
                                S.op("pe", lambda e, scb=scb, lhsT=lhsT, rhs=rhs: e.matmul(scb[:, 0:512], lhsT=lhsT, rhs=rhs, start=True, stop=True),
                                     reads=["kT%d" % (sblk // 4), "q4"], writes=[scn])
                                S.op("act", lambda e, E=E, scb=scb: e.activation(out=E, in_=scb[:, 0:512], func=AF.Exp, scale=0.125),
                                     reads=[scn], writes=[en])
                                m_ = masks[:, (kv * 2 + pc) * 512:(kv * 2 + pc + 1) * 512]
                                S.op("dve", lambda e, E=E, m_=m_: e.tensor_tensor(out=E, in0=E, in1=m_, op=ALU.mult),
                                     reads=[en, "masks"], writes=[en])
                                first_b = (bi == 0)
                                last_b = (bi == len(blocks) - 1)
                                vl = vt[:, sblk * 256 + kv * 64: sblk * 256 + (kv + 1) * 64]
                                S.op("pe", lambda e, vl=vl, E=E, rows=rows, first_b=first_b, last_b=last_b: e.matmul(
                                    nd[0][rows, 0:512], lhsT=vl, rhs=E, start=first_b, stop=last_b),
                                    reads=[en, "v%d" % sblk], writes=["nd0"], inc=False)
                                S.op("pe", lambda e, E=E, rows=rows, first_b=first_b, last_b=last_b: e.matmul(
                                    nd[1][rows, 0:512], lhsT=onesb[:, 0:64], rhs=E, start=first_b, stop=last_b),
                                    reads=[en, "cb16"], writes=["nd1"], inc=True)
                        S.op("dve", lambda e: e.tensor_tensor(out=dS, in0=nd[1][:, 0:512], in1=sinkexp[:, kp * 512:(kp + 1) * 512], op=ALU.add),
                             reads=["nd1", "sinkexp"], writes=["dS"])
                        S.op("dve", lambda e: e.reciprocal(out=dS, in_=dS), reads=["dS"], writes=["dS"])
                        zv = AP(ph8, z4_o + b * 128, [[PH_BYTES // 2, 128], [512, 4], [1, 128]])
                        d3 = AP(ph32, dS_o, [[PH_BYTES // 4, 128], [128, 4], [1, 128]])
                        w3 = AP(ph32, wgt_o, [[PH_BYTES // 4, 128], [128, 4], [1, 128]])
                        S.op("pool", lambda e, zv=zv, d3=d3, w3=w3: e.tensor_tensor(out=w3, in0=d3, in1=zv, op=ALU.mult),
                             reads=["dS", "z4"], writes=["wgt"])
                        mo = AP(mix4, b * 128, [[2048, 128], [512, 4], [1, 128]])
                        n3 = AP(nd[0], 0, [[512, 128], [128, 4], [1, 128]])
                        S.op("dve", lambda e, mo=mo, n3=n3, w3=w3: e.tensor_tensor(out=mo, in0=n3, in1=w3, op=ALU.mult),
                             reads=["nd0", "wgt"], writes=["mix4"])
                    if tg == NG - 1 and kp == 0:
                        load_w_chunk(l, 3, 1)
                    outproj_partial(l, kp == 0, tg)
            if stop == 2:
                return finalize()
            cv = Carver()
            bcT, bc_o = cv.bf(4 * SEQ)
            ubuf, u_o = cv.bf(4 * 515)
            xsT, xsT_o = cv.bf(4 * 512)
            diag, diag_o = cv.bf(16 * 128)
            xdt, _ = cv.bf(512)
            xdtd, _ = cv.bf(512)
            xsD, _ = cv.bf(512)
            zs, _ = cv.bf(512)
            Btm, _ = cv.bf(128)
            cbm, cbm_o = cv.bf(128)
            rhsD, rhsD_o = cv.bf(1024)
            E8, E8_o = cv.bf(1024)
            Sbf, _ = cv.bf(512)
            gn, _ = cv.bf(512)
            junk, _ = cv.bf(512)
            dt16, dt_o = cv.f32(256)
            dtA, dtA_o = cv.f32(256)
            ea, ea_o = cv.f32(256)
            dtdte, dd_o = cv.f32(256)
            cd, cd_o = cv.f32(256)
            tmpA, _ = cv.f32(256)
            tmpB, _ = cv.f32(256)
            nega, nega_o2 = cv.f32(16)
            S32, _ = cv.f32(512)
            t1, _ = cv.f32(512)
            yb, _ = cv.f32(512)
            ss, _ = cv.f32(1)
            rs, _ = cv.f32(1)
            PB = PH_BYTES
            S.barrier()
            S.op("pool", lambda e: e.memset(ubuf, 0.0), reads=[], writes=["ubuf"])
            load_w_chunk(l, 5, 1)
            for t in range(NT):
                items = [(st0[:, t * 16:(t + 1) * 16], xT3(kc, t * 128, 128), wdt[:, kc * 16:(kc + 1) * 16]) for kc in range(8)]
                mm_group(items, reads=["wdt", "xT%d" % t], writes=["st0"])
            dtb_b = AP(small, 0, [[64, 128], [0, 16], [1, 16]])
            v3 = lambda o_: AP(ph32, o_, [[PB // 4, 128], [16, 16], [1, 16]])
            S.op("dve", lambda e: e.tensor_tensor(out=v3(dt_o), in0=AP(st0, 0, [[512, 128], [16, 16], [1, 16]]), in1=dtb_b, op=ALU.add),
                 reads=["st0", "small"], writes=["dt16"])
            S.op("dve", lambda e: e.tensor_scalar(out=tmpA, in0=dt16, scalar1=-1.0, scalar2=None, op0=ALU.mult), reads=["dt16"], writes=["tmpA"])
            S.op("dve", lambda e: e.tensor_tensor(out=tmpA, in0=tmpA, in1=dt16, op=ALU.max), reads=["dt16", "tmpA"], writes=["tmpA"])
            S.op("act", lambda e: e.activation(out=tmpA, in_=tmpA, func=AF.Exp, scale=-1.0), reads=["tmpA"], writes=["tmpA"])
            S.op("act", lambda e: e.activation(out=tmpA, in_=tmpA, func=AF.Ln, bias=1.0), reads=["tmpA"], writes=["tmpA"])
            S.op("dve", lambda e: e.scalar_tensor_tensor(out=dt16, in0=dt16, scalar=0.0, in1=tmpA, op0=ALU.max, op1=ALU.add),
                 reads=["dt16", "tmpA"], writes=["dt16"])
            S.op("act", lambda e: e.activation(out=nega, in_=alog, func=AF.Exp), reads=["small"], writes=["nega"])
            S.op("dve", lambda e: e.tensor_scalar(out=nega, in0=nega, scalar1=-1.0, scalar2=None, op0=ALU.mult), reads=["nega"], writes=["nega"])
            nega_o = nega_o2
            S.op("dve", lambda e: e.tensor_tensor(out=v3(dtA_o), in0=v3(dt_o), in1=AP(ph32, nega_o, [[PB // 4, 128], [0, 16], [1, 16]]), op=ALU.mult),
                 reads=["dt16", "nega"], writes=["dtA"])
            S.op("pe", lambda e: e.matmul(st0[:, 0:256], lhsT=trilef, rhs=dtA, start=True, stop=True), reads=["cf32", "dtA", "dt16"], writes=["st0"])
            S.op("pe", lambda e: e.matmul(st0[:, 256:512], lhsT=onesf, rhs=dtA, start=True, stop=True), reads=["cf32", "dtA"], writes=["st0"])
            S.op("act", lambda e: e.activation(out=ea, in_=st0[:, 0:256], func=AF.Exp), reads=["st0"], writes=["ea"])
            S.op("act", lambda e: e.activation(out=cd, in_=st0[:, 256:512], func=AF.Exp), reads=["st0"], writes=["cd"])
            S.op("act", lambda e: e.activation(out=tmpB, in_=st0[:, 0:256], func=AF.Identity), reads=["st0"], writes=["tmpB"])
            S.op("dve", lambda e: e.tensor_tensor(out=tmpB, in0=st0[:, 256:512], in1=tmpB, op=ALU.subtract), reads=["st0", "tmpB"], writes=["tmpB"])
            S.op("act", lambda e: e.activation(out=tmpB, in_=tmpB, func=AF.Exp), reads=["tmpB"], writes=["tmpB"])
            S.op("dve", lambda e: e.tensor_tensor(out=dtdte, in0=dt16, in1=tmpB, op=ALU.mult), reads=["dt16", "tmpB"], writes=["dtdte"])

            if stop == 3:
                return finalize()

            def build_diag(cc0):
                for j4 in range(4):
                    for tap in range(4):
                        o = diag[:, (j4 * 4 + tap) * 128:(j4 * 4 + tap + 1) * 128]
                        s_ = convw[:, (cc0 + j4) * 4 + tap:(cc0 + j4) * 4 + tap + 1]
                        S.op("pool", lambda e, o=o, s_=s_: e.tensor_scalar(out=o, in0=identb, scalar1=s_, scalar2=None, op0=ALU.mult),
                             reads=["cb16", "convw"], writes=["diag"])

            def conv_group(wb_i, cc0, tg, dst, dst_stride, dst_res):
                if tg > 0:
                    S.op("dve", lambda e: e.tensor_copy(out=AP(ph8, u_o, [[PB // 2, 128], [515, 4], [1, 3]]),
                                                        in_=AP(ph8, u_o + 512, [[PB // 2, 128], [515, 4], [1, 3]])),
                         reads=["ubuf"], writes=["ubuf"])
                else:
                    S.op("pool", lambda e: e.memset(AP(ph8, u_o, [[PB // 2, 128], [515, 4], [1, 3]]), 0.0), reads=[], writes=["ubuf"])
                for j in range(4):
                    bank, bn = next_pj()
                    inproj_fm(wb_i, j * 128, tg, bank, bn)
                    S.op("act", lambda e, j=j, bank=bank: e.activation(out=ubuf[:, j * 515 + 3: j * 515 + 515], in_=bank[:, 0:512], func=AF.Copy),
                         reads=[bn], writes=["ubuf"])
                for j in range(4):
                    bank, bn = next_pj()
                    items = [(bank[:, 0:512], diag[:, (j * 4 + tap) * 128:(j * 4 + tap + 1) * 128],
                              ubuf[:, j * 515 + tap: j * 515 + tap + 512]) for tap in range(4)]
                    mm_group(items, reads=["diag", "ubuf"], writes=[bn])
                    o = dst(j)
                    S.op("act", lambda e, o=o, bank=bank, j=j: e.activation(out=o, in_=bank[:, 0:512], func=AF.Silu,
                                                                             bias=convb[:, cc0 + j: cc0 + j + 1]),
                         reads=[bn, "convb"], writes=[dst_res])

            build_diag(8)
            for tg in range(NG):
                conv_group(1, 8, tg, lambda j, tg=tg: bcT[:, j * SEQ + tg * 512: j * SEQ + (tg + 1) * 512], None, "bcT%d" % tg)
            if stop == 4:
                return finalize()
            for g in range(2):
                load_w_chunk(l, 6 + 2 * g, 0)
                load_w_chunk(l, 7 + 2 * g, 1)
                load_wo(l, 2 + g)
                build_diag(4 * g)
                S.op("pool", lambda e: e.memset(S32, 0.0), reads=[], writes=["S32"])
                S.op("pool", lambda e: e.memset(Sbf, 0.0), reads=[], writes=["Sbf"])
                hb = lambda o_, t: AP(ph32, o_ + t * 16 + 8 * g, [[PB // 4, 128], [1, 8], [0, 64]])
                for tg in range(NG):
                    conv_group(0, 4 * g, tg, lambda j: xsT[:, j * 512:(j + 1) * 512], None, "xsT")
                    if stop == 5:
                        return finalize()
                    for b in range(4):
                        if stop == 6 and b == 1:
                            return finalize()
                        c = 4 * tg + b
                        t = c
                        for j in range(4):
                            S.op("pe", lambda e, j=j: e.transpose(tr[0][:, j * 128:(j + 1) * 128], xsT[:, j * 512 + b * 128: j * 512 + (b + 1) * 128], identb),
                                 reads=["xsT", "cb16"], writes=["tr"], inc=(j == 3))
                        tr3 = AP(trT, 0, [[1024, 128], [64, 8], [1, 64]])
                        for dst_, nm_, sc_o in ((xdt, "xdt", dt_o), (xdtd, "xdtd", dd_o)):
                            S.op("dve", lambda e, dst_=dst_, sc_o=sc_o: e.tensor_tensor(
                                out=dst_.rearrange("p (h d) -> p h d", h=8), in0=tr3, in1=hb(sc_o, t), op=ALU.mult),
                                reads=["tr", "dt16", "dtdte"], writes=[nm_])
                        dsk_b = AP(small, 32 + 8 * g, [[64, 128], [1, 8], [0, 64]])
                        S.op("dve", lambda e: e.tensor_tensor(out=xsD.rearrange("p (h d) -> p h d", h=8), in0=tr3, in1=dsk_b, op=ALU.mult),
                             reads=["tr", "small"], writes=["xsD"])
                        bank, bn = next_pj()
                        inproj_tm(1, 0, 512, t, bank, bn)
                        S.op("act", lambda e, bank=bank: e.activation(out=zs, in_=bank[:, 0:512], func=AF.Silu), reads=[bn], writes=["zs"])
                        S.op("pe", lambda e: e.transpose(tr[1][:, 0:128], bcT[:, g * SEQ + c * 128: g * SEQ + (c + 1) * 128], identb),
                             reads=["bcT%d" % tg, "cb16"], writes=["tr"])
                        S.op("act", lambda e: e.activation(out=Btm, in_=tr[1][:, 0:128], func=AF.Copy), reads=["tr"], writes=["Btm"])
                        bank, bn = next_pj()
                        S.op("pe", lambda e, bank=bank: e.matmul(bank[:, 0:128], lhsT=bcT[:, g * SEQ + c * 128: g * SEQ + (c + 1) * 128],
                                                                 rhs=bcT[:, (2 + g) * SEQ + c * 128: (2 + g) * SEQ + (c + 1) * 128], start=True, stop=True),
                             reads=["bcT%d" % tg], writes=[bn])
                        S.op("dve", lambda e, bank=bank: e.tensor_tensor(out=cbm, in0=bank[:, 0:128], in1=trileb, op=ALU.mult),
                             reads=[bn, "cb16"], writes=["cbm"])
                        S.op("pool", lambda e: e.tensor_tensor(out=AP(ph8, rhsD_o, [[PB // 2, 128], [128, 8], [1, 128]]),
                                                               in0=AP(cf32, 0, [[256, 128], [0, 8], [1, 128]]),
                                                               in1=AP(ph32, dtA_o + t * 16 + 8 * g, [[PB // 4, 128], [1, 8], [0, 128]]), op=ALU.mult),
                             reads=["cf32", "dtA"], writes=["rhsD"])
                        for hf in range(2):
                            S.op("pe", lambda e, hf=hf: e.matmul(sc[hf][:, 0:512], lhsT=tristb, rhs=rhsD[:, hf * 512:(hf + 1) * 512], start=True, stop=True),
                                 reads=["rhsD", "cb16"], writes=["sc%d" % hf])
                            S.op("act", lambda e, hf=hf: e.activation(out=E8[:, hf * 512:(hf + 1) * 512], in_=sc[hf][:, 0:512], func=AF.Exp),
                                 reads=["sc%d" % hf], writes=["E8"])
                        S.op("dve", lambda e: e.tensor_tensor(out=AP(ph8, E8_o, [[PB // 2, 128], [128, 8], [1, 128]]),
                                                              in0=AP(ph8, E8_o, [[PB // 2, 128], [128, 8], [1, 128]]),
                                                              in1=AP(ph8, cbm_o, [[PB // 2, 128], [0, 8], [1, 128]]), op=ALU.mult),
                             reads=["E8", "cbm"], writes=["E8"])
                        S.op("pe", lambda e: e.matmul(nd[0][:, 0:512], lhsT=identb, rhs=xsD, start=True, stop=False),
                             reads=["xsD", "cb16"], writes=["nd0"], inc=False)
                        for h in range(8):
                            S.op("pe", lambda e, h=h: e.matmul(nd[0][:, h * 64:(h + 1) * 64], lhsT=E8[:, h * 128:(h + 1) * 128],
                                                               rhs=xdt[:, h * 64:(h + 1) * 64], start=False, stop=(h == 7)),
                                 reads=["E8", "xdt"], writes=["nd0"], inc=(h == 7))
                        S.op("pe", lambda e: e.matmul(nd[1][:, 0:512], lhsT=bcT[:, (2 + g) * SEQ + c * 128: (2 + g) * SEQ + (c + 1) * 128],
                                                      rhs=Sbf, start=True, stop=True), reads=["bcT%d" % tg, "Sbf"], writes=["nd1"])
                        S.op("dve", lambda e: e.tensor_tensor(out=t1.rearrange("p (h d) -> p h d", h=8), in0=AP(nd[1], 0, [[512, 128], [64, 8], [1, 64]]),
                                                              in1=hb(ea_o, t), op=ALU.mult), reads=["nd1", "ea"], writes=["t1"])
                        S.op("dve", lambda e: e.tensor_tensor(out=yb, in0=nd[0][:, 0:512], in1=t1, op=ALU.add), reads=["nd0", "t1"], writes=["yb"])
                        S.op("pool", lambda e: e.tensor_tensor(out=yb, in0=yb, in1=zs, op=ALU.mult), reads=["yb", "zs"], writes=["yb"])
                        S.op("act", lambda e: e.activation(out=junk, in_=yb, func=AF.Square, accum_out=ss), reads=["yb"], writes=["junk", "ss"])
                        S.op("act", lambda e: e.activation(out=rs, in_=ss, func=AF.Sqrt, bias=float(RMS_EPS), scale=1.0 / 512.0),
                             reads=["ss"], writes=["rs"])
                        S.op("dve", lambda e: e.reciprocal(out=rs, in_=rs), reads=["rs"], writes=["rs"])
                        S.op("dve", lambda e: e.scalar_tensor_tensor(out=gn, in0=yb, scalar=rs[:, 0:1], in1=normw[:, g * 512:(g + 1) * 512],
                                                                     op0=ALU.mult, op1=ALU.mult), reads=["yb", "rs", "normw"], writes=["gn"])
                        for j in range(4):
                            S.op("pe", lambda e, j=j: e.transpose(tr[1][:, j * 128:(j + 1) * 128], gn[:, j * 128:(j + 1) * 128], identb),
                                 reads=["gn", "cb16"], writes=["tr"], inc=(j == 3))
                        S.op("act", lambda e: e.activation(out=AP(mix4, b * 128, [[2048, 128], [512, 4], [1, 128]]),
                                                           in_=AP(trT, 512, [[1024, 128], [128, 4], [1, 128]]), func=AF.Copy),
                             reads=["tr"], writes=["mix4"])
                        if c < NT - 1:
                            S.op("pe", lambda e: e.matmul(st0[:, 0:512], lhsT=Btm, rhs=xdtd, start=True, stop=True),
                                 reads=["Btm", "xdtd"], writes=["st0"])
                            S.op("dve", lambda e: e.tensor_tensor(out=S32.rearrange("p (h d) -> p h d", h=8), in0=S32.rearrange("p (h d) -> p h d", h=8),
                                                                  in1=hb(cd_o, t), op=ALU.mult), reads=["S32", "cd"], writes=["S32"])
                            S.op("dve", lambda e: e.tensor_tensor(out=S32, in0=S32, in1=st0[:, 0:512], op=ALU.add), reads=["S32", "st0"], writes=["S32"])
                            S.op("pool", lambda e: e.tensor_copy(out=Sbf, in_=S32), reads=["S32"], writes=["Sbf"])
                    outproj_partial(l, False, tg)
                    if stop == 7:
                        return finalize()
                    if stop == 8 and g == 0 and tg == NG - 1:
                        return finalize()
                    if g == 1:
                        for b in range(4):
                            layernorm_tile(l, 4 * tg + b, last and stop != 9)
                        if stop == 9:
                            return finalize()
        for tok in S.all_dma_toks:
            S.wait_tok("sp", tok)
        print("instructions emitted:", S.n_ins)
    return nc


_CACHE = {}


def run(inputs, n_layers=DEPTH):
    x = np.asarray(inputs["x"], np.float32)
    w = prep_weights(inputs, n_layers)
    if n_layers not in _CACHE:
        _CACHE[n_layers] = build_program(n_layers)
    nc = _CACHE[n_layers]
    in_maps = []
    for c in range(8):
        m = dict(w)
        m["x"] = np.ascontiguousarray(x[c])
        in_maps.append(m)
    res = run_bass_kernel_spmd(nc, in_maps, core_ids=list(range(8)))
    return np.stack([np.asarray(r["y"], np.float32) for r in res.results], axis=0)


def kernel(**inputs):
    return run(inputs, DEPTH)
```

```python
from contextlib import ExitStack
import numpy as np
import ml_dtypes
import concourse.bass as bass
import concourse.mybir as mybir
from concourse.bass_utils import run_bass_kernel_spmd

F32 = mybir.dt.float32
BF16 = mybir.dt.bfloat16
AF = mybir.ActivationFunctionType
ALU = mybir.AluOpType

D_MODEL = 1024
SEQ = 2048
DEPTH = 4
NT = 16
NG = 4
HEAD = 64
ALPHA = (2.0 * DEPTH) ** 0.25
LN_EPS = 1e-5
RMS_EPS = 1e-5
N_W_CHUNKS = 10


class Res:
    __slots__ = ("w", "r")

    def __init__(self):
        self.w = None
        self.r = {}


class Sched:
    ENGS = ("pe", "act", "dve", "pool", "sp")

    def __init__(self, nc, st, ndma=8):
        self.nc = nc
        self.eng = {"pe": nc.tensor, "act": nc.scalar, "dve": nc.vector, "pool": nc.gpsimd, "sp": nc.sync}
        self.cnt = {e: 0 for e in self.ENGS}
        self.waited = {e: {} for e in self.ENGS}
        self.sems = {}
        for e in self.ENGS:
            self.sems["s_" + e] = st.enter_context(nc.semaphore("s_" + e))
        self.ndma = ndma
        self.dma_names = {}
        self.dma_use = {}
        self.dma_rr = {}
        for q in ("sp", "act", "pool"):
            self.dma_names[q] = ["d_%s_%d" % (q, i) for i in range(ndma)]
            for n in self.dma_names[q]:
                self.sems[n] = st.enter_context(nc.semaphore(n))
            self.dma_use[q] = [0] * ndma
            self.dma_rr[q] = 0
        self.res = {}
        self.pending = {e: [] for e in self.ENGS}
        self.all_dma_toks = []
        self.n_ins = 0

    def R(self, name):
        r = self.res.get(name)
        if r is None:
            r = Res()
            self.res[name] = r
        return r

    def _deps(self, eng, reads, writes):
        need = {}
        for r in reads:
            t = self.R(r).w
            if t is not None and need.get(t[0], 0) < t[1]:
                need[t[0]] = t[1]
        for w in writes:
            rr = self.R(w)
            t = rr.w
            if t is not None and need.get(t[0], 0) < t[1]:
                need[t[0]] = t[1]
            for k, v in rr.r.items():
                if need.get(k, 0) < v:
                    need[k] = v
        wd = self.waited[eng]
        e = self.eng[eng]
        for k, v in need.items():
            if k == "s_pe" and eng == "pe":
                continue
            if wd.get(k, 0) >= v:
                continue
            wd[k] = v
            e.wait_ge(self.sems[k], v)
            self.n_ins += 1

    def _commit(self, tok, reads, writes):
        k, v = tok
        for r in reads:
            d = self.R(r).r
            if d.get(k, 0) < v:
                d[k] = v
        for w in writes:
            rr = self.R(w)
            rr.w = tok
            rr.r = {}

    def op(self, eng, fn, reads=(), writes=(), inc=True):
        self._deps(eng, reads, writes)
        ins = fn(self.eng[eng])
        self.n_ins += 1
        if not inc:
            self.pending[eng].append((reads, writes))
            return None
        self.cnt[eng] += 1
        tok = ("s_" + eng, self.cnt[eng])
        ins.then_inc(self.sems[tok[0]], 1)
        for (r, w) in self.pending[eng]:
            self._commit(tok, r, w)
        self.pending[eng] = []
        self._commit(tok, reads, writes)
        return tok

    def dma(self, q, fn, reads=(), writes=()):
        assert not self.pending[q]
        i = self.dma_rr[q]
        self.dma_rr[q] = (i + 1) % self.ndma
        key = self.dma_names[q][i]
        self._deps(q, reads, writes)
        prev = self.dma_use[q][i] * 16
        if prev > 0 and self.waited[q].get(key, 0) < prev:
            self.waited[q][key] = prev
            self.eng[q].wait_ge(self.sems[key], prev)
        self.dma_use[q][i] += 1
        tok = (key, self.dma_use[q][i] * 16)
        fn(self.eng[q]).then_inc(self.sems[key], 16)
        self.n_ins += 1
        self._commit(tok, reads, writes)
        return tok

    def barrier(self):
        for e in self.ENGS:
            assert not self.pending[e]
        targets = {}
        for e in self.ENGS:
            if self.cnt[e]:
                targets["s_" + e] = self.cnt[e]
        for q in self.dma_names:
            for i, n in enumerate(self.dma_names[q]):
                if self.dma_use[q][i]:
                    targets[n] = self.dma_use[q][i] * 16
        for e in self.ENGS:
            wd = self.waited[e]
            for k, v in targets.items():
                if k == "s_" + e:
                    continue
                if wd.get(k, 0) < v:
                    wd[k] = v
                    self.eng[e].wait_ge(self.sems[k], v)
                    self.n_ins += 1

    def wait_tok(self, eng, tok):
        k, v = tok
        if self.waited[eng].get(k, 0) < v:
            self.waited[eng][k] = v
            self.eng[eng].wait_ge(self.sems[k], v)


def AP(t, off, pat):
    return bass.AP(t, off, [list(p) for p in pat])


def _attn_perm(kp):
    idx = []
    for j in range(4):
        for r in range(2):
            h = (2 * kp + r) * 4 + j
            idx.extend(range(h * 64, h * 64 + 64))
    return np.array(idx)


def _in_col_chunks():
    q0, k0, v0, za0, zs0, x0 = 0, 1024, 1280, 1536, 2560, 3584
    ch = []
    ch.append(np.concatenate([np.arange(k0, k0 + 256), np.arange(v0, v0 + 256)]))
    for kp in range(2):
        p = _attn_perm(kp)
        ch.append(q0 + p)
        ch.append(za0 + p)
    ch.append(np.arange(x0 + 1024, x0 + 1536))
    for g in range(2):
        ch.append(np.arange(x0 + g * 512, x0 + (g + 1) * 512))
        ch.append(np.arange(zs0 + g * 512, zs0 + (g + 1) * 512))
    return ch


def _consts():
    s = np.arange(128)[:, None].astype(np.float64)
    q = np.arange(128)[None, :].astype(np.float64)
    slopes = np.exp2(-8.0 * np.arange(1, 17) / 16.0)
    masks = np.zeros((128, 4, 2, 4, 128), np.float32)
    for kv in range(4):
        for g in range(4):
            sl = slopes[kv * 4 + g]
            masks[:, kv, 0, g, :] = np.where(s <= q, np.exp(-sl * (q - s)), 0.0)
            masks[:, kv, 1, g, :] = np.where(s > q, np.exp(-sl * (128.0 + q - s)), 0.0)
    tri_le = (s <= q).astype(np.float32)
    tri_st = (s > q).astype(np.float32)
    ident = np.eye(128, dtype=np.float32)
    ones = np.ones((128, 128), np.float32)
    cb16 = np.concatenate([ident, tri_le, tri_st, ones], axis=1).astype(ml_dtypes.bfloat16)
    cf32 = np.concatenate([tri_le, ones], axis=1).astype(np.float32)
    return masks.reshape(128, 4096).astype(ml_dtypes.bfloat16), cb16, cf32


def prep_weights(inp, n_layers):
    L = n_layers
    w_in = np.asarray(inp["w_in"], np.float32)[:L]
    w_out = np.asarray(inp["w_out"], np.float32)[:L]
    chunks = _in_col_chunks()
    w_in_p = np.empty((L, N_W_CHUNKS, 128, 8, 512), np.float32)
    for c, cols in enumerate(chunks):
        w = w_in[:, :, cols]
        w_in_p[:, c] = w.reshape(L, 8, 128, 512).transpose(0, 2, 1, 3)
    w_dt = np.ascontiguousarray(w_in[:, :, 5120:5136].reshape(L, 8, 128, 16).transpose(0, 2, 1, 3))
    rows = np.concatenate([_attn_perm(0), _attn_perm(1), np.arange(1024, 2048)])
    w_out_p = np.ascontiguousarray(w_out[:, rows, :].reshape(L, 4, 4, 128, 1024).transpose(0, 1, 3, 2, 4))
    conv_w = np.asarray(inp["conv_w"], np.float32)[:L]
    conv_wT = np.ascontiguousarray(conv_w.reshape(L, 4, 12, 128).transpose(0, 3, 2, 1))
    conv_bT = np.ascontiguousarray(np.asarray(inp["conv_b"], np.float32)[:L].reshape(L, 12, 128).transpose(0, 2, 1))

    def rep(a):
        a = np.asarray(a, np.float32)[:L]
        return np.ascontiguousarray(np.broadcast_to(a[:, None, :], (L, 128, a.shape[1])))
    small = np.concatenate([rep(inp["dt_bias"]), rep(inp["a_log"]), rep(inp["d_skip"]), rep(inp["sinks"])], axis=2)
    masks, cb16, cf32 = _consts()
    return {
        "w_in_p": w_in_p, "w_dt": w_dt, "w_out_p": w_out_p, "conv_wT": conv_wT, "conv_bT": conv_bT,
        "small": np.ascontiguousarray(small), "ln_g": rep(inp["ln_g"]), "ln_b": rep(inp["ln_b"]),
        "normw": rep(inp["ssm_norm_w"]), "masks": masks, "cb16": cb16, "cf32": cf32,
    }


def build_program(n_layers=DEPTH, stop=99):
    L = n_layers
    nc = bass.Bass("TRN2", target_bir_lowering=False)
    x_d = nc.dram_tensor("x", [SEQ, D_MODEL], F32, kind="ExternalInput")
    w_in_d = nc.dram_tensor("w_in_p", [L, N_W_CHUNKS, 128, 8 * 512], F32, kind="ExternalInput")
    w_dt_d = nc.dram_tensor("w_dt", [L, 128, 8 * 16], F32, kind="ExternalInput")
    w_out_d = nc.dram_tensor("w_out_p", [L, 4, 128, 4 * 1024], F32, kind="ExternalInput")
    convw_d = nc.dram_tensor("conv_wT", [L, 128, 48], F32, kind="ExternalInput")
    convb_d = nc.dram_tensor("conv_bT", [L, 128, 12], F32, kind="ExternalInput")
    small_d = nc.dram_tensor("small", [L, 128, 64], F32, kind="ExternalInput")
    lng_d = nc.dram_tensor("ln_g", [L, 128, 1024], F32, kind="ExternalInput")
    lnb_d = nc.dram_tensor("ln_b", [L, 128, 1024], F32, kind="ExternalInput")
    normw_d = nc.dram_tensor("normw", [L, 128, 1024], F32, kind="ExternalInput")
    masks_d = nc.dram_tensor("masks", [128, 4096], BF16, kind="ExternalInput")
    cb16_d = nc.dram_tensor("cb16", [128, 512], BF16, kind="ExternalInput")
    cf32_d = nc.dram_tensor("cf32", [128, 256], F32, kind="ExternalInput")
    y_d = nc.dram_tensor("y", [SEQ, D_MODEL], F32, kind="ExternalOutput")

    with ExitStack() as st:
        def sb(name, cols, dt):
            return st.enter_context(nc.sbuf_tensor("sb_" + name, [128, cols], dt))

        def ps(name, cols, dt):
            return st.enter_context(nc.psum_tensor("ps_" + name, [128, cols], dt))

        S = Sched(nc, st)
        acc = sb("acc", NT * 1024, F32)
        xT = sb("xT", 8 * SEQ, BF16)
        cb16 = sb("cb16", 512, BF16)
        cf32 = sb("cf32", 256, F32)
        identb = cb16[:, 0:128]
        trileb = cb16[:, 128:256]
        tristb = cb16[:, 256:384]
        onesb = cb16[:, 384:512]
        trilef = cf32[:, 0:128]
        onesf = cf32[:, 128:256]
        wbuf = [sb("wbuf%d" % i, 8 * 512, BF16) for i in range(2)]
        wob = sb("wob", 4 * 1024, BF16)
        wdt = sb("wdt", 128, BF16)
        lng = sb("lng", 1024, F32)
        lnb = sb("lnb", 1024, F32)
        normw = sb("normw", 1024, F32)
        convw = sb("convw", 48, F32)
        convb = sb("convb", 12, F32)
        small = sb("small", 64, F32)
        mix4 = sb("mix4", 4 * 512, BF16)
        xb16 = sb("xb16", 1024, BF16)
        stats = sb("stats", 12, F32)
        mv = sb("mv", 2, F32)
        rstd = sb("rstd", 1, F32)
        PH_BYTES = 66 * 1024
        ph8 = sb("phase", PH_BYTES // 2, BF16)
        ph32 = ph8.bitcast(F32)

        class Carver:
            def __init__(self):
                self.off = 0

            def bf(self, n):
                o = self.off
                self.off += 2 * n
                assert self.off <= PH_BYTES, self.off
                return ph8[:, o // 2: o // 2 + n], o // 2

            def f32(self, n):
                self.off = (self.off + 3) // 4 * 4
                o = self.off
                self.off += 4 * n
                assert self.off <= PH_BYTES, self.off
                return ph32[:, o // 4: o // 4 + n], o // 4

        pj = [ps("pj%d" % i, 512, F32) for i in range(2)]
        sc = [ps("sc%d" % i, 512, F32) for i in range(2)]
        nd = [ps("nd%d" % i, 512, F32) for i in range(2)]
        st0 = ps("st0", 512, F32)
        trT = ps("tr", 1024, BF16)
        tr = [trT[:, 0:512], trT[:, 512:1024]]
        pj_rr = [0]

        def next_pj():
            i = pj_rr[0]
            pj_rr[0] = 1 - i
            return pj[i], "pj%d" % i

        S.dma("sp", lambda e: e.dma_start(out=cb16[:], in_=cb16_d[:]), writes=["cb16"])
        S.dma("sp", lambda e: e.dma_start(out=cf32[:], in_=cf32_d[:]), writes=["cf32"])

        xT3 = lambda kc, c0, n: xT[:, kc * SEQ + c0: kc * SEQ + c0 + n]

        def load_w_chunk(l, c, buf_i):
            S.dma("pool", lambda e: e.dma_start(out=wbuf[buf_i][:], in_=w_in_d[l, c]), writes=["wbuf%d" % buf_i])

        def load_wo(l, qi):
            S.dma("pool", lambda e: e.dma_start(out=wob[:], in_=w_out_d[l, qi]), writes=["wob"])

        def mm_group(items, reads, writes):
            n = len(items)
            for i, (o, a, b) in enumerate(items):
                S.op("pe", lambda e, o=o, a=a, b=b, i=i: e.matmul(o, lhsT=a, rhs=b, start=(i == 0), stop=(i == n - 1)),
                     reads=reads, writes=writes, inc=(i == n - 1))

        def inproj_fm(wb_i, col0, tg, bank, bank_name):
            items = [(bank[:, 0:512], wbuf[wb_i][:, kc * 512 + col0: kc * 512 + col0 + 128], xT3(kc, tg * 512, 512))
                     for kc in range(8)]
            mm_group(items, reads=["wbuf%d" % wb_i] + ["xT%d" % t for t in range(4 * tg, 4 * tg + 4)], writes=[bank_name])

        def inproj_tm(wb_i, col0, ncols, t, bank, bank_name):
            items = [(bank[:, 0:ncols], xT3(kc, t * 128, 128), wbuf[wb_i][:, kc * 512 + col0: kc * 512 + col0 + ncols])
                     for kc in range(8)]
            mm_group(items, reads=["wbuf%d" % wb_i, "xT%d" % t], writes=[bank_name])

        def to_xT(t):
            S.op("act", lambda e: e.activation(out=xb16[:], in_=acc[:, t * 1024:(t + 1) * 1024], func=AF.Copy),
                 reads=["acc%d" % t], writes=["xb16"])
            for half in range(2):
                for j in range(4):
                    kc = half * 4 + j
                    S.op("pe", lambda e, kc=kc, j=j, half=half: e.transpose(tr[half][:, j * 128:(j + 1) * 128],
                                                                           xb16[:, kc * 128:(kc + 1) * 128], identb),
                         reads=["xb16", "cb16"], writes=["tr"], inc=(j == 3))
                o = AP(xT, half * 4 * SEQ + t * 128, [[8 * SEQ, 128], [SEQ, 4], [1, 128]])
                i_ = AP(trT, half * 512, [[1024, 128], [128, 4], [1, 128]])
                S.op("dve", lambda e, o=o, i_=i_: e.tensor_copy(out=o, in_=i_), reads=["tr"], writes=["xT%d" % t])

        def outproj_partial(l, first, tg):
            for b in range(4):
                t = 4 * tg + b
                for ch in range(2):
                    bank, bn = next_pj()
                    items = [(bank[:, 0:512], mix4[:, c * 512 + b * 128: c * 512 + (b + 1) * 128],
                              wob[:, c * 1024 + ch * 512: c * 1024 + (ch + 1) * 512]) for c in range(4)]
                    mm_group(items, reads=["mix4", "wob"], writes=[bn])
                    a_ = acc[:, t * 1024 + ch * 512: t * 1024 + (ch + 1) * 512]
                    if first:
                        S.op("dve", lambda e, a_=a_, bank=bank: e.scalar_tensor_tensor(
                            out=a_, in0=a_, scalar=float(ALPHA), in1=bank[:, 0:512], op0=ALU.mult, op1=ALU.add),
                            reads=[bn, "acc%d" % t], writes=["acc%d" % t])
                    else:
                        S.op("dve", lambda e, a_=a_, bank=bank: e.tensor_tensor(out=a_, in0=a_, in1=bank[:, 0:512], op=ALU.add),
                             reads=[bn, "acc%d" % t], writes=["acc%d" % t])

        def layernorm_tile(l, t, last):
            a_ = acc[:, t * 1024:(t + 1) * 1024]
            rw = ["acc%d" % t]
            for h in range(2):
                S.op("dve", lambda e, h=h: e.bn_stats(out=stats[:, h * 6:(h + 1) * 6], in_=acc[:, t * 1024 + h * 512: t * 1024 + (h + 1) * 512]),
                     reads=rw, writes=["stats"])
            S.op("dve", lambda e: e.bn_aggr(out=mv[:], in_=stats[:]), reads=["stats"], writes=["mv"])
            S.op("act", lambda e: e.activation(out=rstd[:], in_=mv[:, 1:2], func=AF.Ln, bias=float(LN_EPS), scale=1.0),
                 reads=["mv"], writes=["rstd"])
            S.op("act", lambda e: e.activation(out=rstd[:], in_=rstd[:], func=AF.Exp, scale=-0.5), reads=["rstd"], writes=["rstd"])
            S.op("dve", lambda e: e.tensor_scalar(out=a_, in0=a_, scalar1=mv[:, 0:1], scalar2=rstd[:, 0:1],
                                                  op0=ALU.subtract, op1=ALU.mult), reads=rw + ["mv", "rstd"], writes=rw)
            S.op("pool", lambda e: e.tensor_tensor(out=a_, in0=a_, in1=lng[:], op=ALU.mult), reads=rw + ["lng"], writes=rw)
            S.op("pool", lambda e: e.tensor_tensor(out=a_, in0=a_, in1=lnb[:], op=ALU.add), reads=rw + ["lnb"], writes=rw)
            if last:
                S.all_dma_toks.append(S.dma("sp", lambda e: e.dma_start(out=y_d[t * 128:(t + 1) * 128, :], in_=a_), reads=rw))
            else:
                to_xT(t)

        def finalize():
            for t in range(NT):
                S.all_dma_toks.append(S.dma("sp", lambda e, t=t: e.dma_start(out=y_d[t * 128:(t + 1) * 128, :], in_=acc[:, t * 1024:(t + 1) * 1024]),
                                            reads=["acc%d" % t]))
            S.barrier()
            for tok in S.all_dma_toks:
                S.wait_tok("sp", tok)
            print("instructions emitted (stopped):", S.n_ins)
            return nc

        for t in range(NT):
            S.dma("sp", lambda e, t=t: e.dma_start(out=acc[:, t * 1024:(t + 1) * 1024], in_=x_d[t * 128:(t + 1) * 128, :]),
                  writes=["acc%d" % t])
        for t in range(NT):
            to_xT(t)

        if stop == 0:
            return finalize()
        for l in range(L):
            last = (l == L - 1)
            S.dma("sp", lambda e: e.dma_start(out=lng[:], in_=lng_d[l]), writes=["lng"])
            S.dma("sp", lambda e: e.dma_start(out=lnb[:], in_=lnb_d[l]), writes=["lnb"])
            S.dma("sp", lambda e: e.dma_start(out=normw[:], in_=normw_d[l]), writes=["normw"])
            S.dma("sp", lambda e: e.dma_start(out=convw[:], in_=convw_d[l]), writes=["convw"])
            S.dma("sp", lambda e: e.dma_start(out=convb[:], in_=convb_d[l]), writes=["convb"])
            S.dma("sp", lambda e: e.dma_start(out=small[:], in_=small_d[l]), writes=["small"])
            S.dma("pool", lambda e: e.dma_start(out=wdt[:], in_=w_dt_d[l]), writes=["wdt"])
            dtb = small[:, 0:16]
            alog = small[:, 16:32]
            dsk = small[:, 32:48]
            snk = small[:, 48:64]

            cv = Carver()
            masks, _ = cv.bf(4096)
            kT, kT_o = cv.bf(2 * SEQ)
            vt, vt_o = cv.bf(NT * 256)
            q4, q4_o = cv.bf(4 * 512)
            z4, z4_o = cv.bf(4 * 512)
            Eb = [cv.bf(512) for _ in range(8)]
            sinkexp, se_o = cv.f32(2 * 512)
            dS, dS_o = cv.f32(512)
            wgt, wgt_o = cv.f32(512)
            es, es_o = cv.f32(16)
            S.barrier()
            S.dma("sp", lambda e: e.dma_start(out=masks, in_=masks_d[:]), reads=[], writes=["masks"])
            load_w_chunk(l, 0, 0)
            load_w_chunk(l, 1, 1)
            S.op("act", lambda e: e.activation(out=es, in_=snk, func=AF.Exp), reads=["small"], writes=["es"])
            for kp in range(2):
                for r in range(2):
                    o = AP(ph32, se_o + kp * 512 + r * 64 * (PH_BYTES // 4), [[PH_BYTES // 4, 64], [128, 4], [1, 128]])
                    hh = (2 * kp + r) * 4
                    i_ = AP(ph32, es_o + hh + r * 64 * (PH_BYTES // 4), [[PH_BYTES // 4, 64], [1, 4], [0, 128]])
                    S.op("dve", lambda e, o=o, i_=i_: e.tensor_copy(out=o, in_=i_), reads=["es"], writes=["sinkexp"])
            for tg in range(NG):
                for c in range(2):
                    bank, bn = next_pj()
                    inproj_fm(0, c * 128, tg, bank, bn)
                    S.op("dve", lambda e, c=c, tg=tg, bank=bank: e.tensor_copy(
                        out=kT[:, c * SEQ + tg * 512: c * SEQ + (tg + 1) * 512], in_=bank[:, 0:512]),
                        reads=[bn], writes=["kT%d" % tg])
            for t in range(NT):
                bank, bn = next_pj()
                inproj_tm(0, 256, 256, t, bank, bn)
                S.op("act", lambda e, t=t, bank=bank: e.activation(out=vt[:, t * 256:(t + 1) * 256], in_=bank[:, 0:256], func=AF.Copy),
                     reads=[bn], writes=["v%d" % t])
            if stop == 1:
                return finalize()
            trF = trT.bitcast(F32)
            sbanks = [(sc[0], "sc0"), (sc[1], "sc1"), (st0, "st0"), (trF, "tr")]

            def att_S(kp, tg, b):
                n = 4 * tg + b
                par = b % 2
                k_ = 0
                for r in range(2):
                    kv = 2 * kp + r
                    rows = slice(r * 64, r * 64 + 64)
                    blocks = ([n - 1] if n > 0 else []) + [n]
                    for sblk in blocks:
                        pc = 0 if sblk == n else 1
                        E, _eo = Eb[par * 4 + k_]
                        en = "E%d" % (par * 4 + k_)
                        scb, scn = sbanks[k_]
                        k_ += 1
                        lhsT = kT[rows, kp * SEQ + sblk * 128: kp * SEQ + (sblk + 1) * 128]
                        rhs = AP(ph8, q4_o + b * 128 + r * 64 * (PH_BYTES // 2), [[PH_BYTES // 2, 64], [512, 4], [1, 128]])
                        S.op("pe", lambda e, scb=scb, lhsT=lhsT, rhs=rhs: e.matmul(scb[:, 0:512], lhsT=lhsT, rhs=rhs, start=True, stop=True),
                             reads=["kT%d" % (sblk // 4), "q4"], writes=[scn])
                        S.op("act", lambda e, E=E, scb=scb: e.activation(out=E, in_=scb[:, 0:512], func=AF.Exp, scale=0.125),
                             reads=[scn], writes=[en])
                        m_ = masks[:, (kv * 2 + pc) * 512:(kv * 2 + pc + 1) * 512]
                        S.op("dve", lambda e, E=E, m_=m_: e.tensor_tensor(out=E, in0=E, in1=m_, op=ALU.mult),
                             reads=[en, "masks"], writes=[en])

            def att_V(kp, tg, b):
                n = 4 * tg + b
                par = b % 2
                k_ = 0
                for r in range(2):
                    kv = 2 * kp + r
                    rows = slice(r * 64, r * 64 + 64)
                    blocks = ([n - 1] if n > 0 else []) + [n]
                    for bi, sblk in enumerate(blocks):
                        E, _eo = Eb[par * 4 + k_]
                        en = "E%d" % (par * 4 + k_)
                        k_ += 1
                        first_b = (bi == 0)
                        last_b = (bi == len(blocks) - 1)
                        vl = vt[:, sblk * 256 + kv * 64: sblk * 256 + (kv + 1) * 64]
                        S.op("pe", lambda e, vl=vl, E=E, rows=rows, first_b=first_b, last_b=last_b: e.matmul(
                            nd[0][rows, 0:512], lhsT=vl, rhs=E, start=first_b, stop=last_b),
                            reads=[en, "v%d" % sblk], writes=["nd0"], inc=False)
                        S.op("pe", lambda e, E=E, rows=rows, first_b=first_b, last_b=last_b: e.matmul(
                            nd[1][rows, 0:512], lhsT=onesb[:, 0:64], rhs=E, start=first_b, stop=last_b),
                            reads=[en, "cb16"], writes=["nd1"], inc=True)
                S.op("dve", lambda e: e.tensor_tensor(out=dS, in0=nd[1][:, 0:512], in1=sinkexp[:, kp * 512:(kp + 1) * 512], op=ALU.add),
                     reads=["nd1", "sinkexp"], writes=["dS"])
                S.op("dve", lambda e: e.reciprocal(out=dS, in_=dS), reads=["dS"], writes=["dS"])
                S.op("dve", lambda e: e.tensor_tensor(out=wgt, in0=nd[0][:, 0:512], in1=dS, op=ALU.mult),
                     reads=["nd0", "dS"], writes=["wgt"])
                zv = AP(ph8, z4_o + b * 128, [[PH_BYTES // 2, 128], [512, 4], [1, 128]])
                w3 = AP(ph32, wgt_o, [[PH_BYTES // 4, 128], [128, 4], [1, 128]])
                mo = AP(mix4, b * 128, [[2048, 128], [512, 4], [1, 128]])
                S.op("pool", lambda e, zv=zv, w3=w3, mo=mo: e.tensor_tensor(out=mo, in0=w3, in1=zv, op=ALU.mult),
                     reads=["wgt", "z4"], writes=["mix4"])

            for kp in range(2):
                wq = 1
                load_w_chunk(l, 2 + 2 * kp, 0)
                load_wo(l, kp)
                for tg in range(NG):
                    for j in range(4):
                        bank, bn = next_pj()
                        inproj_fm(wq, j * 128, tg, bank, bn)
                        S.op("dve", lambda e, j=j, bank=bank: e.tensor_copy(out=q4[:, j * 512:(j + 1) * 512], in_=bank[:, 0:512]),
                             reads=[bn], writes=["q4"])
                    for j in range(4):
                        bank, bn = next_pj()
                        inproj_fm(0, j * 128, tg, bank, bn)
                        S.op("act", lambda e, j=j, bank=bank: e.activation(out=z4[:, j * 512:(j + 1) * 512], in_=bank[:, 0:512], func=AF.Silu),
                             reads=[bn], writes=["z4"])
                    att_S(kp, tg, 0)
                    for b in range(4):
                        if b < 3:
                            att_S(kp, tg, b + 1)
                        att_V(kp, tg, b)
                    if tg == NG - 1 and kp == 0:
                        load_w_chunk(l, 3, 1)
                    outproj_partial(l, kp == 0, tg)
            if stop == 2:
                return finalize()
            cv = Carver()
            bcT, bc_o = cv.bf(4 * SEQ)
            ubuf, u_o = cv.bf(4 * 515)
            xsT, xsT_o = cv.bf(4 * 512)
            diag, diag_o = cv.bf(16 * 128)
            xdt = [cv.bf(512)[0] for _ in range(2)]
            xdtd = [cv.bf(512)[0] for _ in range(2)]
            xsD = [cv.bf(512)[0] for _ in range(2)]
            zs4, _ = cv.bf(4 * 512)
            Btm = [cv.bf(128)[0] for _ in range(2)]
            cbm, cbm_o = cv.bf(128)
            rhsD, rhsD_o = cv.bf(1024)
            _e8 = [cv.bf(1024) for _ in range(2)]
            E8 = [x_[0] for x_ in _e8]
            E8_o = [x_[1] for x_ in _e8]
            Sbf, _ = cv.bf(512)
            gn, _ = cv.bf(512)
            dt16, dt_o = cv.f32(256)
            dtA, dtA_o = cv.f32(256)
            ea, ea_o = cv.f32(256)
            dtdte, dd_o = cv.f32(256)
            cd, cd_o = cv.f32(256)
            tmpA, _ = cv.f32(256)
            tmpB, _ = cv.f32(256)
            nega, nega_o2 = cv.f32(16)
            S32, _ = cv.f32(512)
            t1, t1_o = cv.f32(512)
            junk = ph8[:, 2 * t1_o: 2 * t1_o + 512]
            yb, _ = cv.f32(512)
            ss, _ = cv.f32(1)
            rs, _ = cv.f32(1)
            PB = PH_BYTES
            S.barrier()
            S.op("pool", lambda e: e.memset(ubuf, 0.0), reads=[], writes=["ubuf"])
            load_w_chunk(l, 5, 1)
            for t in range(NT):
                items = [(st0[:, t * 16:(t + 1) * 16], xT3(kc, t * 128, 128), wdt[:, kc * 16:(kc + 1) * 16]) for kc in range(8)]
                mm_group(items, reads=["wdt", "xT%d" % t], writes=["st0"])
            dtb_b = AP(small, 0, [[64, 128], [0, 16], [1, 16]])
            v3 = lambda o_: AP(ph32, o_, [[PB // 4, 128], [16, 16], [1, 16]])
            S.op("dve", lambda e: e.tensor_tensor(out=v3(dt_o), in0=AP(st0, 0, [[512, 128], [16, 16], [1, 16]]), in1=dtb_b, op=ALU.add),
                 reads=["st0", "small"], writes=["dt16"])
            S.op("dve", lambda e: e.tensor_scalar(out=tmpA, in0=dt16, scalar1=-1.0, scalar2=None, op0=ALU.mult), reads=["dt16"], writes=["tmpA"])
            S.op("dve", lambda e: e.tensor_tensor(out=tmpA, in0=tmpA, in1=dt16, op=ALU.max), reads=["dt16", "tmpA"], writes=["tmpA"])
            S.op("act", lambda e: e.activation(out=tmpA, in_=tmpA, func=AF.Exp, scale=-1.0), reads=["tmpA"], writes=["tmpA"])
            S.op("act", lambda e: e.activation(out=tmpA, in_=tmpA, func=AF.Ln, bias=1.0), reads=["tmpA"], writes=["tmpA"])
            S.op("dve", lambda e: e.scalar_tensor_tensor(out=dt16, in0=dt16, scalar=0.0, in1=tmpA, op0=ALU.max, op1=ALU.add),
                 reads=["dt16", "tmpA"], writes=["dt16"])
            S.op("act", lambda e: e.activation(out=nega, in_=alog, func=AF.Exp), reads=["small"], writes=["nega"])
            S.op("dve", lambda e: e.tensor_scalar(out=nega, in0=nega, scalar1=-1.0, scalar2=None, op0=ALU.mult), reads=["nega"], writes=["nega"])
            nega_o = nega_o2
            S.op("dve", lambda e: e.tensor_tensor(out=v3(dtA_o), in0=v3(dt_o), in1=AP(ph32, nega_o, [[PB // 4, 128], [0, 16], [1, 16]]), op=ALU.mult),
                 reads=["dt16", "nega"], writes=["dtA"])
            S.op("pe", lambda e: e.matmul(st0[:, 0:256], lhsT=trilef, rhs=dtA, start=True, stop=True), reads=["cf32", "dtA", "dt16"], writes=["st0"])
            S.op("pe", lambda e: e.matmul(st0[:, 256:512], lhsT=onesf, rhs=dtA, start=True, stop=True), reads=["cf32", "dtA"], writes=["st0"])
            S.op("act", lambda e: e.activation(out=ea, in_=st0[:, 0:256], func=AF.Exp), reads=["st0"], writes=["ea"])
            S.op("act", lambda e: e.activation(out=cd, in_=st0[:, 256:512], func=AF.Exp), reads=["st0"], writes=["cd"])
            S.op("act", lambda e: e.activation(out=tmpB, in_=st0[:, 0:256], func=AF.Identity), reads=["st0"], writes=["tmpB"])
            S.op("dve", lambda e: e.tensor_tensor(out=tmpB, in0=st0[:, 256:512], in1=tmpB, op=ALU.subtract), reads=["st0", "tmpB"], writes=["tmpB"])
            S.op("act", lambda e: e.activation(out=tmpB, in_=tmpB, func=AF.Exp), reads=["tmpB"], writes=["tmpB"])
            S.op("dve", lambda e: e.tensor_tensor(out=dtdte, in0=dt16, in1=tmpB, op=ALU.mult), reads=["dt16", "tmpB"], writes=["dtdte"])

            if stop == 3:
                return finalize()

            def build_diag(cc0):
                for j4 in range(4):
                    for tap in range(4):
                        o = diag[:, (j4 * 4 + tap) * 128:(j4 * 4 + tap + 1) * 128]
                        s_ = convw[:, (cc0 + j4) * 4 + tap:(cc0 + j4) * 4 + tap + 1]
                        S.op("pool", lambda e, o=o, s_=s_: e.tensor_scalar(out=o, in0=identb, scalar1=s_, scalar2=None, op0=ALU.mult),
                             reads=["cb16", "convw"], writes=["diag"])

            def conv_group(wb_i, cc0, tg, dst, dst_stride, dst_res):
                if tg > 0:
                    S.op("dve", lambda e: e.tensor_copy(out=AP(ph8, u_o, [[PB // 2, 128], [515, 4], [1, 3]]),
                                                        in_=AP(ph8, u_o + 512, [[PB // 2, 128], [515, 4], [1, 3]])),
                         reads=["ubuf"], writes=["ubuf"])
                else:
                    S.op("pool", lambda e: e.memset(AP(ph8, u_o, [[PB // 2, 128], [515, 4], [1, 3]]), 0.0), reads=[], writes=["ubuf"])
                for j in range(4):
                    bank, bn = next_pj()
                    inproj_fm(wb_i, j * 128, tg, bank, bn)
                    S.op("act", lambda e, j=j, bank=bank: e.activation(out=ubuf[:, j * 515 + 3: j * 515 + 515], in_=bank[:, 0:512], func=AF.Copy),
                         reads=[bn], writes=["ubuf"])
                for j in range(4):
                    bank, bn = next_pj()
                    items = [(bank[:, 0:512], diag[:, (j * 4 + tap) * 128:(j * 4 + tap + 1) * 128],
                              ubuf[:, j * 515 + tap: j * 515 + tap + 512]) for tap in range(4)]
                    mm_group(items, reads=["diag", "ubuf"], writes=[bn])
                    o = dst(j)
                    S.op("act", lambda e, o=o, bank=bank, j=j: e.activation(out=o, in_=bank[:, 0:512], func=AF.Silu,
                                                                             bias=convb[:, cc0 + j: cc0 + j + 1]),
                         reads=[bn, "convb"], writes=[dst_res])

            build_diag(8)
            for tg in range(NG):
                conv_group(1, 8, tg, lambda j, tg=tg: bcT[:, j * SEQ + tg * 512: j * SEQ + (tg + 1) * 512], None, "bcT%d" % tg)
            if stop == 4:
                return finalize()
            def ssd_I(g, tg, b):
                c = 4 * tg + b
                t = c
                p_ = b % 2
                for j in range(4):
                    S.op("pe", lambda e, j=j: e.transpose(tr[0][:, j * 128:(j + 1) * 128], xsT[:, j * 512 + b * 128: j * 512 + (b + 1) * 128], identb),
                         reads=["xsT", "cb16"], writes=["tr"], inc=(j == 3))
                tr3 = AP(trT, 0, [[1024, 128], [64, 8], [1, 64]])
                for dst_, nm_, sc_o in ((xdt[p_], "xdt%d" % p_, dt_o), (xdtd[p_], "xdtd%d" % p_, dd_o)):
                    S.op("dve", lambda e, dst_=dst_, sc_o=sc_o: e.tensor_tensor(
                        out=dst_.rearrange("p (h d) -> p h d", h=8), in0=tr3, in1=hb(g, sc_o, t), op=ALU.mult),
                        reads=["tr", "dt16", "dtdte"], writes=[nm_])
                dsk_b = AP(small, 32 + 8 * g, [[64, 128], [1, 8], [0, 64]])
                S.op("dve", lambda e: e.tensor_tensor(out=xsD[p_].rearrange("p (h d) -> p h d", h=8), in0=tr3, in1=dsk_b, op=ALU.mult),
                     reads=["tr", "small"], writes=["xsD%d" % p_])
                S.op("pe", lambda e: e.transpose(tr[1][:, 0:128], bcT[:, g * SEQ + c * 128: g * SEQ + (c + 1) * 128], identb),
                     reads=["bcT%d" % tg, "cb16"], writes=["tr"])
                S.op("act", lambda e: e.activation(out=Btm[p_], in_=tr[1][:, 0:128], func=AF.Copy), reads=["tr"], writes=["Btm%d" % p_])
                bank, bn = next_pj()
                S.op("pe", lambda e, bank=bank: e.matmul(bank[:, 0:128], lhsT=bcT[:, g * SEQ + c * 128: g * SEQ + (c + 1) * 128],
                                                         rhs=bcT[:, (2 + g) * SEQ + c * 128: (2 + g) * SEQ + (c + 1) * 128], start=True, stop=True),
                     reads=["bcT%d" % tg], writes=[bn])
                S.op("dve", lambda e, bank=bank: e.tensor_tensor(out=cbm, in0=bank[:, 0:128], in1=trileb, op=ALU.mult),
                     reads=[bn, "cb16"], writes=["cbm"])
                S.op("pool", lambda e: e.tensor_tensor(out=AP(ph8, rhsD_o, [[PB // 2, 128], [128, 8], [1, 128]]),
                                                       in0=AP(cf32, 0, [[256, 128], [0, 8], [1, 128]]),
                                                       in1=AP(ph32, dtA_o + t * 16 + 8 * g, [[PB // 4, 128], [1, 8], [0, 128]]), op=ALU.mult),
                     reads=["cf32", "dtA"], writes=["rhsD"])
                eo = E8_o[p_]
                for hf in range(2):
                    S.op("pe", lambda e, hf=hf: e.matmul(sc[hf][:, 0:512], lhsT=tristb, rhs=rhsD[:, hf * 512:(hf + 1) * 512], start=True, stop=True),
                         reads=["rhsD", "cb16"], writes=["sc%d" % hf])
                    S.op("act", lambda e, hf=hf: e.activation(out=E8[p_][:, hf * 512:(hf + 1) * 512], in_=sc[hf][:, 0:512], func=AF.Exp),
                         reads=["sc%d" % hf], writes=["E8%d" % p_])
                S.op("dve", lambda e: e.tensor_tensor(out=AP(ph8, eo, [[PB // 2, 128], [128, 8], [1, 128]]),
                                                      in0=AP(ph8, eo, [[PB // 2, 128], [128, 8], [1, 128]]),
                                                      in1=AP(ph8, cbm_o, [[PB // 2, 128], [0, 8], [1, 128]]), op=ALU.mult),
                     reads=["E8%d" % p_, "cbm"], writes=["E8%d" % p_])

            def ssd_D(g, tg, b):
                c = 4 * tg + b
                t = c
                p_ = b % 2
                G_ = E8[p_]
                S.op("pe", lambda e: e.matmul(nd[0][:, 0:512], lhsT=identb, rhs=xsD[p_], start=True, stop=False),
                     reads=["xsD%d" % p_, "cb16"], writes=["nd0"], inc=False)
                for h in range(8):
                    S.op("pe", lambda e, h=h: e.matmul(nd[0][:, h * 64:(h + 1) * 64], lhsT=G_[:, h * 128:(h + 1) * 128],
                                                       rhs=xdt[p_][:, h * 64:(h + 1) * 64], start=False, stop=(h == 7)),
                         reads=["E8%d" % p_, "xdt%d" % p_], writes=["nd0"], inc=(h == 7))
                S.op("pe", lambda e: e.matmul(nd[1][:, 0:512], lhsT=bcT[:, (2 + g) * SEQ + c * 128: (2 + g) * SEQ + (c + 1) * 128],
                                              rhs=Sbf, start=True, stop=True), reads=["bcT%d" % tg, "Sbf"], writes=["nd1"])
                if c < NT - 1:
                    S.op("pe", lambda e: e.matmul(st0[:, 0:512], lhsT=Btm[p_], rhs=xdtd[p_], start=True, stop=True),
                         reads=["Btm%d" % p_, "xdtd%d" % p_], writes=["st0"])
                    S.op("dve", lambda e: e.tensor_tensor(out=S32.rearrange("p (h d) -> p h d", h=8), in0=S32.rearrange("p (h d) -> p h d", h=8),
                                                          in1=hb(g, cd_o, t), op=ALU.mult), reads=["S32", "cd"], writes=["S32"])
                    S.op("dve", lambda e: e.tensor_tensor(out=S32, in0=S32, in1=st0[:, 0:512], op=ALU.add), reads=["S32", "st0"], writes=["S32"])
                    S.op("pool", lambda e: e.tensor_copy(out=Sbf, in_=S32), reads=["S32"], writes=["Sbf"])
                S.op("dve", lambda e: e.tensor_tensor(out=t1.rearrange("p (h d) -> p h d", h=8), in0=AP(nd[1], 0, [[512, 128], [64, 8], [1, 64]]),
                                                      in1=hb(g, ea_o, t), op=ALU.mult), reads=["nd1", "ea"], writes=["t1"])
                S.op("dve", lambda e: e.tensor_tensor(out=yb, in0=nd[0][:, 0:512], in1=t1, op=ALU.add), reads=["nd0", "t1"], writes=["yb"])
                S.op("pool", lambda e: e.tensor_tensor(out=yb, in0=yb, in1=zs4[:, b * 512:(b + 1) * 512], op=ALU.mult), reads=["yb", "zs4"], writes=["yb"])
                S.op("act", lambda e: e.activation(out=junk, in_=yb, func=AF.Square, accum_out=ss), reads=["yb", "t1"], writes=["t1", "ss"])
                S.op("act", lambda e: e.activation(out=rs, in_=ss, func=AF.Ln, bias=float(RMS_EPS), scale=1.0 / 512.0), reads=["ss"], writes=["rs"])
                S.op("act", lambda e: e.activation(out=rs, in_=rs, func=AF.Exp, scale=-0.5), reads=["rs"], writes=["rs"])
                S.op("dve", lambda e: e.scalar_tensor_tensor(out=gn, in0=yb, scalar=rs[:, 0:1], in1=normw[:, g * 512:(g + 1) * 512],
                                                             op0=ALU.mult, op1=ALU.mult), reads=["yb", "rs", "normw"], writes=["gn"])
                for j in range(4):
                    S.op("pe", lambda e, j=j: e.transpose(tr[1][:, j * 128:(j + 1) * 128], gn[:, j * 128:(j + 1) * 128], identb),
                         reads=["gn", "cb16"], writes=["tr"], inc=(j == 3))
                S.op("act", lambda e: e.activation(out=AP(mix4, b * 128, [[2048, 128], [512, 4], [1, 128]]),
                                                   in_=AP(trT, 512, [[1024, 128], [128, 4], [1, 128]]), func=AF.Copy),
                     reads=["tr"], writes=["mix4"])

            hb = lambda g, o_, t: AP(ph32, o_ + t * 16 + 8 * g, [[PB // 4, 128], [1, 8], [0, 64]])
            for g in range(2):
                load_w_chunk(l, 6 + 2 * g, 0)
                load_w_chunk(l, 7 + 2 * g, 1)
                load_wo(l, 2 + g)
                build_diag(4 * g)
                S.op("pool", lambda e: e.memset(S32, 0.0), reads=[], writes=["S32"])
                S.op("pool", lambda e: e.memset(Sbf, 0.0), reads=[], writes=["Sbf"])
                for tg in range(NG):
                    conv_group(0, 4 * g, tg, lambda j: xsT[:, j * 512:(j + 1) * 512], None, "xsT")
                    for b in range(4):
                        bank, bn = next_pj()
                        inproj_tm(1, 0, 512, 4 * tg + b, bank, bn)
                        S.op("act", lambda e, bank=bank, b=b: e.activation(out=zs4[:, b * 512:(b + 1) * 512], in_=bank[:, 0:512], func=AF.Silu),
                             reads=[bn], writes=["zs4"])
                    ssd_I(g, tg, 0)
                    for b in range(4):
                        if b < 3:
                            ssd_I(g, tg, b + 1)
                        ssd_D(g, tg, b)
                    outproj_partial(l, False, tg)
                    if g == 1:
                        for b in range(4):
                            layernorm_tile(l, 4 * tg + b, last)
        for tok in S.all_dma_toks:
            S.wait_tok("sp", tok)
        print("instructions emitted:", S.n_ins)
    return nc


_CACHE = {}


def run(inputs, n_layers=DEPTH):
    x = np.asarray(inputs["x"], np.float32)
    w = prep_weights(inputs, n_layers)
    if n_layers not in _CACHE:
        _CACHE[n_layers] = build_program(n_layers)
    nc = _CACHE[n_layers]
    in_maps = []
    for c in range(8):
        m = dict(w)
        m["x"] = np.ascontiguousarray(x[c])
        in_maps.append(m)
    res = run_bass_kernel_spmd(nc, in_maps, core_ids=list(range(8)))
    return np.stack([np.asarray(r["y"], np.float32) for r in res.results], axis=0)


def kernel(**inputs):
    return run(inputs, DEPTH)
```

```python
from contextlib import ExitStack
import numpy as np
import ml_dtypes
import concourse.bass as bass
import concourse.mybir as mybir
from concourse.bass_utils import run_bass_kernel_spmd

F32 = mybir.dt.float32
BF16 = mybir.dt.bfloat16
AF = mybir.ActivationFunctionType
ALU = mybir.AluOpType

D_MODEL = 1024
SEQ = 2048
DEPTH = 4
NT = 16
NG = 4
HEAD = 64
ALPHA = (2.0 * DEPTH) ** 0.25
LN_EPS = 1e-5
RMS_EPS = 1e-5
N_W_CHUNKS = 10


class Res:
    __slots__ = ("w", "r")

    def __init__(self):
        self.w = None
        self.r = {}


class Sched:
    ENGS = ("pe", "act", "dve", "pool", "sp")

    def __init__(self, nc, st, ndma=8):
        self.nc = nc
        self.eng = {"pe": nc.tensor, "act": nc.scalar, "dve": nc.vector, "pool": nc.gpsimd, "sp": nc.sync}
        self.cnt = {e: 0 for e in self.ENGS}
        self.waited = {e: {} for e in self.ENGS}
        self.sems = {}
        for e in self.ENGS:
            self.sems["s_" + e] = st.enter_context(nc.semaphore("s_" + e))
        self.ndma = ndma
        self.dma_names = {}
        self.dma_use = {}
        self.dma_rr = {}
        for q in ("sp", "act", "pool"):
            self.dma_names[q] = ["d_%s_%d" % (q, i) for i in range(ndma)]
            for n in self.dma_names[q]:
                self.sems[n] = st.enter_context(nc.semaphore(n))
            self.dma_use[q] = [0] * ndma
            self.dma_rr[q] = 0
        self.res = {}
        self.pending = {e: [] for e in self.ENGS}
        self.all_dma_toks = []
        self.n_ins = 0

    def R(self, name):
        r = self.res.get(name)
        if r is None:
            r = Res()
            self.res[name] = r
        return r

    def _deps(self, eng, reads, writes):
        need = {}
        for r in reads:
            t = self.R(r).w
            if t is not None and need.get(t[0], 0) < t[1]:
                need[t[0]] = t[1]
        for w in writes:
            rr = self.R(w)
            t = rr.w
            if t is not None and need.get(t[0], 0) < t[1]:
                need[t[0]] = t[1]
            for k, v in rr.r.items():
                if need.get(k, 0) < v:
                    need[k] = v
        wd = self.waited[eng]
        e = self.eng[eng]
        for k, v in need.items():
            if k == "s_pe" and eng == "pe":
                continue
            if wd.get(k, 0) >= v:
                continue
            wd[k] = v
            e.wait_ge(self.sems[k], v)
            self.n_ins += 1

    def _commit(self, tok, reads, writes):
        k, v = tok
        for r in reads:
            d = self.R(r).r
            if d.get(k, 0) < v:
                d[k] = v
        for w in writes:
            rr = self.R(w)
            rr.w = tok
            rr.r = {}

    def op(self, eng, fn, reads=(), writes=(), inc=True):
        self._deps(eng, reads, writes)
        ins = fn(self.eng[eng])
        self.n_ins += 1
        if not inc:
            self.pending[eng].append((reads, writes))
            return None
        self.cnt[eng] += 1
        tok = ("s_" + eng, self.cnt[eng])
        ins.then_inc(self.sems[tok[0]], 1)
        for (r, w) in self.pending[eng]:
            self._commit(tok, r, w)
        self.pending[eng] = []
        self._commit(tok, reads, writes)
        return tok

    def dma(self, q, fn, reads=(), writes=()):
        assert not self.pending[q]
        i = self.dma_rr[q]
        self.dma_rr[q] = (i + 1) % self.ndma
        key = self.dma_names[q][i]
        self._deps(q, reads, writes)
        prev = self.dma_use[q][i] * 16
        if prev > 0 and self.waited[q].get(key, 0) < prev:
            self.waited[q][key] = prev
            self.eng[q].wait_ge(self.sems[key], prev)
        self.dma_use[q][i] += 1
        tok = (key, self.dma_use[q][i] * 16)
        fn(self.eng[q]).then_inc(self.sems[key], 16)
        self.n_ins += 1
        self._commit(tok, reads, writes)
        return tok

    def barrier(self):
        for e in self.ENGS:
            assert not self.pending[e]
        targets = {}
        for e in self.ENGS:
            if self.cnt[e]:
                targets["s_" + e] = self.cnt[e]
        for q in self.dma_names:
            for i, n in enumerate(self.dma_names[q]):
                if self.dma_use[q][i]:
                    targets[n] = self.dma_use[q][i] * 16
        for e in self.ENGS:
            wd = self.waited[e]
            for k, v in targets.items():
                if k == "s_" + e:
                    continue
                if wd.get(k, 0) < v:
                    wd[k] = v
                    self.eng[e].wait_ge(self.sems[k], v)
                    self.n_ins += 1

    def wait_tok(self, eng, tok):
        k, v = tok
        if self.waited[eng].get(k, 0) < v:
            self.waited[eng][k] = v
            self.eng[eng].wait_ge(self.sems[k], v)


def AP(t, off, pat):
    return bass.AP(t, off, [list(p) for p in pat])


def _attn_perm(kp):
    idx = []
    for j in range(4):
        for r in range(2):
            h = (2 * kp + r) * 4 + j
            idx.extend(range(h * 64, h * 64 + 64))
    return np.array(idx)


def _in_col_chunks():
    q0, k0, v0, za0, zs0, x0 = 0, 1024, 1280, 1536, 2560, 3584
    ch = []
    ch.append(np.concatenate([np.arange(k0, k0 + 256), np.arange(v0, v0 + 256)]))
    for kp in range(2):
        p = _attn_perm(kp)
        ch.append(q0 + p)
        ch.append(za0 + p)
    ch.append(np.arange(x0 + 1024, x0 + 1536))
    for g in range(2):
        ch.append(np.arange(x0 + g * 512, x0 + (g + 1) * 512))
        ch.append(np.arange(zs0 + g * 512, zs0 + (g + 1) * 512))
    return ch


def _consts():
    s = np.arange(128)[:, None].astype(np.float64)
    q = np.arange(128)[None, :].astype(np.float64)
    slopes = np.exp2(-8.0 * np.arange(1, 17) / 16.0)
    masks = np.zeros((128, 4, 2, 4, 128), np.float32)
    for kv in range(4):
        for g in range(4):
            sl = slopes[kv * 4 + g]
            masks[:, kv, 0, g, :] = np.where(s <= q, np.exp(-sl * (q - s)), 0.0)
            masks[:, kv, 1, g, :] = np.where(s > q, np.exp(-sl * (128.0 + q - s)), 0.0)
    tri_le = (s <= q).astype(np.float32)
    tri_st = (s > q).astype(np.float32)
    ident = np.eye(128, dtype=np.float32)
    ones = np.ones((128, 128), np.float32)
    cb16 = np.concatenate([ident, tri_le, tri_st, ones], axis=1).astype(ml_dtypes.bfloat16)
    cf32 = np.concatenate([tri_le, ones], axis=1).astype(np.float32)
    return masks.reshape(128, 4096).astype(ml_dtypes.bfloat16), cb16, cf32


def prep_weights(inp, n_layers):
    L = n_layers
    w_in = np.asarray(inp["w_in"], np.float32)[:L]
    w_out = np.asarray(inp["w_out"], np.float32)[:L]
    chunks = _in_col_chunks()
    w_in_p = np.empty((L, N_W_CHUNKS, 128, 8, 512), np.float32)
    for c, cols in enumerate(chunks):
        w = w_in[:, :, cols]
        w_in_p[:, c] = w.reshape(L, 8, 128, 512).transpose(0, 2, 1, 3)
    w_dt = np.ascontiguousarray(w_in[:, :, 5120:5136].reshape(L, 8, 128, 16).transpose(0, 2, 1, 3))
    rows = np.concatenate([_attn_perm(0), _attn_perm(1), np.arange(1024, 2048)])
    w_out_p = np.ascontiguousarray(w_out[:, rows, :].reshape(L, 4, 4, 128, 1024).transpose(0, 1, 3, 2, 4))
    conv_w = np.asarray(inp["conv_w"], np.float32)[:L]
    conv_wT = np.ascontiguousarray(conv_w.reshape(L, 4, 12, 128).transpose(0, 3, 2, 1))
    conv_bT = np.ascontiguousarray(np.asarray(inp["conv_b"], np.float32)[:L].reshape(L, 12, 128).transpose(0, 2, 1))

    def rep(a):
        a = np.asarray(a, np.float32)[:L]
        return np.ascontiguousarray(np.broadcast_to(a[:, None, :], (L, 128, a.shape[1])))
    small = np.concatenate([rep(inp["dt_bias"]), rep(inp["a_log"]), rep(inp["d_skip"]), rep(inp["sinks"])], axis=2)
    masks, cb16, cf32 = _consts()
    return {
        "w_in_p": w_in_p, "w_dt": w_dt, "w_out_p": w_out_p, "conv_wT": conv_wT, "conv_bT": conv_bT,
        "small": np.ascontiguousarray(small), "ln_g": rep(inp["ln_g"]), "ln_b": rep(inp["ln_b"]),
        "normw": rep(inp["ssm_norm_w"]), "masks": masks, "cb16": cb16, "cf32": cf32,
    }


def build_program(n_layers=DEPTH, stop=99):
    L = n_layers
    nc = bass.Bass("TRN2", target_bir_lowering=False)
    x_d = nc.dram_tensor("x", [SEQ, D_MODEL], F32, kind="ExternalInput")
    w_in_d = nc.dram_tensor("w_in_p", [L, N_W_CHUNKS, 128, 8 * 512], F32, kind="ExternalInput")
    w_dt_d = nc.dram_tensor("w_dt", [L, 128, 8 * 16], F32, kind="ExternalInput")
    w_out_d = nc.dram_tensor("w_out_p", [L, 4, 128, 4 * 1024], F32, kind="ExternalInput")
    convw_d = nc.dram_tensor("conv_wT", [L, 128, 48], F32, kind="ExternalInput")
    convb_d = nc.dram_tensor("conv_bT", [L, 128, 12], F32, kind="ExternalInput")
    small_d = nc.dram_tensor("small", [L, 128, 64], F32, kind="ExternalInput")
    lng_d = nc.dram_tensor("ln_g", [L, 128, 1024], F32, kind="ExternalInput")
    lnb_d = nc.dram_tensor("ln_b", [L, 128, 1024], F32, kind="ExternalInput")
    normw_d = nc.dram_tensor("normw", [L, 128, 1024], F32, kind="ExternalInput")
    masks_d = nc.dram_tensor("masks", [128, 4096], BF16, kind="ExternalInput")
    cb16_d = nc.dram_tensor("cb16", [128, 512], BF16, kind="ExternalInput")
    cf32_d = nc.dram_tensor("cf32", [128, 256], F32, kind="ExternalInput")
    y_d = nc.dram_tensor("y", [SEQ, D_MODEL], F32, kind="ExternalOutput")

    with ExitStack() as st:
        def sb(name, cols, dt):
            return st.enter_context(nc.sbuf_tensor("sb_" + name, [128, cols], dt))

        def ps(name, cols, dt):
            return st.enter_context(nc.psum_tensor("ps_" + name, [128, cols], dt))

        S = Sched(nc, st)
        acc = sb("acc", NT * 1024, F32)
        xT = sb("xT", 8 * SEQ, BF16)
        cb16 = sb("cb16", 512, BF16)
        cf32 = sb("cf32", 256, F32)
        identb = cb16[:, 0:128]
        trileb = cb16[:, 128:256]
        tristb = cb16[:, 256:384]
        onesb = cb16[:, 384:512]
        trilef = cf32[:, 0:128]
        onesf = cf32[:, 128:256]
        wbuf = [sb("wbuf%d" % i, 8 * 512, BF16) for i in range(2)]
        wob = sb("wob", 4 * 1024, BF16)
        wdt = sb("wdt", 128, BF16)
        lng = sb("lng", 1024, F32)
        lnb = sb("lnb", 1024, F32)
        normw = sb("normw", 1024, F32)
        convw = sb("convw", 48, F32)
        convb = sb("convb", 12, F32)
        small = sb("small", 64, F32)
        mix4 = sb("mix4", 4 * 512, BF16)
        xb16 = sb("xb16", 1024, BF16)
        stats = sb("stats", 12, F32)
        mv = sb("mv", 2, F32)
        rstd = sb("rstd", 1, F32)
        PH_BYTES = 66 * 1024
        ph8 = sb("phase", PH_BYTES // 2, BF16)
        ph32 = ph8.bitcast(F32)

        class Carver:
            def __init__(self):
                self.off = 0

            def bf(self, n):
                o = self.off
                self.off += 2 * n
                assert self.off <= PH_BYTES, self.off
                return ph8[:, o // 2: o // 2 + n], o // 2

            def f32(self, n):
                self.off = (self.off + 3) // 4 * 4
                o = self.off
                self.off += 4 * n
                assert self.off <= PH_BYTES, self.off
                return ph32[:, o // 4: o // 4 + n], o // 4

        pj = [ps("pj%d" % i, 512, F32) for i in range(2)]
        sc = [ps("sc%d" % i, 512, F32) for i in range(2)]
        nd = [ps("nd%d" % i, 512, F32) for i in range(2)]
        st0 = ps("st0", 512, F32)
        trT = ps("tr", 1024, BF16)
        tr = [trT[:, 0:512], trT[:, 512:1024]]
        pj_rr = [0]

        def next_pj():
            i = pj_rr[0]
            pj_rr[0] = 1 - i
            return pj[i], "pj%d" % i

        S.dma("sp", lambda e: e.dma_start(out=cb16[:], in_=cb16_d[:]), writes=["cb16"])
        S.dma("sp", lambda e: e.dma_start(out=cf32[:], in_=cf32_d[:]), writes=["cf32"])

        xT3 = lambda kc, c0, n: xT[:, kc * SEQ + c0: kc * SEQ + c0 + n]

        def load_w_chunk(l, c, buf_i):
            S.dma("pool", lambda e: e.dma_start(out=wbuf[buf_i][:], in_=w_in_d[l, c]), writes=["wbuf%d" % buf_i])

        def load_wo(l, qi):
            S.dma("pool", lambda e: e.dma_start(out=wob[:], in_=w_out_d[l, qi]), writes=["wob"])

        def mm_group(items, reads, writes):
            n = len(items)
            for i, (o, a, b) in enumerate(items):
                S.op("pe", lambda e, o=o, a=a, b=b, i=i: e.matmul(o, lhsT=a, rhs=b, start=(i == 0), stop=(i == n - 1)),
                     reads=reads, writes=writes, inc=(i == n - 1))

        def inproj_fm(wb_i, col0, tg, bank, bank_name):
            items = [(bank[:, 0:512], wbuf[wb_i][:, kc * 512 + col0: kc * 512 + col0 + 128], xT3(kc, tg * 512, 512))
                     for kc in range(8)]
            mm_group(items, reads=["wbuf%d" % wb_i] + ["xT%d" % t for t in range(4 * tg, 4 * tg + 4)], writes=[bank_name])

        def inproj_tm(wb_i, col0, ncols, t, bank, bank_name):
            items = [(bank[:, 0:ncols], xT3(kc, t * 128, 128), wbuf[wb_i][:, kc * 512 + col0: kc * 512 + col0 + ncols])
                     for kc in range(8)]
            mm_group(items, reads=["wbuf%d" % wb_i, "xT%d" % t], writes=[bank_name])

        def to_xT(t):
            S.op("act", lambda e: e.activation(out=xb16[:], in_=acc[:, t * 1024:(t + 1) * 1024], func=AF.Copy),
                 reads=["acc%d" % t], writes=["xb16"])
            for half in range(2):
                for j in range(4):
                    kc = half * 4 + j
                    S.op("pe", lambda e, kc=kc, j=j, half=half: e.transpose(tr[half][:, j * 128:(j + 1) * 128],
                                                                           xb16[:, kc * 128:(kc + 1) * 128], identb),
                         reads=["xb16", "cb16"], writes=["tr"], inc=(j == 3))
                o = AP(xT, half * 4 * SEQ + t * 128, [[8 * SEQ, 128], [SEQ, 4], [1, 128]])
                i_ = AP(trT, half * 512, [[1024, 128], [128, 4], [1, 128]])
                S.op("dve", lambda e, o=o, i_=i_: e.tensor_copy(out=o, in_=i_), reads=["tr"], writes=["xT%d" % t])

        def outproj_partial(l, first, tg):
            for b in range(4):
                t = 4 * tg + b
                for ch in range(2):
                    bank, bn = next_pj()
                    items = [(bank[:, 0:512], mix4[:, c * 512 + b * 128: c * 512 + (b + 1) * 128],
                              wob[:, c * 1024 + ch * 512: c * 1024 + (ch + 1) * 512]) for c in range(4)]
                    mm_group(items, reads=["mix4", "wob"], writes=[bn])
                    a_ = acc[:, t * 1024 + ch * 512: t * 1024 + (ch + 1) * 512]
                    if first:
                        S.op("dve", lambda e, a_=a_, bank=bank: e.scalar_tensor_tensor(
                            out=a_, in0=a_, scalar=float(ALPHA), in1=bank[:, 0:512], op0=ALU.mult, op1=ALU.add),
                            reads=[bn, "acc%d" % t], writes=["acc%d" % t])
                    else:
                        S.op("dve", lambda e, a_=a_, bank=bank: e.tensor_tensor(out=a_, in0=a_, in1=bank[:, 0:512], op=ALU.add),
                             reads=[bn, "acc%d" % t], writes=["acc%d" % t])

        def layernorm_tile(l, t, last):
            a_ = acc[:, t * 1024:(t + 1) * 1024]
            rw = ["acc%d" % t]
            for h in range(2):
                S.op("dve", lambda e, h=h: e.bn_stats(out=stats[:, h * 6:(h + 1) * 6], in_=acc[:, t * 1024 + h * 512: t * 1024 + (h + 1) * 512]),
                     reads=rw, writes=["stats"])
            S.op("dve", lambda e: e.bn_aggr(out=mv[:], in_=stats[:]), reads=["stats"], writes=["mv"])
            S.op("act", lambda e: e.activation(out=rstd[:], in_=mv[:, 1:2], func=AF.Ln, bias=float(LN_EPS), scale=1.0),
                 reads=["mv"], writes=["rstd"])
            S.op("act", lambda e: e.activation(out=rstd[:], in_=rstd[:], func=AF.Exp, scale=-0.5), reads=["rstd"], writes=["rstd"])
            S.op("dve", lambda e: e.tensor_scalar(out=a_, in0=a_, scalar1=mv[:, 0:1], scalar2=rstd[:, 0:1],
                                                  op0=ALU.subtract, op1=ALU.mult), reads=rw + ["mv", "rstd"], writes=rw)
            S.op("pool", lambda e: e.tensor_tensor(out=a_, in0=a_, in1=lng[:], op=ALU.mult), reads=rw + ["lng"], writes=rw)
            S.op("pool", lambda e: e.tensor_tensor(out=a_, in0=a_, in1=lnb[:], op=ALU.add), reads=rw + ["lnb"], writes=rw)
            if last:
                S.all_dma_toks.append(S.dma("sp", lambda e: e.dma_start(out=y_d[t * 128:(t + 1) * 128, :], in_=a_), reads=rw))
            else:
                to_xT(t)

        def finalize():
            for t in range(NT):
                S.all_dma_toks.append(S.dma("sp", lambda e, t=t: e.dma_start(out=y_d[t * 128:(t + 1) * 128, :], in_=acc[:, t * 1024:(t + 1) * 1024]),
                                            reads=["acc%d" % t]))
            S.barrier()
            for tok in S.all_dma_toks:
                S.wait_tok("sp", tok)
            print("instructions emitted (stopped):", S.n_ins)
            return nc

        for t in range(NT):
            S.dma("sp", lambda e, t=t: e.dma_start(out=acc[:, t * 1024:(t + 1) * 1024], in_=x_d[t * 128:(t + 1) * 128, :]),
                  writes=["acc%d" % t])
        for t in range(NT):
            to_xT(t)

        if stop == 0:
            return finalize()
        for l in range(L):
            last = (l == L - 1)
            S.dma("sp", lambda e: e.dma_start(out=lng[:], in_=lng_d[l]), writes=["lng"])
            S.dma("sp", lambda e: e.dma_start(out=lnb[:], in_=lnb_d[l]), writes=["lnb"])
            S.dma("sp", lambda e: e.dma_start(out=normw[:], in_=normw_d[l]), writes=["normw"])
            S.dma("sp", lambda e: e.dma_start(out=convw[:], in_=convw_d[l]), writes=["convw"])
            S.dma("sp", lambda e: e.dma_start(out=convb[:], in_=convb_d[l]), writes=["convb"])
            S.dma("sp", lambda e: e.dma_start(out=small[:], in_=small_d[l]), writes=["small"])
            S.dma("pool", lambda e: e.dma_start(out=wdt[:], in_=w_dt_d[l]), writes=["wdt"])
            dtb = small[:, 0:16]
            alog = small[:, 16:32]
            dsk = small[:, 32:48]
            snk = small[:, 48:64]

            cv = Carver()
            masks, _ = cv.bf(4096)
            kT, kT_o = cv.bf(2 * SEQ)
            vt, vt_o = cv.bf(NT * 256)
            q4, q4_o = cv.bf(4 * 512)
            z4, z4_o = cv.bf(4 * 512)
            Eb = [cv.bf(512) for _ in range(8)]
            sinkb, _ = cv.bf(2 * 512)
            sinkexp, se_o = cv.f32(2 * 512)
            dS, dS_o = cv.f32(512)
            wgt, wgt_o = cv.f32(512)
            es, es_o = cv.f32(16)
            S.barrier()
            S.dma("sp", lambda e: e.dma_start(out=masks, in_=masks_d[:]), reads=[], writes=["masks"])
            load_w_chunk(l, 0, 0)
            load_w_chunk(l, 1, 1)
            S.op("act", lambda e: e.activation(out=es, in_=snk, func=AF.Exp), reads=["small"], writes=["es"])
            for kp in range(2):
                for r in range(2):
                    o = AP(ph32, se_o + kp * 512 + r * 64 * (PH_BYTES // 4), [[PH_BYTES // 4, 64], [128, 4], [1, 128]])
                    hh = (2 * kp + r) * 4
                    i_ = AP(ph32, es_o + hh + r * 64 * (PH_BYTES // 4), [[PH_BYTES // 4, 64], [1, 4], [0, 128]])
                    S.op("dve", lambda e, o=o, i_=i_: e.tensor_copy(out=o, in_=i_), reads=["es"], writes=["sinkexp"])
            S.op("dve", lambda e: e.tensor_copy(out=sinkb, in_=sinkexp), reads=["sinkexp"], writes=["sinkb"])
            for tg in range(NG):
                for c in range(2):
                    bank, bn = next_pj()
                    inproj_fm(0, c * 128, tg, bank, bn)
                    S.op("dve", lambda e, c=c, tg=tg, bank=bank: e.tensor_copy(
                        out=kT[:, c * SEQ + tg * 512: c * SEQ + (tg + 1) * 512], in_=bank[:, 0:512]),
                        reads=[bn], writes=["kT%d" % tg])
            for t in range(NT):
                bank, bn = next_pj()
                inproj_tm(0, 256, 256, t, bank, bn)
                S.op("act", lambda e, t=t, bank=bank: e.activation(out=vt[:, t * 256:(t + 1) * 256], in_=bank[:, 0:256], func=AF.Copy),
                     reads=[bn], writes=["v%d" % t])
            if stop == 1:
                return finalize()
            trF = trT.bitcast(F32)
            sbanks = [(sc[0], "sc0"), (sc[1], "sc1"), (st0, "st0"), (trF, "tr")]

            def att_S(kp, tg, b):
                n = 4 * tg + b
                par = b % 2
                k_ = 0
                for r in range(2):
                    kv = 2 * kp + r
                    rows = slice(r * 64, r * 64 + 64)
                    blocks = ([n - 1] if n > 0 else []) + [n]
                    for sblk in blocks:
                        pc = 0 if sblk == n else 1
                        E, _eo = Eb[par * 4 + k_]
                        en = "E%d" % (par * 4 + k_)
                        scb, scn = sbanks[k_]
                        k_ += 1
                        lhsT = kT[rows, kp * SEQ + sblk * 128: kp * SEQ + (sblk + 1) * 128]
                        rhs = AP(ph8, q4_o + b * 128 + r * 64 * (PH_BYTES // 2), [[PH_BYTES // 2, 64], [512, 4], [1, 128]])
                        S.op("pe", lambda e, scb=scb, lhsT=lhsT, rhs=rhs: e.matmul(scb[:, 0:512], lhsT=lhsT, rhs=rhs, start=True, stop=True),
                             reads=["kT%d" % (sblk // 4), "q4"], writes=[scn])
                        S.op("act", lambda e, E=E, scb=scb: e.activation(out=E, in_=scb[:, 0:512], func=AF.Exp, scale=0.125),
                             reads=[scn], writes=[en])
                        m_ = masks[:, (kv * 2 + pc) * 512:(kv * 2 + pc + 1) * 512]
                        S.op("dve", lambda e, E=E, m_=m_: e.tensor_tensor(out=E, in0=E, in1=m_, op=ALU.mult),
                             reads=[en, "masks"], writes=[en])

            def att_V(kp, tg, b):
                n = 4 * tg + b
                par = b % 2
                k_ = 0
                for r in range(2):
                    kv = 2 * kp + r
                    rows = slice(r * 64, r * 64 + 64)
                    blocks = ([n - 1] if n > 0 else []) + [n]
                    for bi, sblk in enumerate(blocks):
                        E, _eo = Eb[par * 4 + k_]
                        en = "E%d" % (par * 4 + k_)
                        k_ += 1
                        first_b = (bi == 0)
                        last_b = (bi == len(blocks) - 1)
                        vl = vt[:, sblk * 256 + kv * 64: sblk * 256 + (kv + 1) * 64]
                        S.op("pe", lambda e, vl=vl, E=E, rows=rows, first_b=first_b, last_b=last_b: e.matmul(
                            nd[0][rows, 0:512], lhsT=vl, rhs=E, start=first_b, stop=last_b),
                            reads=[en, "v%d" % sblk], writes=["nd0"], inc=False)
                        S.op("pe", lambda e, E=E, rows=rows, first_b=first_b, last_b=last_b: e.matmul(
                            nd[1][rows, 0:512], lhsT=onesb[:, 0:64], rhs=E, start=first_b, stop=last_b),
                            reads=[en, "cb16"], writes=["nd1"], inc=True)
                S.op("dve", lambda e: e.tensor_tensor(out=dS, in0=nd[1][:, 0:512], in1=sinkexp[:, kp * 512:(kp + 1) * 512], op=ALU.add),
                     reads=["nd1", "sinkexp"], writes=["dS"])
                S.op("act", lambda e: e.activation(out=dS, in_=dS, func=AF.Ln), reads=["dS"], writes=["dS"])
                S.op("act", lambda e: e.activation(out=dS, in_=dS, func=AF.Exp, scale=-1.0), reads=["dS"], writes=["dS"])
                S.op("dve", lambda e: e.tensor_tensor(out=wgt, in0=nd[0][:, 0:512], in1=dS, op=ALU.mult),
                     reads=["nd0", "dS"], writes=["wgt"])
                zv = AP(ph8, z4_o + b * 128, [[PH_BYTES // 2, 128], [512, 4], [1, 128]])
                w3 = AP(ph32, wgt_o, [[PH_BYTES // 4, 128], [128, 4], [1, 128]])
                mo = AP(mix4, b * 128, [[2048, 128], [512, 4], [1, 128]])
                S.op("pool", lambda e, zv=zv, w3=w3, mo=mo: e.tensor_tensor(out=mo, in0=w3, in1=zv, op=ALU.mult),
                     reads=["wgt", "z4"], writes=["mix4"])

            for kp in range(2):
                wq = 1
                load_w_chunk(l, 2 + 2 * kp, 0)
                load_wo(l, kp)
                for tg in range(NG):
                    for j in range(4):
                        bank, bn = next_pj()
                        inproj_fm(wq, j * 128, tg, bank, bn)
                        S.op("dve", lambda e, j=j, bank=bank: e.tensor_copy(out=q4[:, j * 512:(j + 1) * 512], in_=bank[:, 0:512]),
                             reads=[bn], writes=["q4"])
                    for j in range(4):
                        bank, bn = next_pj()
                        inproj_fm(0, j * 128, tg, bank, bn)
                        S.op("act", lambda e, j=j, bank=bank: e.activation(out=z4[:, j * 512:(j + 1) * 512], in_=bank[:, 0:512], func=AF.Silu),
                             reads=[bn], writes=["z4"])
                    att_S(kp, tg, 0)
                    for b in range(4):
                        if b < 3:
                            att_S(kp, tg, b + 1)
                        att_V(kp, tg, b)
                    if tg == NG - 1 and kp == 0:
                        load_w_chunk(l, 3, 1)
                    outproj_partial(l, kp == 0, tg)
            if stop == 2:
                return finalize()
            cv = Carver()
            bcT, bc_o = cv.bf(4 * SEQ)
            ubuf, u_o = cv.bf(4 * 515)
            xsT, xsT_o = cv.bf(4 * 512)
            diag, diag_o = cv.bf(16 * 128)
            xdt = [cv.bf(512)[0] for _ in range(2)]
            xdtd = [cv.bf(512)[0] for _ in range(2)]
            xsD = [cv.bf(512)[0] for _ in range(2)]
            zs4, _ = cv.bf(4 * 512)
            Btm = [cv.bf(128)[0] for _ in range(2)]
            cbm, cbm_o = cv.bf(128)
            rhsD, rhsD_o = cv.bf(1024)
            _e8 = [cv.bf(1024) for _ in range(2)]
            E8 = [x_[0] for x_ in _e8]
            E8_o = [x_[1] for x_ in _e8]
            Sbf, _ = cv.bf(512)
            gn, _ = cv.bf(512)
            dt16, dt_o = cv.f32(256)
            dtA, dtA_o = cv.f32(256)
            ea, ea_o = cv.f32(256)
            dtdte, dd_o = cv.f32(256)
            cd, cd_o = cv.f32(256)
            tmpA, _ = cv.f32(256)
            tmpB, _ = cv.f32(256)
            nega, nega_o2 = cv.f32(16)
            S32, _ = cv.f32(512)
            t1, t1_o = cv.f32(512)
            junk = ph8[:, 2 * t1_o: 2 * t1_o + 512]
            yb, _ = cv.f32(512)
            ss, _ = cv.f32(1)
            rs, _ = cv.f32(1)
            PB = PH_BYTES
            S.barrier()
            S.op("pool", lambda e: e.memset(ubuf, 0.0), reads=[], writes=["ubuf"])
            load_w_chunk(l, 5, 1)
            for t in range(NT):
                items = [(st0[:, t * 16:(t + 1) * 16], xT3(kc, t * 128, 128), wdt[:, kc * 16:(kc + 1) * 16]) for kc in range(8)]
                mm_group(items, reads=["wdt", "xT%d" % t], writes=["st0"])
            dtb_b = AP(small, 0, [[64, 128], [0, 16], [1, 16]])
            v3 = lambda o_: AP(ph32, o_, [[PB // 4, 128], [16, 16], [1, 16]])
            S.op("dve", lambda e: e.tensor_tensor(out=v3(dt_o), in0=AP(st0, 0, [[512, 128], [16, 16], [1, 16]]), in1=dtb_b, op=ALU.add),
                 reads=["st0", "small"], writes=["dt16"])
            S.op("dve", lambda e: e.tensor_scalar(out=tmpA, in0=dt16, scalar1=-1.0, scalar2=None, op0=ALU.mult), reads=["dt16"], writes=["tmpA"])
            S.op("dve", lambda e: e.tensor_tensor(out=tmpA, in0=tmpA, in1=dt16, op=ALU.max), reads=["dt16", "tmpA"], writes=["tmpA"])
            S.op("act", lambda e: e.activation(out=tmpA, in_=tmpA, func=AF.Exp, scale=-1.0), reads=["tmpA"], writes=["tmpA"])
            S.op("act", lambda e: e.activation(out=tmpA, in_=tmpA, func=AF.Ln, bias=1.0), reads=["tmpA"], writes=["tmpA"])
            S.op("dve", lambda e: e.scalar_tensor_tensor(out=dt16, in0=dt16, scalar=0.0, in1=tmpA, op0=ALU.max, op1=ALU.add),
                 reads=["dt16", "tmpA"], writes=["dt16"])
            S.op("act", lambda e: e.activation(out=nega, in_=alog, func=AF.Exp), reads=["small"], writes=["nega"])
            S.op("dve", lambda e: e.tensor_scalar(out=nega, in0=nega, scalar1=-1.0, scalar2=None, op0=ALU.mult), reads=["nega"], writes=["nega"])
            nega_o = nega_o2
            S.op("dve", lambda e: e.tensor_tensor(out=v3(dtA_o), in0=v3(dt_o), in1=AP(ph32, nega_o, [[PB // 4, 128], [0, 16], [1, 16]]), op=ALU.mult),
                 reads=["dt16", "nega"], writes=["dtA"])
            S.op("pe", lambda e: e.matmul(st0[:, 0:256], lhsT=trilef, rhs=dtA, start=True, stop=True), reads=["cf32", "dtA", "dt16"], writes=["st0"])
            S.op("pe", lambda e: e.matmul(st0[:, 256:512], lhsT=onesf, rhs=dtA, start=True, stop=True), reads=["cf32", "dtA"], writes=["st0"])
            S.op("act", lambda e: e.activation(out=ea, in_=st0[:, 0:256], func=AF.Exp), reads=["st0"], writes=["ea"])
            S.op("act", lambda e: e.activation(out=cd, in_=st0[:, 256:512], func=AF.Exp), reads=["st0"], writes=["cd"])
            S.op("act", lambda e: e.activation(out=tmpB, in_=st0[:, 0:256], func=AF.Identity), reads=["st0"], writes=["tmpB"])
            S.op("dve", lambda e: e.tensor_tensor(out=tmpB, in0=st0[:, 256:512], in1=tmpB, op=ALU.subtract), reads=["st0", "tmpB"], writes=["tmpB"])
            S.op("act", lambda e: e.activation(out=tmpB, in_=tmpB, func=AF.Exp), reads=["tmpB"], writes=["tmpB"])
            S.op("dve", lambda e: e.tensor_tensor(out=dtdte, in0=dt16, in1=tmpB, op=ALU.mult), reads=["dt16", "tmpB"], writes=["dtdte"])

            if stop == 3:
                return finalize()

            def build_diag(cc0):
                for j4 in range(4):
                    for tap in range(4):
                        o = diag[:, (j4 * 4 + tap) * 128:(j4 * 4 + tap + 1) * 128]
                        s_ = convw[:, (cc0 + j4) * 4 + tap:(cc0 + j4) * 4 + tap + 1]
                        S.op("pool", lambda e, o=o, s_=s_: e.tensor_scalar(out=o, in0=identb, scalar1=s_, scalar2=None, op0=ALU.mult),
                             reads=["cb16", "convw"], writes=["diag"])

            def conv_group(wb_i, cc0, tg, dst, dst_stride, dst_res):
                if tg > 0:
                    S.op("dve", lambda e: e.tensor_copy(out=AP(ph8, u_o, [[PB // 2, 128], [515, 4], [1, 3]]),
                                                        in_=AP(ph8, u_o + 512, [[PB // 2, 128], [515, 4], [1, 3]])),
                         reads=["ubuf"], writes=["ubuf"])
                else:
                    S.op("pool", lambda e: e.memset(AP(ph8, u_o, [[PB // 2, 128], [515, 4], [1, 3]]), 0.0), reads=[], writes=["ubuf"])
                for j in range(4):
                    bank, bn = next_pj()
                    inproj_fm(wb_i, j * 128, tg, bank, bn)
                    S.op("act", lambda e, j=j, bank=bank: e.activation(out=ubuf[:, j * 515 + 3: j * 515 + 515], in_=bank[:, 0:512], func=AF.Copy),
                         reads=[bn], writes=["ubuf"])
                for j in range(4):
                    bank, bn = next_pj()
                    items = [(bank[:, 0:512], diag[:, (j * 4 + tap) * 128:(j * 4 + tap + 1) * 128],
                              ubuf[:, j * 515 + tap: j * 515 + tap + 512]) for tap in range(4)]
                    mm_group(items, reads=["diag", "ubuf"], writes=[bn])
                    o = dst(j)
                    S.op("act", lambda e, o=o, bank=bank, j=j: e.activation(out=o, in_=bank[:, 0:512], func=AF.Silu,
                                                                             bias=convb[:, cc0 + j: cc0 + j + 1]),
                         reads=[bn, "convb"], writes=[dst_res])

            build_diag(8)
            for tg in range(NG):
                conv_group(1, 8, tg, lambda j, tg=tg: bcT[:, j * SEQ + tg * 512: j * SEQ + (tg + 1) * 512], None, "bcT%d" % tg)
            if stop == 4:
                return finalize()
            def ssd_I(g, tg, b):
                c = 4 * tg + b
                t = c
                p_ = b % 2
                for j in range(4):
                    S.op("pe", lambda e, j=j: e.transpose(tr[0][:, j * 128:(j + 1) * 128], xsT[:, j * 512 + b * 128: j * 512 + (b + 1) * 128], identb),
                         reads=["xsT", "cb16"], writes=["tr"], inc=(j == 3))
                tr3 = AP(trT, 0, [[1024, 128], [64, 8], [1, 64]])
                for dst_, nm_, sc_o in ((xdt[p_], "xdt%d" % p_, dt_o), (xdtd[p_], "xdtd%d" % p_, dd_o)):
                    S.op("dve", lambda e, dst_=dst_, sc_o=sc_o: e.tensor_tensor(
                        out=dst_.rearrange("p (h d) -> p h d", h=8), in0=tr3, in1=hb(g, sc_o, t), op=ALU.mult),
                        reads=["tr", "dt16", "dtdte"], writes=[nm_])
                dsk_b = AP(small, 32 + 8 * g, [[64, 128], [1, 8], [0, 64]])
                S.op("dve", lambda e: e.tensor_tensor(out=xsD[p_].rearrange("p (h d) -> p h d", h=8), in0=tr3, in1=dsk_b, op=ALU.mult),
                     reads=["tr", "small"], writes=["xsD%d" % p_])
                S.op("pe", lambda e: e.transpose(tr[1][:, 0:128], bcT[:, g * SEQ + c * 128: g * SEQ + (c + 1) * 128], identb),
                     reads=["bcT%d" % tg, "cb16"], writes=["tr"])
                S.op("act", lambda e: e.activation(out=Btm[p_], in_=tr[1][:, 0:128], func=AF.Copy), reads=["tr"], writes=["Btm%d" % p_])
                bank, bn = next_pj()
                S.op("pe", lambda e, bank=bank: e.matmul(bank[:, 0:128], lhsT=bcT[:, g * SEQ + c * 128: g * SEQ + (c + 1) * 128],
                                                         rhs=bcT[:, (2 + g) * SEQ + c * 128: (2 + g) * SEQ + (c + 1) * 128], start=True, stop=True),
                     reads=["bcT%d" % tg], writes=[bn])
                S.op("dve", lambda e, bank=bank: e.tensor_tensor(out=cbm, in0=bank[:, 0:128], in1=trileb, op=ALU.mult),
                     reads=[bn, "cb16"], writes=["cbm"])
                S.op("pool", lambda e: e.tensor_tensor(out=AP(ph8, rhsD_o, [[PB // 2, 128], [128, 8], [1, 128]]),
                                                       in0=AP(cf32, 0, [[256, 128], [0, 8], [1, 128]]),
                                                       in1=AP(ph32, dtA_o + t * 16 + 8 * g, [[PB // 4, 128], [1, 8], [0, 128]]), op=ALU.mult),
                     reads=["cf32", "dtA"], writes=["rhsD"])
                eo = E8_o[p_]
                for hf in range(2):
                    S.op("pe", lambda e, hf=hf: e.matmul(sc[hf][:, 0:512], lhsT=tristb, rhs=rhsD[:, hf * 512:(hf + 1) * 512], start=True, stop=True),
                         reads=["rhsD", "cb16"], writes=["sc%d" % hf])
                    S.op("act", lambda e, hf=hf: e.activation(out=E8[p_][:, hf * 512:(hf + 1) * 512], in_=sc[hf][:, 0:512], func=AF.Exp),
                         reads=["sc%d" % hf], writes=["E8%d" % p_])
                S.op("dve", lambda e: e.tensor_tensor(out=AP(ph8, eo, [[PB // 2, 128], [128, 8], [1, 128]]),
                                                      in0=AP(ph8, eo, [[PB // 2, 128], [128, 8], [1, 128]]),
                                                      in1=AP(ph8, cbm_o, [[PB // 2, 128], [0, 8], [1, 128]]), op=ALU.mult),
                     reads=["E8%d" % p_, "cbm"], writes=["E8%d" % p_])

            def ssd_D(g, tg, b):
                c = 4 * tg + b
                t = c
                p_ = b % 2
                G_ = E8[p_]
                S.op("pe", lambda e: e.matmul(nd[0][:, 0:512], lhsT=identb, rhs=xsD[p_], start=True, stop=False),
                     reads=["xsD%d" % p_, "cb16"], writes=["nd0"], inc=False)
                for h in range(8):
                    S.op("pe", lambda e, h=h: e.matmul(nd[0][:, h * 64:(h + 1) * 64], lhsT=G_[:, h * 128:(h + 1) * 128],
                                                       rhs=xdt[p_][:, h * 64:(h + 1) * 64], start=False, stop=(h == 7)),
                         reads=["E8%d" % p_, "xdt%d" % p_], writes=["nd0"], inc=(h == 7))
                S.op("pe", lambda e: e.matmul(nd[1][:, 0:512], lhsT=bcT[:, (2 + g) * SEQ + c * 128: (2 + g) * SEQ + (c + 1) * 128],
                                              rhs=Sbf, start=True, stop=True), reads=["bcT%d" % tg, "Sbf"], writes=["nd1"])
                if c < NT - 1:
                    S.op("pe", lambda e: e.matmul(st0[:, 0:512], lhsT=Btm[p_], rhs=xdtd[p_], start=True, stop=True),
                         reads=["Btm%d" % p_, "xdtd%d" % p_], writes=["st0"])
                    S.op("dve", lambda e: e.tensor_tensor(out=S32.rearrange("p (h d) -> p h d", h=8), in0=S32.rearrange("p (h d) -> p h d", h=8),
                                                          in1=hb(g, cd_o, t), op=ALU.mult), reads=["S32", "cd"], writes=["S32"])
                    S.op("dve", lambda e: e.tensor_tensor(out=Sbf, in0=S32, in1=st0[:, 0:512], op=ALU.add), reads=["S32", "st0"], writes=["Sbf"])
                    S.op("dve", lambda e: e.tensor_tensor(out=S32, in0=S32, in1=st0[:, 0:512], op=ALU.add), reads=["S32", "st0"], writes=["S32"])
                S.op("dve", lambda e: e.tensor_tensor(out=t1.rearrange("p (h d) -> p h d", h=8), in0=AP(nd[1], 0, [[512, 128], [64, 8], [1, 64]]),
                                                      in1=hb(g, ea_o, t), op=ALU.mult), reads=["nd1", "ea"], writes=["t1"])
                S.op("dve", lambda e: e.tensor_tensor(out=yb, in0=nd[0][:, 0:512], in1=t1, op=ALU.add), reads=["nd0", "t1"], writes=["yb"])
                S.op("dve", lambda e: e.tensor_tensor(out=yb, in0=yb, in1=zs4[:, b * 512:(b + 1) * 512], op=ALU.mult), reads=["yb", "zs4"], writes=["yb"])
                S.op("act", lambda e: e.activation(out=junk, in_=yb, func=AF.Square, accum_out=ss), reads=["yb", "t1"], writes=["t1", "ss"])
                S.op("act", lambda e: e.activation(out=rs, in_=ss, func=AF.Ln, bias=float(RMS_EPS), scale=1.0 / 512.0), reads=["ss"], writes=["rs"])
                S.op("act", lambda e: e.activation(out=rs, in_=rs, func=AF.Exp, scale=-0.5), reads=["rs"], writes=["rs"])

            def ssd_D2(g, tg, b):
                S.op("dve", lambda e: e.scalar_tensor_tensor(out=gn, in0=yb, scalar=rs[:, 0:1], in1=normw[:, g * 512:(g + 1) * 512],
                                                             op0=ALU.mult, op1=ALU.mult), reads=["yb", "rs", "normw"], writes=["gn"])
                for j in range(4):
                    S.op("pe", lambda e, j=j: e.transpose(tr[1][:, j * 128:(j + 1) * 128], gn[:, j * 128:(j + 1) * 128], identb),
                         reads=["gn", "cb16"], writes=["tr"], inc=(j == 3))
                S.op("act", lambda e: e.activation(out=AP(mix4, b * 128, [[2048, 128], [512, 4], [1, 128]]),
                                                   in_=AP(trT, 512, [[1024, 128], [128, 4], [1, 128]]), func=AF.Copy),
                     reads=["tr"], writes=["mix4"])

            hb = lambda g, o_, t: AP(ph32, o_ + t * 16 + 8 * g, [[PB // 4, 128], [1, 8], [0, 64]])
            for g in range(2):
                load_w_chunk(l, 6 + 2 * g, 0)
                load_w_chunk(l, 7 + 2 * g, 1)
                load_wo(l, 2 + g)
                build_diag(4 * g)
                S.op("pool", lambda e: e.memset(S32, 0.0), reads=[], writes=["S32"])
                S.op("pool", lambda e: e.memset(Sbf, 0.0), reads=[], writes=["Sbf"])
                for tg in range(NG):
                    conv_group(0, 4 * g, tg, lambda j: xsT[:, j * 512:(j + 1) * 512], None, "xsT")
                    for b in range(4):
                        bank, bn = next_pj()
                        inproj_tm(1, 0, 512, 4 * tg + b, bank, bn)
                        S.op("act", lambda e, bank=bank, b=b: e.activation(out=zs4[:, b * 512:(b + 1) * 512], in_=bank[:, 0:512], func=AF.Silu),
                             reads=[bn], writes=["zs4"])
                    ssd_I(g, tg, 0)
                    ssd_I(g, tg, 1)
                    ssd_D(g, tg, 0)
                    for b in range(1, 4):
                        if b < 3:
                            ssd_I(g, tg, b + 1)
                        ssd_D2(g, tg, b - 1)
                        ssd_D(g, tg, b)
                    ssd_D2(g, tg, 3)
                    outproj_partial(l, False, tg)
                    if g == 1:
                        for b in range(4):
                            layernorm_tile(l, 4 * tg + b, last)
        for tok in S.all_dma_toks:
            S.wait_tok("sp", tok)
        print("instructions emitted:", S.n_ins)
    return nc


_CACHE = {}


def run(inputs, n_layers=DEPTH):
    x = np.asarray(inputs["x"], np.float32)
    w = prep_weights(inputs, n_layers)
    if n_layers not in _CACHE:
        _CACHE[n_layers] = build_program(n_layers)
    nc = _CACHE[n_layers]
    in_maps = []
    for c in range(8):
        m = dict(w)
        m["x"] = np.ascontiguousarray(x[c])
        in_maps.append(m)
    res = run_bass_kernel_spmd(nc, in_maps, core_ids=list(range(8)))
    return np.stack([np.asarray(r["y"], np.float32) for r in res.results], axis=0)


def kernel(**inputs):
    return run(inputs, DEPTH)
```

```python
from contextlib import ExitStack
import numpy as np
import ml_dtypes
import concourse.bass as bass
import concourse.mybir as mybir
from concourse.bass_utils import run_bass_kernel_spmd

F32 = mybir.dt.float32
BF16 = mybir.dt.bfloat16
AF = mybir.ActivationFunctionType
ALU = mybir.AluOpType

D_MODEL = 1024
SEQ = 2048
DEPTH = 4
NT = 16
NG = 4
HEAD = 64
ALPHA = (2.0 * DEPTH) ** 0.25
LN_EPS = 1e-5
RMS_EPS = 1e-5
N_W_CHUNKS = 10


class Res:
    __slots__ = ("w", "r")

    def __init__(self):
        self.w = None
        self.r = {}


class Sched:
    ENGS = ("pe", "act", "dve", "pool", "sp")

    def __init__(self, nc, st, ndma=8):
        self.nc = nc
        self.eng = {"pe": nc.tensor, "act": nc.scalar, "dve": nc.vector, "pool": nc.gpsimd, "sp": nc.sync}
        self.cnt = {e: 0 for e in self.ENGS}
        self.waited = {e: {} for e in self.ENGS}
        self.sems = {}
        for e in self.ENGS:
            self.sems["s_" + e] = st.enter_context(nc.semaphore("s_" + e))
        self.ndma = ndma
        self.dma_names = {}
        self.dma_use = {}
        self.dma_rr = {}
        for q in ("sp", "act", "pool"):
            self.dma_names[q] = ["d_%s_%d" % (q, i) for i in range(ndma)]
            for n in self.dma_names[q]:
                self.sems[n] = st.enter_context(nc.semaphore(n))
            self.dma_use[q] = [0] * ndma
            self.dma_rr[q] = 0
        self.res = {}
        self.pending = {e: [] for e in self.ENGS}
        self.all_dma_toks = []
        self.n_ins = 0

    def R(self, name):
        r = self.res.get(name)
        if r is None:
            r = Res()
            self.res[name] = r
        return r

    def _deps(self, eng, reads, writes):
        need = {}
        for r in reads:
            t = self.R(r).w
            if t is not None and need.get(t[0], 0) < t[1]:
                need[t[0]] = t[1]
        for w in writes:
            rr = self.R(w)
            t = rr.w
            if t is not None and need.get(t[0], 0) < t[1]:
                need[t[0]] = t[1]
            for k, v in rr.r.items():
                if need.get(k, 0) < v:
                    need[k] = v
        wd = self.waited[eng]
        e = self.eng[eng]
        for k, v in need.items():
            if k == "s_pe" and eng == "pe":
                continue
            if wd.get(k, 0) >= v:
                continue
            wd[k] = v
            e.wait_ge(self.sems[k], v)
            self.n_ins += 1

    def _commit(self, tok, reads, writes):
        k, v = tok
        for r in reads:
            d = self.R(r).r
            if d.get(k, 0) < v:
                d[k] = v
        for w in writes:
            rr = self.R(w)
            rr.w = tok
            rr.r = {}

    def op(self, eng, fn, reads=(), writes=(), inc=True):
        self._deps(eng, reads, writes)
        ins = fn(self.eng[eng])
        self.n_ins += 1
        if not inc:
            self.pending[eng].append((reads, writes))
            return None
        self.cnt[eng] += 1
        tok = ("s_" + eng, self.cnt[eng])
        ins.then_inc(self.sems[tok[0]], 1)
        for (r, w) in self.pending[eng]:
            self._commit(tok, r, w)
        self.pending[eng] = []
        self._commit(tok, reads, writes)
        return tok

    def dma(self, q, fn, reads=(), writes=()):
        assert not self.pending[q]
        i = self.dma_rr[q]
        self.dma_rr[q] = (i + 1) % self.ndma
        key = self.dma_names[q][i]
        self._deps(q, reads, writes)
        prev = self.dma_use[q][i] * 16
        if prev > 0 and self.waited[q].get(key, 0) < prev:
            self.waited[q][key] = prev
            self.eng[q].wait_ge(self.sems[key], prev)
        self.dma_use[q][i] += 1
        tok = (key, self.dma_use[q][i] * 16)
        fn(self.eng[q]).then_inc(self.sems[key], 16)
        self.n_ins += 1
        self._commit(tok, reads, writes)
        return tok

    def barrier(self):
        for e in self.ENGS:
            assert not self.pending[e]
        targets = {}
        for e in self.ENGS:
            if self.cnt[e]:
                targets["s_" + e] = self.cnt[e]
        for q in self.dma_names:
            for i, n in enumerate(self.dma_names[q]):
                if self.dma_use[q][i]:
                    targets[n] = self.dma_use[q][i] * 16
        for e in self.ENGS:
            wd = self.waited[e]
            for k, v in targets.items():
                if k == "s_" + e:
                    continue
                if wd.get(k, 0) < v:
                    wd[k] = v
                    self.eng[e].wait_ge(self.sems[k], v)
                    self.n_ins += 1

    def wait_tok(self, eng, tok):
        k, v = tok
        if self.waited[eng].get(k, 0) < v:
            self.waited[eng][k] = v
            self.eng[eng].wait_ge(self.sems[k], v)


def AP(t, off, pat):
    return bass.AP(t, off, [list(p) for p in pat])


def _attn_perm(kp):
    idx = []
    for j in range(4):
        for r in range(2):
            h = (2 * kp + r) * 4 + j
            idx.extend(range(h * 64, h * 64 + 64))
    return np.array(idx)


def _in_col_chunks():
    q0, k0, v0, za0, zs0, x0 = 0, 1024, 1280, 1536, 2560, 3584
    ch = []
    ch.append(np.concatenate([np.arange(k0, k0 + 256), np.arange(v0, v0 + 256)]))
    for kp in range(2):
        p = _attn_perm(kp)
        ch.append(q0 + p)
        ch.append(za0 + p)
    ch.append(np.arange(x0 + 1024, x0 + 1536))
    for g in range(2):
        ch.append(np.arange(x0 + g * 512, x0 + (g + 1) * 512))
        ch.append(np.arange(zs0 + g * 512, zs0 + (g + 1) * 512))
    return ch


def _consts():
    s = np.arange(128)[:, None].astype(np.float64)
    q = np.arange(128)[None, :].astype(np.float64)
    slopes = np.exp2(-8.0 * np.arange(1, 17) / 16.0)
    masks = np.zeros((128, 4, 2, 4, 128), np.float32)
    for kv in range(4):
        for g in range(4):
            sl = slopes[kv * 4 + g]
            masks[:, kv, 0, g, :] = np.where(s <= q, np.exp(-sl * (q - s)), 0.0)
            masks[:, kv, 1, g, :] = np.where(s > q, np.exp(-sl * (128.0 + q - s)), 0.0)
    tri_le = (s <= q).astype(np.float32)
    tri_st = (s > q).astype(np.float32)
    ident = np.eye(128, dtype=np.float32)
    ones = np.ones((128, 128), np.float32)
    cb16 = np.concatenate([ident, tri_le, tri_st, ones], axis=1).astype(ml_dtypes.bfloat16)
    cf32 = np.concatenate([tri_le, ones], axis=1).astype(np.float32)
    return masks.reshape(128, 4096).astype(ml_dtypes.bfloat16), cb16, cf32


def prep_weights(inp, n_layers):
    L = n_layers
    w_in = np.asarray(inp["w_in"], np.float32)[:L]
    w_out = np.asarray(inp["w_out"], np.float32)[:L]
    chunks = _in_col_chunks()
    w_in_p = np.empty((L, N_W_CHUNKS, 128, 8, 512), np.float32)
    for c, cols in enumerate(chunks):
        w = w_in[:, :, cols]
        w_in_p[:, c] = w.reshape(L, 8, 128, 512).transpose(0, 2, 1, 3)
    w_dt = np.ascontiguousarray(w_in[:, :, 5120:5136].reshape(L, 8, 128, 16).transpose(0, 2, 1, 3))
    rows = np.concatenate([_attn_perm(0), _attn_perm(1), np.arange(1024, 2048)])
    w_out_p = np.ascontiguousarray(w_out[:, rows, :].reshape(L, 4, 4, 128, 1024).transpose(0, 1, 3, 2, 4))
    conv_w = np.asarray(inp["conv_w"], np.float32)[:L]
    conv_wT = np.ascontiguousarray(conv_w.reshape(L, 4, 12, 128).transpose(0, 3, 2, 1))
    conv_bT = np.ascontiguousarray(np.asarray(inp["conv_b"], np.float32)[:L].reshape(L, 12, 128).transpose(0, 2, 1))

    def rep(a):
        a = np.asarray(a, np.float32)[:L]
        return np.ascontiguousarray(np.broadcast_to(a[:, None, :], (L, 128, a.shape[1])))
    small = np.concatenate([rep(inp["dt_bias"]), rep(inp["a_log"]), rep(inp["d_skip"]), rep(inp["sinks"])], axis=2)
    masks, cb16, cf32 = _consts()
    return {
        "w_in_p": w_in_p, "w_dt": w_dt, "w_out_p": w_out_p, "conv_wT": conv_wT, "conv_bT": conv_bT,
        "small": np.ascontiguousarray(small), "ln_g": rep(inp["ln_g"]), "ln_b": rep(inp["ln_b"]),
        "normw": rep(inp["ssm_norm_w"]), "masks": masks, "cb16": cb16, "cf32": cf32,
    }


def build_program(n_layers=DEPTH, stop=99):
    L = n_layers
    nc = bass.Bass("TRN2", target_bir_lowering=False)
    x_d = nc.dram_tensor("x", [SEQ, D_MODEL], F32, kind="ExternalInput")
    w_in_d = nc.dram_tensor("w_in_p", [L, N_W_CHUNKS, 128, 8 * 512], F32, kind="ExternalInput")
    w_dt_d = nc.dram_tensor("w_dt", [L, 128, 8 * 16], F32, kind="ExternalInput")
    w_out_d = nc.dram_tensor("w_out_p", [L, 4, 128, 4 * 1024], F32, kind="ExternalInput")
    convw_d = nc.dram_tensor("conv_wT", [L, 128, 48], F32, kind="ExternalInput")
    convb_d = nc.dram_tensor("conv_bT", [L, 128, 12], F32, kind="ExternalInput")
    small_d = nc.dram_tensor("small", [L, 128, 64], F32, kind="ExternalInput")
    lng_d = nc.dram_tensor("ln_g", [L, 128, 1024], F32, kind="ExternalInput")
    lnb_d = nc.dram_tensor("ln_b", [L, 128, 1024], F32, kind="ExternalInput")
    normw_d = nc.dram_tensor("normw", [L, 128, 1024], F32, kind="ExternalInput")
    masks_d = nc.dram_tensor("masks", [128, 4096], BF16, kind="ExternalInput")
    cb16_d = nc.dram_tensor("cb16", [128, 512], BF16, kind="ExternalInput")
    cf32_d = nc.dram_tensor("cf32", [128, 256], F32, kind="ExternalInput")
    y_d = nc.dram_tensor("y", [SEQ, D_MODEL], F32, kind="ExternalOutput")

    with ExitStack() as st:
        def sb(name, cols, dt):
            return st.enter_context(nc.sbuf_tensor("sb_" + name, [128, cols], dt))

        def ps(name, cols, dt):
            return st.enter_context(nc.psum_tensor("ps_" + name, [128, cols], dt))

        S = Sched(nc, st)
        acc = sb("acc", NT * 1024, F32)
        xT = sb("xT", 8 * SEQ, BF16)
        cb16 = sb("cb16", 512, BF16)
        cf32 = sb("cf32", 256, F32)
        identb = cb16[:, 0:128]
        trileb = cb16[:, 128:256]
        tristb = cb16[:, 256:384]
        onesb = cb16[:, 384:512]
        trilef = cf32[:, 0:128]
        onesf = cf32[:, 128:256]
        wbuf = [sb("wbuf%d" % i, 8 * 512, BF16) for i in range(2)]
        wob = sb("wob", 4 * 1024, BF16)
        wdt = sb("wdt", 128, BF16)
        lng = sb("lng", 1024, F32)
        lnb = sb("lnb", 1024, F32)
        normw = sb("normw", 1024, F32)
        convw = sb("convw", 48, F32)
        convb = sb("convb", 12, F32)
        small = sb("small", 64, F32)
        mix4 = sb("mix4", 4 * 512, BF16)
        xb16 = sb("xb16", 1024, BF16)
        stats = sb("stats", 12, F32)
        mv = sb("mv", 2, F32)
        rstd = sb("rstd", 1, F32)
        PH_BYTES = 66 * 1024
        ph8 = sb("phase", PH_BYTES // 2, BF16)
        ph32 = ph8.bitcast(F32)

        class Carver:
            def __init__(self):
                self.off = 0

            def bf(self, n):
                o = self.off
                self.off += 2 * n
                assert self.off <= PH_BYTES, self.off
                return ph8[:, o // 2: o // 2 + n], o // 2

            def f32(self, n):
                self.off = (self.off + 3) // 4 * 4
                o = self.off
                self.off += 4 * n
                assert self.off <= PH_BYTES, self.off
                return ph32[:, o // 4: o // 4 + n], o // 4

        pj = [ps("pj%d" % i, 512, F32) for i in range(2)]
        sc = [ps("sc%d" % i, 512, F32) for i in range(2)]
        nd = [ps("nd%d" % i, 512, F32) for i in range(2)]
        st0 = ps("st0", 512, F32)
        trT = ps("tr", 1024, BF16)
        tr = [trT[:, 0:512], trT[:, 512:1024]]
        pj_rr = [0]

        def next_pj():
            i = pj_rr[0]
            pj_rr[0] = 1 - i
            return pj[i], "pj%d" % i

        S.dma("sp", lambda e: e.dma_start(out=cb16[:], in_=cb16_d[:]), writes=["cb16"])
        S.dma("sp", lambda e: e.dma_start(out=cf32[:], in_=cf32_d[:]), writes=["cf32"])

        xT3 = lambda kc, c0, n: xT[:, kc * SEQ + c0: kc * SEQ + c0 + n]

        def load_w_chunk(l, c, buf_i):
            S.dma("pool", lambda e: e.dma_start(out=wbuf[buf_i][:], in_=w_in_d[l, c]), writes=["wbuf%d" % buf_i])

        def load_wo(l, qi):
            S.dma("pool", lambda e: e.dma_start(out=wob[:], in_=w_out_d[l, qi]), writes=["wob"])

        def mm_group(items, reads, writes):
            n = len(items)
            for i, (o, a, b) in enumerate(items):
                S.op("pe", lambda e, o=o, a=a, b=b, i=i: e.matmul(o, lhsT=a, rhs=b, start=(i == 0), stop=(i == n - 1)),
                     reads=reads, writes=writes, inc=(i == n - 1))

        def inproj_fm(wb_i, col0, tg, bank, bank_name):
            items = [(bank[:, 0:512], wbuf[wb_i][:, kc * 512 + col0: kc * 512 + col0 + 128], xT3(kc, tg * 512, 512))
                     for kc in range(8)]
            mm_group(items, reads=["wbuf%d" % wb_i] + ["xT%d" % t for t in range(4 * tg, 4 * tg + 4)], writes=[bank_name])

        def inproj_tm(wb_i, col0, ncols, t, bank, bank_name):
            items = [(bank[:, 0:ncols], xT3(kc, t * 128, 128), wbuf[wb_i][:, kc * 512 + col0: kc * 512 + col0 + ncols])
                     for kc in range(8)]
            mm_group(items, reads=["wbuf%d" % wb_i, "xT%d" % t], writes=[bank_name])

        def to_xT(t):
            S.op("act", lambda e: e.activation(out=xb16[:], in_=acc[:, t * 1024:(t + 1) * 1024], func=AF.Copy),
                 reads=["acc%d" % t], writes=["xb16"])
            for half in range(2):
                for j in range(4):
                    kc = half * 4 + j
                    S.op("pe", lambda e, kc=kc, j=j, half=half: e.transpose(tr[half][:, j * 128:(j + 1) * 128],
                                                                           xb16[:, kc * 128:(kc + 1) * 128], identb),
                         reads=["xb16", "cb16"], writes=["tr"], inc=(j == 3))
                o = AP(xT, half * 4 * SEQ + t * 128, [[8 * SEQ, 128], [SEQ, 4], [1, 128]])
                i_ = AP(trT, half * 512, [[1024, 128], [128, 4], [1, 128]])
                S.op("dve", lambda e, o=o, i_=i_: e.tensor_copy(out=o, in_=i_), reads=["tr"], writes=["xT%d" % t])

        def outproj_partial(l, first, tg):
            for b in range(4):
                t = 4 * tg + b
                for ch in range(2):
                    bank, bn = next_pj()
                    items = [(bank[:, 0:512], mix4[:, c * 512 + b * 128: c * 512 + (b + 1) * 128],
                              wob[:, c * 1024 + ch * 512: c * 1024 + (ch + 1) * 512]) for c in range(4)]
                    mm_group(items, reads=["mix4", "wob"], writes=[bn])
                    a_ = acc[:, t * 1024 + ch * 512: t * 1024 + (ch + 1) * 512]
                    if first:
                        S.op("dve", lambda e, a_=a_, bank=bank: e.scalar_tensor_tensor(
                            out=a_, in0=a_, scalar=float(ALPHA), in1=bank[:, 0:512], op0=ALU.mult, op1=ALU.add),
                            reads=[bn, "acc%d" % t], writes=["acc%d" % t])
                    else:
                        S.op("dve", lambda e, a_=a_, bank=bank: e.tensor_tensor(out=a_, in0=a_, in1=bank[:, 0:512], op=ALU.add),
                             reads=[bn, "acc%d" % t], writes=["acc%d" % t])

        def layernorm_tile(l, t, last):
            a_ = acc[:, t * 1024:(t + 1) * 1024]
            rw = ["acc%d" % t]
            for h in range(2):
                S.op("dve", lambda e, h=h: e.bn_stats(out=stats[:, h * 6:(h + 1) * 6], in_=acc[:, t * 1024 + h * 512: t * 1024 + (h + 1) * 512]),
                     reads=rw, writes=["stats"])
            S.op("dve", lambda e: e.bn_aggr(out=mv[:], in_=stats[:]), reads=["stats"], writes=["mv"])
            S.op("act", lambda e: e.activation(out=rstd[:], in_=mv[:, 1:2], func=AF.Ln, bias=float(LN_EPS), scale=1.0),
                 reads=["mv"], writes=["rstd"])
            S.op("act", lambda e: e.activation(out=rstd[:], in_=rstd[:], func=AF.Exp, scale=-0.5), reads=["rstd"], writes=["rstd"])
            S.op("dve", lambda e: e.tensor_scalar(out=a_, in0=a_, scalar1=mv[:, 0:1], scalar2=rstd[:, 0:1],
                                                  op0=ALU.subtract, op1=ALU.mult), reads=rw + ["mv", "rstd"], writes=rw)
            S.op("pool", lambda e: e.tensor_tensor(out=a_, in0=a_, in1=lng[:], op=ALU.mult), reads=rw + ["lng"], writes=rw)
            S.op("pool", lambda e: e.tensor_tensor(out=a_, in0=a_, in1=lnb[:], op=ALU.add), reads=rw + ["lnb"], writes=rw)
            if last:
                S.all_dma_toks.append(S.dma("sp", lambda e: e.dma_start(out=y_d[t * 128:(t + 1) * 128, :], in_=a_), reads=rw))
            else:
                to_xT(t)

        def finalize():
            for t in range(NT):
                S.all_dma_toks.append(S.dma("sp", lambda e, t=t: e.dma_start(out=y_d[t * 128:(t + 1) * 128, :], in_=acc[:, t * 1024:(t + 1) * 1024]),
                                            reads=["acc%d" % t]))
            S.barrier()
            for tok in S.all_dma_toks:
                S.wait_tok("sp", tok)
            print("instructions emitted (stopped):", S.n_ins)
            return nc

        for t in range(NT):
            S.dma("sp", lambda e, t=t: e.dma_start(out=acc[:, t * 1024:(t + 1) * 1024], in_=x_d[t * 128:(t + 1) * 128, :]),
                  writes=["acc%d" % t])
        for t in range(NT):
            to_xT(t)

        if stop == 0:
            return finalize()
        for l in range(L):
            last = (l == L - 1)
            S.dma("sp", lambda e: e.dma_start(out=lng[:], in_=lng_d[l]), writes=["lng"])
            S.dma("sp", lambda e: e.dma_start(out=lnb[:], in_=lnb_d[l]), writes=["lnb"])
            S.dma("sp", lambda e: e.dma_start(out=normw[:], in_=normw_d[l]), writes=["normw"])
            S.dma("sp", lambda e: e.dma_start(out=convw[:], in_=convw_d[l]), writes=["convw"])
            S.dma("sp", lambda e: e.dma_start(out=convb[:], in_=convb_d[l]), writes=["convb"])
            S.dma("sp", lambda e: e.dma_start(out=small[:], in_=small_d[l]), writes=["small"])
            S.dma("pool", lambda e: e.dma_start(out=wdt[:], in_=w_dt_d[l]), writes=["wdt"])
            dtb = small[:, 0:16]
            alog = small[:, 16:32]
            dsk = small[:, 32:48]
            snk = small[:, 48:64]

            cv = Carver()
            masks, _ = cv.bf(4096)
            kT, kT_o = cv.bf(2 * SEQ)
            vt, vt_o = cv.bf(NT * 256)
            q4, q4_o = cv.bf(4 * 512)
            z4, z4_o = cv.bf(4 * 512)
            Eb = [cv.bf(512) for _ in range(8)]
            sinkb, _ = cv.bf(2 * 512)
            sinkexp, se_o = cv.f32(2 * 512)
            dS, dS_o = cv.f32(512)
            wgt, wgt_o = cv.f32(512)
            es, es_o = cv.f32(16)
            S.barrier()
            S.dma("sp", lambda e: e.dma_start(out=masks, in_=masks_d[:]), reads=[], writes=["masks"])
            load_w_chunk(l, 0, 0)
            load_w_chunk(l, 1, 1)
            S.op("act", lambda e: e.activation(out=es, in_=snk, func=AF.Exp), reads=["small"], writes=["es"])
            for kp in range(2):
                for r in range(2):
                    o = AP(ph32, se_o + kp * 512 + r * 64 * (PH_BYTES // 4), [[PH_BYTES // 4, 64], [128, 4], [1, 128]])
                    hh = (2 * kp + r) * 4
                    i_ = AP(ph32, es_o + hh + r * 64 * (PH_BYTES // 4), [[PH_BYTES // 4, 64], [1, 4], [0, 128]])
                    S.op("dve", lambda e, o=o, i_=i_: e.tensor_copy(out=o, in_=i_), reads=["es"], writes=["sinkexp"])
            S.op("dve", lambda e: e.tensor_copy(out=sinkb, in_=sinkexp), reads=["sinkexp"], writes=["sinkb"])
            for tg in range(NG):
                for c in range(2):
                    bank, bn = next_pj()
                    inproj_fm(0, c * 128, tg, bank, bn)
                    S.op("dve", lambda e, c=c, tg=tg, bank=bank: e.tensor_copy(
                        out=kT[:, c * SEQ + tg * 512: c * SEQ + (tg + 1) * 512], in_=bank[:, 0:512]),
                        reads=[bn], writes=["kT%d" % tg])
            for t in range(NT):
                bank, bn = next_pj()
                inproj_tm(0, 256, 256, t, bank, bn)
                S.op("act", lambda e, t=t, bank=bank: e.activation(out=vt[:, t * 256:(t + 1) * 256], in_=bank[:, 0:256], func=AF.Copy),
                     reads=[bn], writes=["v%d" % t])
            if stop == 1:
                return finalize()
            trF = trT.bitcast(F32)
            sbanks = [(sc[0], "sc0"), (sc[1], "sc1"), (st0, "st0"), (trF, "tr")]

            def att_S(kp, tg, b):
                n = 4 * tg + b
                par = b % 2
                k_ = 0
                for r in range(2):
                    kv = 2 * kp + r
                    rows = slice(r * 64, r * 64 + 64)
                    blocks = ([n - 1] if n > 0 else []) + [n]
                    for sblk in blocks:
                        pc = 0 if sblk == n else 1
                        E, _eo = Eb[par * 4 + k_]
                        en = "E%d" % (par * 4 + k_)
                        scb, scn = sbanks[k_]
                        k_ += 1
                        lhsT = kT[rows, kp * SEQ + sblk * 128: kp * SEQ + (sblk + 1) * 128]
                        rhs = AP(ph8, q4_o + b * 128 + r * 64 * (PH_BYTES // 2), [[PH_BYTES // 2, 64], [512, 4], [1, 128]])
                        S.op("pe", lambda e, scb=scb, lhsT=lhsT, rhs=rhs: e.matmul(scb[:, 0:512], lhsT=lhsT, rhs=rhs, start=True, stop=True),
                             reads=["kT%d" % (sblk // 4), "q4"], writes=[scn])
                        S.op("act", lambda e, E=E, scb=scb: e.activation(out=E, in_=scb[:, 0:512], func=AF.Exp, scale=0.125),
                             reads=[scn], writes=[en])
                        m_ = masks[:, (kv * 2 + pc) * 512:(kv * 2 + pc + 1) * 512]
                        S.op("dve", lambda e, E=E, m_=m_: e.tensor_tensor(out=E, in0=E, in1=m_, op=ALU.mult),
                             reads=[en, "masks"], writes=[en])

            def att_V(kp, tg, b):
                n = 4 * tg + b
                par = b % 2
                k_ = 0
                for r in range(2):
                    kv = 2 * kp + r
                    rows = slice(r * 64, r * 64 + 64)
                    blocks = ([n - 1] if n > 0 else []) + [n]
                    for bi, sblk in enumerate(blocks):
                        E, _eo = Eb[par * 4 + k_]
                        en = "E%d" % (par * 4 + k_)
                        k_ += 1
                        first_b = (bi == 0)
                        last_b = (bi == len(blocks) - 1)
                        vl = vt[:, sblk * 256 + kv * 64: sblk * 256 + (kv + 1) * 64]
                        S.op("pe", lambda e, vl=vl, E=E, rows=rows, first_b=first_b, last_b=last_b: e.matmul(
                            nd[0][rows, 0:512], lhsT=vl, rhs=E, start=first_b, stop=last_b),
                            reads=[en, "v%d" % sblk], writes=["nd0"], inc=False)
                        S.op("pe", lambda e, E=E, rows=rows, first_b=first_b, last_b=last_b: e.matmul(
                            nd[1][rows, 0:512], lhsT=onesb[:, 0:64], rhs=E, start=first_b, stop=last_b),
                            reads=[en, "cb16"], writes=["nd1"], inc=True)
                S.op("dve", lambda e: e.tensor_tensor(out=dS, in0=nd[1][:, 0:512], in1=sinkexp[:, kp * 512:(kp + 1) * 512], op=ALU.add),
                     reads=["nd1", "sinkexp"], writes=["dS"])
                S.op("act", lambda e: e.activation(out=dS, in_=dS, func=AF.Ln), reads=["dS"], writes=["dS"])
                S.op("act", lambda e: e.activation(out=dS, in_=dS, func=AF.Exp, scale=-1.0), reads=["dS"], writes=["dS"])
                S.op("dve", lambda e: e.tensor_tensor(out=wgt, in0=nd[0][:, 0:512], in1=dS, op=ALU.mult),
                     reads=["nd0", "dS"], writes=["wgt"])
                zv = AP(ph8, z4_o + b * 128, [[PH_BYTES // 2, 128], [512, 4], [1, 128]])
                w3 = AP(ph32, wgt_o, [[PH_BYTES // 4, 128], [128, 4], [1, 128]])
                mo = AP(mix4, b * 128, [[2048, 128], [512, 4], [1, 128]])
                S.op("pool", lambda e, zv=zv, w3=w3, mo=mo: e.tensor_tensor(out=mo, in0=w3, in1=zv, op=ALU.mult),
                     reads=["wgt", "z4"], writes=["mix4"])

            for kp in range(2):
                wq = 1
                load_w_chunk(l, 2 + 2 * kp, 0)
                load_wo(l, kp)
                for tg in range(NG):
                    for j in range(4):
                        bank, bn = next_pj()
                        inproj_fm(wq, j * 128, tg, bank, bn)
                        S.op("dve", lambda e, j=j, bank=bank: e.tensor_copy(out=q4[:, j * 512:(j + 1) * 512], in_=bank[:, 0:512]),
                             reads=[bn], writes=["q4"])
                    for j in range(4):
                        bank, bn = next_pj()
                        inproj_fm(0, j * 128, tg, bank, bn)
                        S.op("act", lambda e, j=j, bank=bank: e.activation(out=z4[:, j * 512:(j + 1) * 512], in_=bank[:, 0:512], func=AF.Silu),
                             reads=[bn], writes=["z4"])
                    att_S(kp, tg, 0)
                    for b in range(4):
                        if b < 3:
                            att_S(kp, tg, b + 1)
                        att_V(kp, tg, b)
                    if tg == NG - 1 and kp == 0:
                        load_w_chunk(l, 3, 1)
                    outproj_partial(l, kp == 0, tg)
            if stop == 2:
                return finalize()
            cv = Carver()
            bcT, bc_o = cv.bf(4 * SEQ)
            ubuf, u_o = cv.bf(4 * 515)
            xsT, xsT_o = cv.bf(4 * 512)
            diag, diag_o = cv.bf(16 * 128)
            xdt = [cv.bf(512)[0] for _ in range(2)]
            xdtd = [cv.bf(512)[0] for _ in range(2)]
            xsD = [cv.bf(512)[0] for _ in range(2)]
            zs4, _ = cv.bf(4 * 512)
            Btm = [cv.bf(128)[0] for _ in range(2)]
            cbm, cbm_o = cv.bf(128)
            rhsD, rhsD_o = cv.bf(1024)
            _e8 = [cv.bf(1024) for _ in range(2)]
            E8 = [x_[0] for x_ in _e8]
            E8_o = [x_[1] for x_ in _e8]
            Sbf, _ = cv.bf(512)
            gn, _ = cv.bf(512)
            dt16, dt_o = cv.f32(256)
            dtA, dtA_o = cv.f32(256)
            ea, ea_o = cv.f32(256)
            dtdte, dd_o = cv.f32(256)
            cd, cd_o = cv.f32(256)
            tmpA, _ = cv.f32(256)
            tmpB, _ = cv.f32(256)
            nega, nega_o2 = cv.f32(16)
            S32, _ = cv.f32(512)
            t1, t1_o = cv.f32(512)
            junk = ph8[:, 2 * t1_o: 2 * t1_o + 512]
            yb, _ = cv.f32(512)
            ss, _ = cv.f32(1)
            rs, _ = cv.f32(1)
            PB = PH_BYTES
            S.barrier()
            S.op("pool", lambda e: e.memset(ubuf, 0.0), reads=[], writes=["ubuf"])
            load_w_chunk(l, 5, 1)
            for t in range(NT):
                items = [(st0[:, t * 16:(t + 1) * 16], xT3(kc, t * 128, 128), wdt[:, kc * 16:(kc + 1) * 16]) for kc in range(8)]
                mm_group(items, reads=["wdt", "xT%d" % t], writes=["st0"])
            dtb_b = AP(small, 0, [[64, 128], [0, 16], [1, 16]])
            v3 = lambda o_: AP(ph32, o_, [[PB // 4, 128], [16, 16], [1, 16]])
            S.op("dve", lambda e: e.tensor_tensor(out=v3(dt_o), in0=AP(st0, 0, [[512, 128], [16, 16], [1, 16]]), in1=dtb_b, op=ALU.add),
                 reads=["st0", "small"], writes=["dt16"])
            S.op("dve", lambda e: e.tensor_scalar(out=tmpA, in0=dt16, scalar1=-1.0, scalar2=None, op0=ALU.mult), reads=["dt16"], writes=["tmpA"])
            S.op("dve", lambda e: e.tensor_tensor(out=tmpA, in0=tmpA, in1=dt16, op=ALU.max), reads=["dt16", "tmpA"], writes=["tmpA"])
            S.op("act", lambda e: e.activation(out=tmpA, in_=tmpA, func=AF.Exp, scale=-1.0), reads=["tmpA"], writes=["tmpA"])
            S.op("act", lambda e: e.activation(out=tmpA, in_=tmpA, func=AF.Ln, bias=1.0), reads=["tmpA"], writes=["tmpA"])
            S.op("dve", lambda e: e.scalar_tensor_tensor(out=dt16, in0=dt16, scalar=0.0, in1=tmpA, op0=ALU.max, op1=ALU.add),
                 reads=["dt16", "tmpA"], writes=["dt16"])
            S.op("act", lambda e: e.activation(out=nega, in_=alog, func=AF.Exp), reads=["small"], writes=["nega"])
            S.op("dve", lambda e: e.tensor_scalar(out=nega, in0=nega, scalar1=-1.0, scalar2=None, op0=ALU.mult), reads=["nega"], writes=["nega"])
            nega_o = nega_o2
            S.op("dve", lambda e: e.tensor_tensor(out=v3(dtA_o), in0=v3(dt_o), in1=AP(ph32, nega_o, [[PB // 4, 128], [0, 16], [1, 16]]), op=ALU.mult),
                 reads=["dt16", "nega"], writes=["dtA"])
            S.op("pe", lambda e: e.matmul(st0[:, 0:256], lhsT=trilef, rhs=dtA, start=True, stop=True), reads=["cf32", "dtA", "dt16"], writes=["st0"])
            S.op("pe", lambda e: e.matmul(st0[:, 256:512], lhsT=onesf, rhs=dtA, start=True, stop=True), reads=["cf32", "dtA"], writes=["st0"])
            S.op("act", lambda e: e.activation(out=ea, in_=st0[:, 0:256], func=AF.Exp), reads=["st0"], writes=["ea"])
            S.op("act", lambda e: e.activation(out=cd, in_=st0[:, 256:512], func=AF.Exp), reads=["st0"], writes=["cd"])
            S.op("act", lambda e: e.activation(out=tmpB, in_=st0[:, 0:256], func=AF.Identity), reads=["st0"], writes=["tmpB"])
            S.op("dve", lambda e: e.tensor_tensor(out=tmpB, in0=st0[:, 256:512], in1=tmpB, op=ALU.subtract), reads=["st0", "tmpB"], writes=["tmpB"])
            S.op("act", lambda e: e.activation(out=tmpB, in_=tmpB, func=AF.Exp), reads=["tmpB"], writes=["tmpB"])
            S.op("dve", lambda e: e.tensor_tensor(out=dtdte, in0=dt16, in1=tmpB, op=ALU.mult), reads=["dt16", "tmpB"], writes=["dtdte"])

            if stop == 3:
                return finalize()

            def build_diag(cc0):
                for j4 in range(4):
                    for tap in range(4):
                        o = diag[:, (j4 * 4 + tap) * 128:(j4 * 4 + tap + 1) * 128]
                        s_ = convw[:, (cc0 + j4) * 4 + tap:(cc0 + j4) * 4 + tap + 1]
                        S.op("pool", lambda e, o=o, s_=s_: e.tensor_scalar(out=o, in0=identb, scalar1=s_, scalar2=None, op0=ALU.mult),
                             reads=["cb16", "convw"], writes=["diag"])

            def conv_group(wb_i, cc0, tg, dst, dst_stride, dst_res):
                if tg > 0:
                    S.op("dve", lambda e: e.tensor_copy(out=AP(ph8, u_o, [[PB // 2, 128], [515, 4], [1, 3]]),
                                                        in_=AP(ph8, u_o + 512, [[PB // 2, 128], [515, 4], [1, 3]])),
                         reads=["ubuf"], writes=["ubuf"])
                else:
                    S.op("pool", lambda e: e.memset(AP(ph8, u_o, [[PB // 2, 128], [515, 4], [1, 3]]), 0.0), reads=[], writes=["ubuf"])
                for j in range(4):
                    bank, bn = next_pj()
                    inproj_fm(wb_i, j * 128, tg, bank, bn)
                    S.op("act", lambda e, j=j, bank=bank: e.activation(out=ubuf[:, j * 515 + 3: j * 515 + 515], in_=bank[:, 0:512], func=AF.Copy),
                         reads=[bn], writes=["ubuf"])
                for j in range(4):
                    bank, bn = next_pj()
                    items = [(bank[:, 0:512], diag[:, (j * 4 + tap) * 128:(j * 4 + tap + 1) * 128],
                              ubuf[:, j * 515 + tap: j * 515 + tap + 512]) for tap in range(4)]
                    mm_group(items, reads=["diag", "ubuf"], writes=[bn])
                    o = dst(j)
                    S.op("act", lambda e, o=o, bank=bank, j=j: e.activation(out=o, in_=bank[:, 0:512], func=AF.Silu,
                                                                             bias=convb[:, cc0 + j: cc0 + j + 1]),
                         reads=[bn, "convb"], writes=[dst_res])

            build_diag(8)
            for tg in range(NG):
                conv_group(1, 8, tg, lambda j, tg=tg: bcT[:, j * SEQ + tg * 512: j * SEQ + (tg + 1) * 512], None, "bcT%d" % tg)
            if stop == 4:
                return finalize()
            def ssd_I(g, tg, b):
                c = 4 * tg + b
                t = c
                p_ = b % 2
                for j in range(4):
                    S.op("pe", lambda e, j=j: e.transpose(tr[0][:, j * 128:(j + 1) * 128], xsT[:, j * 512 + b * 128: j * 512 + (b + 1) * 128], identb),
                         reads=["xsT", "cb16"], writes=["tr"], inc=(j == 3))
                tr3 = AP(trT, 0, [[1024, 128], [64, 8], [1, 64]])
                for dst_, nm_, sc_o in ((xdt[p_], "xdt%d" % p_, dt_o), (xdtd[p_], "xdtd%d" % p_, dd_o)):
                    S.op("dve", lambda e, dst_=dst_, sc_o=sc_o: e.tensor_tensor(
                        out=dst_.rearrange("p (h d) -> p h d", h=8), in0=tr3, in1=hb(g, sc_o, t), op=ALU.mult),
                        reads=["tr", "dt16", "dtdte"], writes=[nm_])
                dsk_b = AP(small, 32 + 8 * g, [[64, 128], [1, 8], [0, 64]])
                S.op("dve", lambda e: e.tensor_tensor(out=xsD[p_].rearrange("p (h d) -> p h d", h=8), in0=tr3, in1=dsk_b, op=ALU.mult),
                     reads=["tr", "small"], writes=["xsD%d" % p_])
                S.op("pe", lambda e: e.transpose(tr[1][:, 0:128], bcT[:, g * SEQ + c * 128: g * SEQ + (c + 1) * 128], identb),
                     reads=["bcT%d" % tg, "cb16"], writes=["tr"])
                S.op("act", lambda e: e.activation(out=Btm[p_], in_=tr[1][:, 0:128], func=AF.Copy), reads=["tr"], writes=["Btm%d" % p_])
                bank, bn = next_pj()
                S.op("pe", lambda e, bank=bank: e.matmul(bank[:, 0:128], lhsT=bcT[:, g * SEQ + c * 128: g * SEQ + (c + 1) * 128],
                                                         rhs=bcT[:, (2 + g) * SEQ + c * 128: (2 + g) * SEQ + (c + 1) * 128], start=True, stop=True),
                     reads=["bcT%d" % tg], writes=[bn])
                S.op("dve", lambda e, bank=bank: e.tensor_tensor(out=cbm, in0=bank[:, 0:128], in1=trileb, op=ALU.mult),
                     reads=[bn, "cb16"], writes=["cbm"])
                S.op("pool", lambda e: e.tensor_tensor(out=AP(ph8, rhsD_o, [[PB // 2, 128], [128, 8], [1, 128]]),
                                                       in0=AP(cf32, 0, [[256, 128], [0, 8], [1, 128]]),
                                                       in1=AP(ph32, dtA_o + t * 16 + 8 * g, [[PB // 4, 128], [1, 8], [0, 128]]), op=ALU.mult),
                     reads=["cf32", "dtA"], writes=["rhsD"])
                eo = E8_o[p_]
                for hf in range(2):
                    S.op("pe", lambda e, hf=hf: e.matmul(sc[hf][:, 0:512], lhsT=tristb, rhs=rhsD[:, hf * 512:(hf + 1) * 512], start=True, stop=True),
                         reads=["rhsD", "cb16"], writes=["sc%d" % hf])
                    S.op("act", lambda e, hf=hf: e.activation(out=E8[p_][:, hf * 512:(hf + 1) * 512], in_=sc[hf][:, 0:512], func=AF.Exp),
                         reads=["sc%d" % hf], writes=["E8%d" % p_])
                S.op("dve", lambda e: e.tensor_tensor(out=AP(ph8, eo, [[PB // 2, 128], [128, 8], [1, 128]]),
                                                      in0=AP(ph8, eo, [[PB // 2, 128], [128, 8], [1, 128]]),
                                                      in1=AP(ph8, cbm_o, [[PB // 2, 128], [0, 8], [1, 128]]), op=ALU.mult),
                     reads=["E8%d" % p_, "cbm"], writes=["E8%d" % p_])

            def ssd_D(g, tg, b):
                c = 4 * tg + b
                t = c
                p_ = b % 2
                G_ = E8[p_]
                S.op("pe", lambda e: e.matmul(nd[0][:, 0:512], lhsT=identb, rhs=xsD[p_], start=True, stop=False),
                     reads=["xsD%d" % p_, "cb16"], writes=["nd0"], inc=False)
                for h in range(8):
                    S.op("pe", lambda e, h=h: e.matmul(nd[0][:, h * 64:(h + 1) * 64], lhsT=G_[:, h * 128:(h + 1) * 128],
                                                       rhs=xdt[p_][:, h * 64:(h + 1) * 64], start=False, stop=(h == 7)),
                         reads=["E8%d" % p_, "xdt%d" % p_], writes=["nd0"], inc=(h == 7))
                S.op("pe", lambda e: e.matmul(nd[1][:, 0:512], lhsT=bcT[:, (2 + g) * SEQ + c * 128: (2 + g) * SEQ + (c + 1) * 128],
                                              rhs=Sbf, start=True, stop=True), reads=["bcT%d" % tg, "Sbf"], writes=["nd1"])
                if c < NT - 1:
                    S.op("pe", lambda e: e.matmul(st0[:, 0:512], lhsT=Btm[p_], rhs=xdtd[p_], start=True, stop=True),
                         reads=["Btm%d" % p_, "xdtd%d" % p_], writes=["st0"])
                    S.op("dve", lambda e: e.tensor_tensor(out=S32.rearrange("p (h d) -> p h d", h=8), in0=S32.rearrange("p (h d) -> p h d", h=8),
                                                          in1=hb(g, cd_o, t), op=ALU.mult), reads=["S32", "cd"], writes=["S32"])
                    S.op("dve", lambda e: e.tensor_tensor(out=Sbf, in0=S32, in1=st0[:, 0:512], op=ALU.add), reads=["S32", "st0"], writes=["Sbf"])
                    S.op("dve", lambda e: e.tensor_tensor(out=S32, in0=S32, in1=st0[:, 0:512], op=ALU.add), reads=["S32", "st0"], writes=["S32"])
                S.op("dve", lambda e: e.tensor_tensor(out=t1.rearrange("p (h d) -> p h d", h=8), in0=AP(nd[1], 0, [[512, 128], [64, 8], [1, 64]]),
                                                      in1=hb(g, ea_o, t), op=ALU.mult), reads=["nd1", "ea"], writes=["t1"])
                S.op("dve", lambda e: e.tensor_tensor(out=yb, in0=nd[0][:, 0:512], in1=t1, op=ALU.add), reads=["nd0", "t1"], writes=["yb"])
                S.op("dve", lambda e: e.tensor_tensor(out=yb, in0=yb, in1=zs4[:, b * 512:(b + 1) * 512], op=ALU.mult), reads=["yb", "zs4_%d" % b], writes=["yb"])
                S.op("act", lambda e: e.activation(out=junk, in_=yb, func=AF.Square, accum_out=ss), reads=["yb", "t1"], writes=["t1", "ss"])
                S.op("act", lambda e: e.activation(out=rs, in_=ss, func=AF.Ln, bias=float(RMS_EPS), scale=1.0 / 512.0), reads=["ss"], writes=["rs"])
                S.op("act", lambda e: e.activation(out=rs, in_=rs, func=AF.Exp, scale=-0.5), reads=["rs"], writes=["rs"])

            def ssd_D2(g, tg, b):
                S.op("dve", lambda e: e.scalar_tensor_tensor(out=gn, in0=yb, scalar=rs[:, 0:1], in1=normw[:, g * 512:(g + 1) * 512],
                                                             op0=ALU.mult, op1=ALU.mult), reads=["yb", "rs", "normw"], writes=["gn"])
                for j in range(4):
                    S.op("pe", lambda e, j=j: e.transpose(tr[1][:, j * 128:(j + 1) * 128], gn[:, j * 128:(j + 1) * 128], identb),
                         reads=["gn", "cb16"], writes=["tr"], inc=(j == 3))
                S.op("act", lambda e: e.activation(out=AP(mix4, b * 128, [[2048, 128], [512, 4], [1, 128]]),
                                                   in_=AP(trT, 512, [[1024, 128], [128, 4], [1, 128]]), func=AF.Copy),
                     reads=["tr"], writes=["mix4"])

            hb = lambda g, o_, t: AP(ph32, o_ + t * 16 + 8 * g, [[PB // 4, 128], [1, 8], [0, 64]])
            UB = lambda c0, n_: AP(ph8, u_o + c0, [[PB // 2, 128], [515, 4], [1, n_]])

            def xs_halo(tg):
                allu = ["ubuf%d" % j for j in range(4)] + ["ubuf"]
                if tg > 0:
                    S.op("dve", lambda e: e.tensor_copy(out=UB(0, 3), in_=UB(512, 3)), reads=allu, writes=allu)
                else:
                    S.op("pool", lambda e: e.memset(UB(0, 3), 0.0), reads=[], writes=allu)

            def xs_inproj_piece(g, tg, j):
                bank, bn = next_pj()
                inproj_fm(0, j * 128, tg, bank, bn)
                S.op("act", lambda e, bank=bank: e.activation(out=ubuf[:, j * 515 + 3: j * 515 + 515], in_=bank[:, 0:512], func=AF.Copy),
                     reads=[bn], writes=["ubuf%d" % j])

            def xs_conv_piece(g, tg, j):
                bank, bn = next_pj()
                items = [(bank[:, 0:512], diag[:, (j * 4 + tap) * 128:(j * 4 + tap + 1) * 128],
                          ubuf[:, j * 515 + tap: j * 515 + tap + 512]) for tap in range(4)]
                mm_group(items, reads=["diag", "ubuf%d" % j], writes=[bn])
                S.op("act", lambda e, bank=bank: e.activation(out=xsT[:, j * 512:(j + 1) * 512], in_=bank[:, 0:512], func=AF.Silu,
                                                              bias=convb[:, 4 * g + j: 4 * g + j + 1]),
                     reads=[bn, "convb"], writes=["xsT"])

            def z_piece(g, tg, b):
                bank, bn = next_pj()
                inproj_tm(1, 0, 512, 4 * tg + b, bank, bn)
                S.op("act", lambda e, bank=bank: e.activation(out=zs4[:, b * 512:(b + 1) * 512], in_=bank[:, 0:512], func=AF.Silu),
                     reads=[bn], writes=["zs4_%d" % b])

            for g in range(2):
                load_w_chunk(l, 6 + 2 * g, 0)
                load_w_chunk(l, 7 + 2 * g, 1)
                load_wo(l, 2 + g)
                build_diag(4 * g)
                S.op("pool", lambda e: e.memset(S32, 0.0), reads=[], writes=["S32"])
                S.op("pool", lambda e: e.memset(Sbf, 0.0), reads=[], writes=["Sbf"])
                xs_halo(0)
                for j in range(4):
                    xs_inproj_piece(g, 0, j)
                for j in range(4):
                    xs_conv_piece(g, 0, j)
                for b in range(4):
                    z_piece(g, 0, b)
                for tg in range(NG):
                    nxt = tg + 1 < NG
                    ssd_I(g, tg, 0)
                    ssd_I(g, tg, 1)
                    ssd_D(g, tg, 0)
                    if nxt:
                        xs_halo(tg + 1)
                    ssd_I(g, tg, 2)
                    ssd_D2(g, tg, 0)
                    ssd_D(g, tg, 1)
                    if nxt:
                        xs_inproj_piece(g, tg + 1, 0)
                        xs_inproj_piece(g, tg + 1, 1)
                    ssd_I(g, tg, 3)
                    ssd_D2(g, tg, 1)
                    ssd_D(g, tg, 2)
                    if nxt:
                        xs_inproj_piece(g, tg + 1, 2)
                        xs_inproj_piece(g, tg + 1, 3)
                        z_piece(g, tg + 1, 0)
                        z_piece(g, tg + 1, 1)
                    ssd_D2(g, tg, 2)
                    ssd_D(g, tg, 3)
                    if nxt:
                        for j in range(4):
                            xs_conv_piece(g, tg + 1, j)
                        z_piece(g, tg + 1, 2)
                    ssd_D2(g, tg, 3)
                    if nxt:
                        z_piece(g, tg + 1, 3)
                    outproj_partial(l, False, tg)
                    if g == 1:
                        for b in range(4):
                            layernorm_tile(l, 4 * tg + b, last)
        for tok in S.all_dma_toks:
            S.wait_tok("sp", tok)
        print("instructions emitted:", S.n_ins)
    return nc


_CACHE = {}


def run(inputs, n_layers=DEPTH):
    x = np.asarray(inputs["x"], np.float32)
    w = prep_weights(inputs, n_layers)
    if n_layers not in _CACHE:
        _CACHE[n_layers] = build_program(n_layers)
    nc = _CACHE[n_layers]
    in_maps = []
    for c in range(8):
        m = dict(w)
        m["x"] = np.ascontiguousarray(x[c])
        in_maps.append(m)
    res = run_bass_kernel_spmd(nc, in_maps, core_ids=list(range(8)))
    return np.stack([np.asarray(r["y"], np.float32) for r in res.results], axis=0)


def kernel(**inputs):
    return run(inputs, DEPTH)
```

```python
from contextlib import ExitStack
import numpy as np
import ml_dtypes
import concourse.bass as bass
import concourse.mybir as mybir
from concourse.bass_utils import run_bass_kernel_spmd

F32 = mybir.dt.float32
BF16 = mybir.dt.bfloat16
AF = mybir.ActivationFunctionType
ALU = mybir.AluOpType

D_MODEL = 1024
SEQ = 2048
DEPTH = 4
NT = 16
NG = 4
HEAD = 64
ALPHA = (2.0 * DEPTH) ** 0.25
LN_EPS = 1e-5
RMS_EPS = 1e-5
N_W_CHUNKS = 10


class Res:
    __slots__ = ("w", "r")

    def __init__(self):
        self.w = None
        self.r = {}


class Sched:
    ENGS = ("pe", "act", "dve", "pool", "sp")

    def __init__(self, nc, st, ndma=8):
        self.nc = nc
        self.eng = {"pe": nc.tensor, "act": nc.scalar, "dve": nc.vector, "pool": nc.gpsimd, "sp": nc.sync}
        self.cnt = {e: 0 for e in self.ENGS}
        self.waited = {e: {} for e in self.ENGS}
        self.sems = {}
        for e in self.ENGS:
            self.sems["s_" + e] = st.enter_context(nc.semaphore("s_" + e))
        self.ndma = ndma
        self.dma_names = {}
        self.dma_use = {}
        self.dma_rr = {}
        for q in ("sp", "act", "pool"):
            self.dma_names[q] = ["d_%s_%d" % (q, i) for i in range(ndma)]
            for n in self.dma_names[q]:
                self.sems[n] = st.enter_context(nc.semaphore(n))
            self.dma_use[q] = [0] * ndma
            self.dma_rr[q] = 0
        self.res = {}
        self.pending = {e: [] for e in self.ENGS}
        self.all_dma_toks = []
        self.n_ins = 0

    def R(self, name):
        r = self.res.get(name)
        if r is None:
            r = Res()
            self.res[name] = r
        return r

    def _deps(self, eng, reads, writes):
        need = {}
        for r in reads:
            t = self.R(r).w
            if t is not None and need.get(t[0], 0) < t[1]:
                need[t[0]] = t[1]
        for w in writes:
            rr = self.R(w)
            t = rr.w
            if t is not None and need.get(t[0], 0) < t[1]:
                need[t[0]] = t[1]
            for k, v in rr.r.items():
                if need.get(k, 0) < v:
                    need[k] = v
        wd = self.waited[eng]
        e = self.eng[eng]
        for k, v in need.items():
            if k == "s_pe" and eng == "pe":
                continue
            if wd.get(k, 0) >= v:
                continue
            wd[k] = v
            e.wait_ge(self.sems[k], v)
            self.n_ins += 1

    def _commit(self, tok, reads, writes):
        k, v = tok
        for r in reads:
            d = self.R(r).r
            if d.get(k, 0) < v:
                d[k] = v
        for w in writes:
            rr = self.R(w)
            rr.w = tok
            rr.r = {}

    def op(self, eng, fn, reads=(), writes=(), inc=True):
        self._deps(eng, reads, writes)
        ins = fn(self.eng[eng])
        self.n_ins += 1
        if not inc:
            self.pending[eng].append((reads, writes))
            return None
        self.cnt[eng] += 1
        tok = ("s_" + eng, self.cnt[eng])
        ins.then_inc(self.sems[tok[0]], 1)
        for (r, w) in self.pending[eng]:
            self._commit(tok, r, w)
        self.pending[eng] = []
        self._commit(tok, reads, writes)
        return tok

    def dma(self, q, fn, reads=(), writes=()):
        assert not self.pending[q]
        i = self.dma_rr[q]
        self.dma_rr[q] = (i + 1) % self.ndma
        key = self.dma_names[q][i]
        self._deps(q, reads, writes)
        prev = self.dma_use[q][i] * 16
        if prev > 0 and self.waited[q].get(key, 0) < prev:
            self.waited[q][key] = prev
            self.eng[q].wait_ge(self.sems[key], prev)
        self.dma_use[q][i] += 1
        tok = (key, self.dma_use[q][i] * 16)
        fn(self.eng[q]).then_inc(self.sems[key], 16)
        self.n_ins += 1
        self._commit(tok, reads, writes)
        return tok

    def barrier(self):
        for e in self.ENGS:
            assert not self.pending[e]
        targets = {}
        for e in self.ENGS:
            if self.cnt[e]:
                targets["s_" + e] = self.cnt[e]
        for q in self.dma_names:
            for i, n in enumerate(self.dma_names[q]):
                if self.dma_use[q][i]:
                    targets[n] = self.dma_use[q][i] * 16
        for e in self.ENGS:
            wd = self.waited[e]
            for k, v in targets.items():
                if k == "s_" + e:
                    continue
                if wd.get(k, 0) < v:
                    wd[k] = v
                    self.eng[e].wait_ge(self.sems[k], v)
                    self.n_ins += 1

    def wait_tok(self, eng, tok):
        k, v = tok
        if self.waited[eng].get(k, 0) < v:
            self.waited[eng][k] = v
            self.eng[eng].wait_ge(self.sems[k], v)


def AP(t, off, pat):
    return bass.AP(t, off, [list(p) for p in pat])


def _attn_perm(kp):
    idx = []
    for j in range(4):
        for r in range(2):
            h = (2 * kp + r) * 4 + j
            idx.extend(range(h * 64, h * 64 + 64))
    return np.array(idx)


def _in_col_chunks():
    q0, k0, v0, za0, zs0, x0 = 0, 1024, 1280, 1536, 2560, 3584
    ch = []
    ch.append(np.concatenate([np.arange(k0, k0 + 256), np.arange(v0, v0 + 256)]))
    for kp in range(2):
        p = _attn_perm(kp)
        ch.append(q0 + p)
        ch.append(za0 + p)
    ch.append(np.arange(x0 + 1024, x0 + 1536))
    for g in range(2):
        ch.append(np.arange(x0 + g * 512, x0 + (g + 1) * 512))
        ch.append(np.arange(zs0 + g * 512, zs0 + (g + 1) * 512))
    return ch


def _consts():
    s = np.arange(128)[:, None].astype(np.float64)
    q = np.arange(128)[None, :].astype(np.float64)
    slopes = np.exp2(-8.0 * np.arange(1, 17) / 16.0)
    masks = np.zeros((128, 4, 2, 4, 128), np.float32)
    for kv in range(4):
        for g in range(4):
            sl = slopes[kv * 4 + g]
            masks[:, kv, 0, g, :] = np.where(s <= q, np.exp(-sl * (q - s)), 0.0)
            masks[:, kv, 1, g, :] = np.where(s > q, np.exp(-sl * (128.0 + q - s)), 0.0)
    tri_le = (s <= q).astype(np.float32)
    tri_st = (s > q).astype(np.float32)
    ident = np.eye(128, dtype=np.float32)
    ones = np.ones((128, 128), np.float32)
    cb16 = np.concatenate([ident, tri_le, tri_st, ones], axis=1).astype(ml_dtypes.bfloat16)
    cf32 = np.concatenate([tri_le, ones], axis=1).astype(np.float32)
    return masks.reshape(128, 4096).astype(ml_dtypes.bfloat16), cb16, cf32


def prep_weights(inp, n_layers):
    L = n_layers
    w_in = np.asarray(inp["w_in"], np.float32)[:L]
    w_out = np.asarray(inp["w_out"], np.float32)[:L]
    chunks = _in_col_chunks()
    w_in_p = np.empty((L, N_W_CHUNKS, 128, 8, 512), np.float32)
    for c, cols in enumerate(chunks):
        w = w_in[:, :, cols]
        w_in_p[:, c] = w.reshape(L, 8, 128, 512).transpose(0, 2, 1, 3)
    w_dt = np.ascontiguousarray(w_in[:, :, 5120:5136].reshape(L, 8, 128, 16).transpose(0, 2, 1, 3))
    rows = np.concatenate([_attn_perm(0), _attn_perm(1), np.arange(1024, 2048)])
    w_out_p = np.ascontiguousarray(w_out[:, rows, :].reshape(L, 4, 4, 128, 1024).transpose(0, 1, 3, 2, 4))
    conv_w = np.asarray(inp["conv_w"], np.float32)[:L]
    conv_wT = np.ascontiguousarray(conv_w.reshape(L, 4, 12, 128).transpose(0, 3, 2, 1))
    conv_bT = np.ascontiguousarray(np.asarray(inp["conv_b"], np.float32)[:L].reshape(L, 12, 128).transpose(0, 2, 1))

    def rep(a):
        a = np.asarray(a, np.float32)[:L]
        return np.ascontiguousarray(np.broadcast_to(a[:, None, :], (L, 128, a.shape[1])))
    small = np.concatenate([rep(inp["dt_bias"]), rep(inp["a_log"]), rep(inp["d_skip"]), rep(inp["sinks"])], axis=2)
    masks, cb16, cf32 = _consts()
    return {
        "w_in_p": w_in_p, "w_dt": w_dt, "w_out_p": w_out_p, "conv_wT": conv_wT, "conv_bT": conv_bT,
        "small": np.ascontiguousarray(small), "ln_g": rep(inp["ln_g"]), "ln_b": rep(inp["ln_b"]),
        "normw": rep(inp["ssm_norm_w"]), "masks": masks, "cb16": cb16, "cf32": cf32,
    }


def build_program(n_layers=DEPTH, stop=99):
    L = n_layers
    nc = bass.Bass("TRN2", target_bir_lowering=False)
    x_d = nc.dram_tensor("x", [SEQ, D_MODEL], F32, kind="ExternalInput")
    w_in_d = nc.dram_tensor("w_in_p", [L, N_W_CHUNKS, 128, 8 * 512], F32, kind="ExternalInput")
    w_dt_d = nc.dram_tensor("w_dt", [L, 128, 8 * 16], F32, kind="ExternalInput")
    w_out_d = nc.dram_tensor("w_out_p", [L, 4, 128, 4 * 1024], F32, kind="ExternalInput")
    convw_d = nc.dram_tensor("conv_wT", [L, 128, 48], F32, kind="ExternalInput")
    convb_d = nc.dram_tensor("conv_bT", [L, 128, 12], F32, kind="ExternalInput")
    small_d = nc.dram_tensor("small", [L, 128, 64], F32, kind="ExternalInput")
    lng_d = nc.dram_tensor("ln_g", [L, 128, 1024], F32, kind="ExternalInput")
    lnb_d = nc.dram_tensor("ln_b", [L, 128, 1024], F32, kind="ExternalInput")
    normw_d = nc.dram_tensor("normw", [L, 128, 1024], F32, kind="ExternalInput")
    masks_d = nc.dram_tensor("masks", [128, 4096], BF16, kind="ExternalInput")
    cb16_d = nc.dram_tensor("cb16", [128, 512], BF16, kind="ExternalInput")
    cf32_d = nc.dram_tensor("cf32", [128, 256], F32, kind="ExternalInput")
    y_d = nc.dram_tensor("y", [SEQ, D_MODEL], F32, kind="ExternalOutput")

    with ExitStack() as st:
        def sb(name, cols, dt):
            return st.enter_context(nc.sbuf_tensor("sb_" + name, [128, cols], dt))

        def ps(name, cols, dt):
            return st.enter_context(nc.psum_tensor("ps_" + name, [128, cols], dt))

        S = Sched(nc, st)
        acc = sb("acc", NT * 1024, F32)
        xT = sb("xT", 8 * SEQ, BF16)
        cb16 = sb("cb16", 512, BF16)
        cf32 = sb("cf32", 256, F32)
        identb = cb16[:, 0:128]
        trileb = cb16[:, 128:256]
        tristb = cb16[:, 256:384]
        onesb = cb16[:, 384:512]
        trilef = cf32[:, 0:128]
        onesf = cf32[:, 128:256]
        wbuf = [sb("wbuf%d" % i, 8 * 512, BF16) for i in range(2)]
        wob = sb("wob", 4 * 1024, BF16)
        wdt = sb("wdt", 128, BF16)
        lng = sb("lng", 1024, F32)
        lnb = sb("lnb", 1024, F32)
        normw = sb("normw", 1024, F32)
        convw = sb("convw", 48, F32)
        convb = sb("convb", 12, F32)
        small = sb("small", 64, F32)
        mix4 = sb("mix4", 4 * 512, BF16)
        xb16 = sb("xb16", 1024, BF16)
        stats = sb("stats", 12, F32)
        mv = sb("mv", 2, F32)
        rstd = sb("rstd", 1, F32)
        PH_BYTES = 66 * 1024
        ph8 = sb("phase", PH_BYTES // 2, BF16)
        ph32 = ph8.bitcast(F32)

        class Carver:
            def __init__(self):
                self.off = 0

            def bf(self, n):
                o = self.off
                self.off += 2 * n
                assert self.off <= PH_BYTES, self.off
                return ph8[:, o // 2: o // 2 + n], o // 2

            def f32(self, n):
                self.off = (self.off + 3) // 4 * 4
                o = self.off
                self.off += 4 * n
                assert self.off <= PH_BYTES, self.off
                return ph32[:, o // 4: o // 4 + n], o // 4

        pj = [ps("pj%d" % i, 512, F32) for i in range(2)]
        sc = [ps("sc%d" % i, 512, F32) for i in range(2)]
        nd = [ps("nd%d" % i, 512, F32) for i in range(2)]
        st0 = ps("st0", 512, F32)
        trT = ps("tr", 1024, BF16)
        tr = [trT[:, 0:512], trT[:, 512:1024]]
        pj_rr = [0]

        def next_pj():
            i = pj_rr[0]
            pj_rr[0] = 1 - i
            return pj[i], "pj%d" % i

        S.dma("sp", lambda e: e.dma_start(out=cb16[:], in_=cb16_d[:]), writes=["cb16"])
        S.dma("sp", lambda e: e.dma_start(out=cf32[:], in_=cf32_d[:]), writes=["cf32"])

        xT3 = lambda kc, c0, n: xT[:, kc * SEQ + c0: kc * SEQ + c0 + n]

        def load_w_chunk(l, c, buf_i):
            S.dma("pool", lambda e: e.dma_start(out=wbuf[buf_i][:], in_=w_in_d[l, c]), writes=["wbuf%d" % buf_i])

        def load_wo(l, qi):
            S.dma("pool", lambda e: e.dma_start(out=wob[:], in_=w_out_d[l, qi]), writes=["wob"])

        def mm_group(items, reads, writes):
            n = len(items)
            for i, (o, a, b) in enumerate(items):
                S.op("pe", lambda e, o=o, a=a, b=b, i=i: e.matmul(o, lhsT=a, rhs=b, start=(i == 0), stop=(i == n - 1)),
                     reads=reads, writes=writes, inc=(i == n - 1))

        def inproj_fm(wb_i, col0, tg, bank, bank_name):
            items = [(bank[:, 0:512], wbuf[wb_i][:, kc * 512 + col0: kc * 512 + col0 + 128], xT3(kc, tg * 512, 512))
                     for kc in range(8)]
            mm_group(items, reads=["wbuf%d" % wb_i] + ["xT%d" % t for t in range(4 * tg, 4 * tg + 4)], writes=[bank_name])

        def inproj_tm(wb_i, col0, ncols, t, bank, bank_name):
            items = [(bank[:, 0:ncols], xT3(kc, t * 128, 128), wbuf[wb_i][:, kc * 512 + col0: kc * 512 + col0 + ncols])
                     for kc in range(8)]
            mm_group(items, reads=["wbuf%d" % wb_i, "xT%d" % t], writes=[bank_name])

        def to_xT(t):
            S.op("act", lambda e: e.activation(out=xb16[:], in_=acc[:, t * 1024:(t + 1) * 1024], func=AF.Copy),
                 reads=["acc%d" % t], writes=["xb16"])
            for half in range(2):
                for j in range(4):
                    kc = half * 4 + j
                    S.op("pe", lambda e, kc=kc, j=j, half=half: e.transpose(tr[half][:, j * 128:(j + 1) * 128],
                                                                           xb16[:, kc * 128:(kc + 1) * 128], identb),
                         reads=["xb16", "cb16"], writes=["tr"], inc=(j == 3))
                o = AP(xT, half * 4 * SEQ + t * 128, [[8 * SEQ, 128], [SEQ, 4], [1, 128]])
                i_ = AP(trT, half * 512, [[1024, 128], [128, 4], [1, 128]])
                S.op("dve", lambda e, o=o, i_=i_: e.tensor_copy(out=o, in_=i_), reads=["tr"], writes=["xT%d" % t])

        def outproj_partial(l, first, tg):
            for b in range(4):
                t = 4 * tg + b
                for ch in range(2):
                    bank, bn = next_pj()
                    items = [(bank[:, 0:512], mix4[:, c * 512 + b * 128: c * 512 + (b + 1) * 128],
                              wob[:, c * 1024 + ch * 512: c * 1024 + (ch + 1) * 512]) for c in range(4)]
                    mm_group(items, reads=["mix4", "wob"], writes=[bn])
                    a_ = acc[:, t * 1024 + ch * 512: t * 1024 + (ch + 1) * 512]
                    if first:
                        S.op("dve", lambda e, a_=a_, bank=bank: e.scalar_tensor_tensor(
                            out=a_, in0=a_, scalar=float(ALPHA), in1=bank[:, 0:512], op0=ALU.mult, op1=ALU.add),
                            reads=[bn, "acc%d" % t], writes=["acc%d" % t])
                    else:
                        S.op("dve", lambda e, a_=a_, bank=bank: e.tensor_tensor(out=a_, in0=a_, in1=bank[:, 0:512], op=ALU.add),
                             reads=[bn, "acc%d" % t], writes=["acc%d" % t])

        def layernorm_tile(l, t, last):
            a_ = acc[:, t * 1024:(t + 1) * 1024]
            rw = ["acc%d" % t]
            for h in range(2):
                S.op("dve", lambda e, h=h: e.bn_stats(out=stats[:, h * 6:(h + 1) * 6], in_=acc[:, t * 1024 + h * 512: t * 1024 + (h + 1) * 512]),
                     reads=rw, writes=["stats"])
            S.op("dve", lambda e: e.bn_aggr(out=mv[:], in_=stats[:]), reads=["stats"], writes=["mv"])
            S.op("act", lambda e: e.activation(out=rstd[:], in_=mv[:, 1:2], func=AF.Ln, bias=float(LN_EPS), scale=1.0),
                 reads=["mv"], writes=["rstd"])
            S.op("act", lambda e: e.activation(out=rstd[:], in_=rstd[:], func=AF.Exp, scale=-0.5), reads=["rstd"], writes=["rstd"])
            S.op("dve", lambda e: e.tensor_scalar(out=a_, in0=a_, scalar1=mv[:, 0:1], scalar2=rstd[:, 0:1],
                                                  op0=ALU.subtract, op1=ALU.mult), reads=rw + ["mv", "rstd"], writes=rw)
            S.op("pool", lambda e: e.tensor_tensor(out=a_, in0=a_, in1=lng[:], op=ALU.mult), reads=rw + ["lng"], writes=rw)
            S.op("pool", lambda e: e.tensor_tensor(out=a_, in0=a_, in1=lnb[:], op=ALU.add), reads=rw + ["lnb"], writes=rw)
            if last:
                S.all_dma_toks.append(S.dma("sp", lambda e: e.dma_start(out=y_d[t * 128:(t + 1) * 128, :], in_=a_), reads=rw))
            else:
                to_xT(t)

        def finalize():
            for t in range(NT):
                S.all_dma_toks.append(S.dma("sp", lambda e, t=t: e.dma_start(out=y_d[t * 128:(t + 1) * 128, :], in_=acc[:, t * 1024:(t + 1) * 1024]),
                                            reads=["acc%d" % t]))
            S.barrier()
            for tok in S.all_dma_toks:
                S.wait_tok("sp", tok)
            print("instructions emitted (stopped):", S.n_ins)
            return nc

        for t in range(NT):
            S.dma("sp", lambda e, t=t: e.dma_start(out=acc[:, t * 1024:(t + 1) * 1024], in_=x_d[t * 128:(t + 1) * 128, :]),
                  writes=["acc%d" % t])
        for t in range(NT):
            to_xT(t)

        if stop == 0:
            return finalize()
        for l in range(L):
            last = (l == L - 1)
            S.dma("sp", lambda e: e.dma_start(out=lng[:], in_=lng_d[l]), writes=["lng"])
            S.dma("sp", lambda e: e.dma_start(out=lnb[:], in_=lnb_d[l]), writes=["lnb"])
            S.dma("sp", lambda e: e.dma_start(out=normw[:], in_=normw_d[l]), writes=["normw"])
            S.dma("sp", lambda e: e.dma_start(out=convw[:], in_=convw_d[l]), writes=["convw"])
            S.dma("sp", lambda e: e.dma_start(out=convb[:], in_=convb_d[l]), writes=["convb"])
            S.dma("sp", lambda e: e.dma_start(out=small[:], in_=small_d[l]), writes=["small"])
            S.dma("pool", lambda e: e.dma_start(out=wdt[:], in_=w_dt_d[l]), writes=["wdt"])
            dtb = small[:, 0:16]
            alog = small[:, 16:32]
            dsk = small[:, 32:48]
            snk = small[:, 48:64]

            cv = Carver()
            masks, _ = cv.bf(4096)
            kT, kT_o = cv.bf(2 * SEQ)
            vt, vt_o = cv.bf(NT * 256)
            _q = [cv.bf(4 * 512) for _ in range(2)]
            _z = [cv.bf(4 * 512) for _ in range(2)]
            q4 = [x_[0] for x_ in _q]
            q4_o = [x_[1] for x_ in _q]
            z4 = [x_[0] for x_ in _z]
            z4_o = [x_[1] for x_ in _z]
            Eb = [cv.bf(512) for _ in range(8)]
            sinkexp, se_o = cv.f32(2 * 512)
            dS, dS_o = cv.f32(512)
            wgt, wgt_o = cv.f32(512)
            es, es_o = cv.f32(16)
            S.barrier()
            S.dma("sp", lambda e: e.dma_start(out=masks, in_=masks_d[:]), reads=[], writes=["masks"])
            load_w_chunk(l, 0, 0)
            load_w_chunk(l, 1, 1)
            S.op("act", lambda e: e.activation(out=es, in_=snk, func=AF.Exp), reads=["small"], writes=["es"])
            for kp in range(2):
                for r in range(2):
                    o = AP(ph32, se_o + kp * 512 + r * 64 * (PH_BYTES // 4), [[PH_BYTES // 4, 64], [128, 4], [1, 128]])
                    hh = (2 * kp + r) * 4
                    i_ = AP(ph32, es_o + hh + r * 64 * (PH_BYTES // 4), [[PH_BYTES // 4, 64], [1, 4], [0, 128]])
                    S.op("dve", lambda e, o=o, i_=i_: e.tensor_copy(out=o, in_=i_), reads=["es"], writes=["sinkexp"])
            for tg in range(NG):
                for c in range(2):
                    bank, bn = next_pj()
                    inproj_fm(0, c * 128, tg, bank, bn)
                    S.op("dve", lambda e, c=c, tg=tg, bank=bank: e.tensor_copy(
                        out=kT[:, c * SEQ + tg * 512: c * SEQ + (tg + 1) * 512], in_=bank[:, 0:512]),
                        reads=[bn], writes=["kT%d" % tg])
            for t in range(NT):
                bank, bn = next_pj()
                inproj_tm(0, 256, 256, t, bank, bn)
                S.op("act", lambda e, t=t, bank=bank: e.activation(out=vt[:, t * 256:(t + 1) * 256], in_=bank[:, 0:256], func=AF.Copy),
                     reads=[bn], writes=["v%d" % t])
            if stop == 1:
                return finalize()
            trF = trT.bitcast(F32)
            sbanks = [(sc[0], "sc0"), (sc[1], "sc1"), (st0, "st0"), (trF, "tr")]

            def att_S(kp, tg, b):
                n = 4 * tg + b
                par = b % 2
                k_ = 0
                for r in range(2):
                    kv = 2 * kp + r
                    rows = slice(r * 64, r * 64 + 64)
                    blocks = ([n - 1] if n > 0 else []) + [n]
                    for sblk in blocks:
                        pc = 0 if sblk == n else 1
                        E, _eo = Eb[par * 4 + k_]
                        en = "E%d" % (par * 4 + k_)
                        scb, scn = sbanks[k_]
                        k_ += 1
                        lhsT = kT[rows, kp * SEQ + sblk * 128: kp * SEQ + (sblk + 1) * 128]
                        rhs = AP(ph8, q4_o[tg % 2] + b * 128 + r * 64 * (PH_BYTES // 2), [[PH_BYTES // 2, 64], [512, 4], [1, 128]])
                        S.op("pe", lambda e, scb=scb, lhsT=lhsT, rhs=rhs: e.matmul(scb[:, 0:512], lhsT=lhsT, rhs=rhs, start=True, stop=True),
                             reads=["kT%d" % (sblk // 4), "q4_%d" % (tg % 2)], writes=[scn])
                        S.op("act", lambda e, E=E, scb=scb: e.activation(out=E, in_=scb[:, 0:512], func=AF.Exp, scale=0.125),
                             reads=[scn], writes=[en])
                        m_ = masks[:, (kv * 2 + pc) * 512:(kv * 2 + pc + 1) * 512]
                        S.op("dve", lambda e, E=E, m_=m_: e.tensor_tensor(out=E, in0=E, in1=m_, op=ALU.mult),
                             reads=[en, "masks"], writes=[en])

            def att_V(kp, tg, b):
                n = 4 * tg + b
                par = b % 2
                k_ = 0
                for r in range(2):
                    kv = 2 * kp + r
                    rows = slice(r * 64, r * 64 + 64)
                    blocks = ([n - 1] if n > 0 else []) + [n]
                    for bi, sblk in enumerate(blocks):
                        E, _eo = Eb[par * 4 + k_]
                        en = "E%d" % (par * 4 + k_)
                        k_ += 1
                        first_b = (bi == 0)
                        last_b = (bi == len(blocks) - 1)
                        vl = vt[:, sblk * 256 + kv * 64: sblk * 256 + (kv + 1) * 64]
                        S.op("pe", lambda e, vl=vl, E=E, rows=rows, first_b=first_b, last_b=last_b: e.matmul(
                            nd[0][rows, 0:512], lhsT=vl, rhs=E, start=first_b, stop=last_b),
                            reads=[en, "v%d" % sblk], writes=["nd0"], inc=False)
                        S.op("pe", lambda e, E=E, rows=rows, first_b=first_b, last_b=last_b: e.matmul(
                            nd[1][rows, 0:512], lhsT=onesb[:, 0:64], rhs=E, start=first_b, stop=last_b),
                            reads=[en, "cb16"], writes=["nd1"], inc=True)
                S.op("dve", lambda e: e.tensor_tensor(out=dS, in0=nd[1][:, 0:512], in1=sinkexp[:, kp * 512:(kp + 1) * 512], op=ALU.add),
                     reads=["nd1", "sinkexp"], writes=["dS"])
                S.op("act", lambda e: e.activation(out=dS, in_=dS, func=AF.Ln), reads=["dS"], writes=["dS"])
                S.op("act", lambda e: e.activation(out=dS, in_=dS, func=AF.Exp, scale=-1.0), reads=["dS"], writes=["dS"])
                S.op("dve", lambda e: e.tensor_tensor(out=wgt, in0=nd[0][:, 0:512], in1=dS, op=ALU.mult),
                     reads=["nd0", "dS"], writes=["wgt"])
                zv = AP(ph8, z4_o[tg % 2] + b * 128, [[PH_BYTES // 2, 128], [512, 4], [1, 128]])
                w3 = AP(ph32, wgt_o, [[PH_BYTES // 4, 128], [128, 4], [1, 128]])
                mo = AP(mix4, b * 128, [[2048, 128], [512, 4], [1, 128]])
                S.op("pool", lambda e, zv=zv, w3=w3, mo=mo: e.tensor_tensor(out=mo, in0=w3, in1=zv, op=ALU.mult),
                     reads=["wgt", "z4_%d" % (tg % 2)], writes=["mix4"])

            def q_piece(kp, tg, j):
                bank, bn = next_pj()
                inproj_fm(1, j * 128, tg, bank, bn)
                S.op("dve", lambda e, bank=bank: e.tensor_copy(out=q4[tg % 2][:, j * 512:(j + 1) * 512], in_=bank[:, 0:512]),
                     reads=[bn], writes=["q4_%d" % (tg % 2)])

            def za_piece(kp, tg, j):
                bank, bn = next_pj()
                inproj_fm(0, j * 128, tg, bank, bn)
                S.op("act", lambda e, bank=bank: e.activation(out=z4[tg % 2][:, j * 512:(j + 1) * 512], in_=bank[:, 0:512], func=AF.Silu),
                     reads=[bn], writes=["z4_%d" % (tg % 2)])

            for kp in range(2):
                load_w_chunk(l, 2 + 2 * kp, 0)
                load_wo(l, kp)
                for j in range(4):
                    q_piece(kp, 0, j)
                for j in range(4):
                    za_piece(kp, 0, j)
                for tg in range(NG):
                    nxt = tg + 1 < NG
                    att_S(kp, tg, 0)
                    for b in range(4):
                        if b < 3:
                            att_S(kp, tg, b + 1)
                        att_V(kp, tg, b)
                        if nxt:
                            q_piece(kp, tg + 1, b)
                            za_piece(kp, tg + 1, b)
                    if tg == NG - 2 and kp == 0:
                        load_w_chunk(l, 3, 1)
                    outproj_partial(l, kp == 0, tg)
            if stop == 2:
                return finalize()
            cv = Carver()
            bcT, bc_o = cv.bf(4 * SEQ)
            ubuf, u_o = cv.bf(4 * 515)
            xsT, xsT_o = cv.bf(4 * 512)
            diag, diag_o = cv.bf(16 * 128)
            xdt = [cv.bf(512)[0] for _ in range(2)]
            xdtd = [cv.bf(512)[0] for _ in range(2)]
            xsD = [cv.bf(512)[0] for _ in range(2)]
            zs4, _ = cv.bf(4 * 512)
            Btm = [cv.bf(128)[0] for _ in range(2)]
            cbm, cbm_o = cv.bf(128)
            rhsD, rhsD_o = cv.bf(1024)
            _e8 = [cv.bf(1024) for _ in range(2)]
            E8 = [x_[0] for x_ in _e8]
            E8_o = [x_[1] for x_ in _e8]
            Sbf, _ = cv.bf(512)
            gn, _ = cv.bf(512)
            dt16, dt_o = cv.f32(256)
            dtA, dtA_o = cv.f32(256)
            ea, ea_o = cv.f32(256)
            dtdte, dd_o = cv.f32(256)
            cd, cd_o = cv.f32(256)
            tmpA, _ = cv.f32(256)
            tmpB, _ = cv.f32(256)
            nega, nega_o2 = cv.f32(16)
            S32, _ = cv.f32(512)
            t1, t1_o = cv.f32(512)
            junk = ph8[:, 2 * t1_o: 2 * t1_o + 512]
            yb, _ = cv.f32(512)
            ss, _ = cv.f32(1)
            rs, _ = cv.f32(1)
            PB = PH_BYTES
            S.barrier()
            S.op("pool", lambda e: e.memset(ubuf, 0.0), reads=[], writes=["ubuf"])
            load_w_chunk(l, 5, 1)
            for t in range(NT):
                items = [(st0[:, t * 16:(t + 1) * 16], xT3(kc, t * 128, 128), wdt[:, kc * 16:(kc + 1) * 16]) for kc in range(8)]
                mm_group(items, reads=["wdt", "xT%d" % t], writes=["st0"])
            dtb_b = AP(small, 0, [[64, 128], [0, 16], [1, 16]])
            v3 = lambda o_: AP(ph32, o_, [[PB // 4, 128], [16, 16], [1, 16]])
            S.op("dve", lambda e: e.tensor_tensor(out=v3(dt_o), in0=AP(st0, 0, [[512, 128], [16, 16], [1, 16]]), in1=dtb_b, op=ALU.add),
                 reads=["st0", "small"], writes=["dt16"])
            S.op("dve", lambda e: e.tensor_scalar(out=tmpA, in0=dt16, scalar1=-1.0, scalar2=None, op0=ALU.mult), reads=["dt16"], writes=["tmpA"])
            S.op("dve", lambda e: e.tensor_tensor(out=tmpA, in0=tmpA, in1=dt16, op=ALU.max), reads=["dt16", "tmpA"], writes=["tmpA"])
            S.op("act", lambda e: e.activation(out=tmpA, in_=tmpA, func=AF.Exp, scale=-1.0), reads=["tmpA"], writes=["tmpA"])
            S.op("act", lambda e: e.activation(out=tmpA, in_=tmpA, func=AF.Ln, bias=1.0), reads=["tmpA"], writes=["tmpA"])
            S.op("dve", lambda e: e.scalar_tensor_tensor(out=dt16, in0=dt16, scalar=0.0, in1=tmpA, op0=ALU.max, op1=ALU.add),
                 reads=["dt16", "tmpA"], writes=["dt16"])
            S.op("act", lambda e: e.activation(out=nega, in_=alog, func=AF.Exp), reads=["small"], writes=["nega"])
            S.op("dve", lambda e: e.tensor_scalar(out=nega, in0=nega, scalar1=-1.0, scalar2=None, op0=ALU.mult), reads=["nega"], writes=["nega"])
            nega_o = nega_o2
            S.op("dve", lambda e: e.tensor_tensor(out=v3(dtA_o), in0=v3(dt_o), in1=AP(ph32, nega_o, [[PB // 4, 128], [0, 16], [1, 16]]), op=ALU.mult),
                 reads=["dt16", "nega"], writes=["dtA"])
            S.op("pe", lambda e: e.matmul(st0[:, 0:256], lhsT=trilef, rhs=dtA, start=True, stop=True), reads=["cf32", "dtA", "dt16"], writes=["st0"])
            S.op("pe", lambda e: e.matmul(st0[:, 256:512], lhsT=onesf, rhs=dtA, start=True, stop=True), reads=["cf32", "dtA"], writes=["st0"])
            S.op("act", lambda e: e.activation(out=ea, in_=st0[:, 0:256], func=AF.Exp), reads=["st0"], writes=["ea"])
            S.op("act", lambda e: e.activation(out=cd, in_=st0[:, 256:512], func=AF.Exp), reads=["st0"], writes=["cd"])
            S.op("act", lambda e: e.activation(out=tmpB, in_=st0[:, 0:256], func=AF.Identity), reads=["st0"], writes=["tmpB"])
            S.op("dve", lambda e: e.tensor_tensor(out=tmpB, in0=st0[:, 256:512], in1=tmpB, op=ALU.subtract), reads=["st0", "tmpB"], writes=["tmpB"])
            S.op("act", lambda e: e.activation(out=tmpB, in_=tmpB, func=AF.Exp), reads=["tmpB"], writes=["tmpB"])
            S.op("dve", lambda e: e.tensor_tensor(out=dtdte, in0=dt16, in1=tmpB, op=ALU.mult), reads=["dt16", "tmpB"], writes=["dtdte"])

            if stop == 3:
                return finalize()

            def build_diag(cc0):
                for j4 in range(4):
                    for tap in range(4):
                        o = diag[:, (j4 * 4 + tap) * 128:(j4 * 4 + tap + 1) * 128]
                        s_ = convw[:, (cc0 + j4) * 4 + tap:(cc0 + j4) * 4 + tap + 1]
                        S.op("pool", lambda e, o=o, s_=s_: e.tensor_scalar(out=o, in0=identb, scalar1=s_, scalar2=None, op0=ALU.mult),
                             reads=["cb16", "convw"], writes=["diag"])

            def conv_group(wb_i, cc0, tg, dst, dst_stride, dst_res):
                if tg > 0:
                    S.op("dve", lambda e: e.tensor_copy(out=AP(ph8, u_o, [[PB // 2, 128], [515, 4], [1, 3]]),
                                                        in_=AP(ph8, u_o + 512, [[PB // 2, 128], [515, 4], [1, 3]])),
                         reads=["ubuf"], writes=["ubuf"])
                else:
                    S.op("pool", lambda e: e.memset(AP(ph8, u_o, [[PB // 2, 128], [515, 4], [1, 3]]), 0.0), reads=[], writes=["ubuf"])
                for j in range(4):
                    bank, bn = next_pj()
                    inproj_fm(wb_i, j * 128, tg, bank, bn)
                    S.op("act", lambda e, j=j, bank=bank: e.activation(out=ubuf[:, j * 515 + 3: j * 515 + 515], in_=bank[:, 0:512], func=AF.Copy),
                         reads=[bn], writes=["ubuf"])
                for j in range(4):
                    bank, bn = next_pj()
                    items = [(bank[:, 0:512], diag[:, (j * 4 + tap) * 128:(j * 4 + tap + 1) * 128],
                              ubuf[:, j * 515 + tap: j * 515 + tap + 512]) for tap in range(4)]
                    mm_group(items, reads=["diag", "ubuf"], writes=[bn])
                    o = dst(j)
                    S.op("act", lambda e, o=o, bank=bank, j=j: e.activation(out=o, in_=bank[:, 0:512], func=AF.Silu,
                                                                             bias=convb[:, cc0 + j: cc0 + j + 1]),
                         reads=[bn, "convb"], writes=[dst_res])

            build_diag(8)
            for tg in range(NG):
                conv_group(1, 8, tg, lambda j, tg=tg: bcT[:, j * SEQ + tg * 512: j * SEQ + (tg + 1) * 512], None, "bcT%d" % tg)
            if stop == 4:
                return finalize()
            def ssd_I(g, tg, b):
                c = 4 * tg + b
                t = c
                p_ = b % 2
                for j in range(4):
                    S.op("pe", lambda e, j=j: e.transpose(tr[0][:, j * 128:(j + 1) * 128], xsT[:, j * 512 + b * 128: j * 512 + (b + 1) * 128], identb),
                         reads=["xsT", "cb16"], writes=["tr"], inc=(j == 3))
                tr3 = AP(trT, 0, [[1024, 128], [64, 8], [1, 64]])
                for dst_, nm_, sc_o in ((xdt[p_], "xdt%d" % p_, dt_o), (xdtd[p_], "xdtd%d" % p_, dd_o)):
                    S.op("dve", lambda e, dst_=dst_, sc_o=sc_o: e.tensor_tensor(
                        out=dst_.rearrange("p (h d) -> p h d", h=8), in0=tr3, in1=hb(g, sc_o, t), op=ALU.mult),
                        reads=["tr", "dt16", "dtdte"], writes=[nm_])
                dsk_b = AP(small, 32 + 8 * g, [[64, 128], [1, 8], [0, 64]])
                S.op("dve", lambda e: e.tensor_tensor(out=xsD[p_].rearrange("p (h d) -> p h d", h=8), in0=tr3, in1=dsk_b, op=ALU.mult),
                     reads=["tr", "small"], writes=["xsD%d" % p_])
                S.op("pe", lambda e: e.transpose(tr[1][:, 0:128], bcT[:, g * SEQ + c * 128: g * SEQ + (c + 1) * 128], identb),
                     reads=["bcT%d" % tg, "cb16"], writes=["tr"])
                S.op("act", lambda e: e.activation(out=Btm[p_], in_=tr[1][:, 0:128], func=AF.Copy), reads=["tr"], writes=["Btm%d" % p_])
                bank, bn = next_pj()
                S.op("pe", lambda e, bank=bank: e.matmul(bank[:, 0:128], lhsT=bcT[:, g * SEQ + c * 128: g * SEQ + (c + 1) * 128],
                                                         rhs=bcT[:, (2 + g) * SEQ + c * 128: (2 + g) * SEQ + (c + 1) * 128], start=True, stop=True),
                     reads=["bcT%d" % tg], writes=[bn])
                S.op("dve", lambda e, bank=bank: e.tensor_tensor(out=cbm, in0=bank[:, 0:128], in1=trileb, op=ALU.mult),
                     reads=[bn, "cb16"], writes=["cbm"])
                S.op("pool", lambda e: e.tensor_tensor(out=AP(ph8, rhsD_o, [[PB // 2, 128], [128, 8], [1, 128]]),
                                                       in0=AP(cf32, 0, [[256, 128], [0, 8], [1, 128]]),
                                                       in1=AP(ph32, dtA_o + t * 16 + 8 * g, [[PB // 4, 128], [1, 8], [0, 128]]), op=ALU.mult),
                     reads=["cf32", "dtA"], writes=["rhsD"])
                eo = E8_o[p_]
                for hf in range(2):
                    S.op("pe", lambda e, hf=hf: e.matmul(sc[hf][:, 0:512], lhsT=tristb, rhs=rhsD[:, hf * 512:(hf + 1) * 512], start=True, stop=True),
                         reads=["rhsD", "cb16"], writes=["sc%d" % hf])
                    S.op("act", lambda e, hf=hf: e.activation(out=E8[p_][:, hf * 512:(hf + 1) * 512], in_=sc[hf][:, 0:512], func=AF.Exp),
                         reads=["sc%d" % hf], writes=["E8%d" % p_])
                S.op("dve", lambda e: e.tensor_tensor(out=AP(ph8, eo, [[PB // 2, 128], [128, 8], [1, 128]]),
                                                      in0=AP(ph8, eo, [[PB // 2, 128], [128, 8], [1, 128]]),
                                                      in1=AP(ph8, cbm_o, [[PB // 2, 128], [0, 8], [1, 128]]), op=ALU.mult),
                     reads=["E8%d" % p_, "cbm"], writes=["E8%d" % p_])

            def ssd_D(g, tg, b):
                c = 4 * tg + b
                t = c
                p_ = b % 2
                G_ = E8[p_]
                S.op("pe", lambda e: e.matmul(nd[0][:, 0:512], lhsT=identb, rhs=xsD[p_], start=True, stop=False),
                     reads=["xsD%d" % p_, "cb16"], writes=["nd0"], inc=False)
                for h in range(8):
                    S.op("pe", lambda e, h=h: e.matmul(nd[0][:, h * 64:(h + 1) * 64], lhsT=G_[:, h * 128:(h + 1) * 128],
                                                       rhs=xdt[p_][:, h * 64:(h + 1) * 64], start=False, stop=(h == 7)),
                         reads=["E8%d" % p_, "xdt%d" % p_], writes=["nd0"], inc=(h == 7))
                S.op("pe", lambda e: e.matmul(nd[1][:, 0:512], lhsT=bcT[:, (2 + g) * SEQ + c * 128: (2 + g) * SEQ + (c + 1) * 128],
                                              rhs=Sbf, start=True, stop=True), reads=["bcT%d" % tg, "Sbf"], writes=["nd1"])
                if c < NT - 1:
                    S.op("pe", lambda e: e.matmul(st0[:, 0:512], lhsT=Btm[p_], rhs=xdtd[p_], start=True, stop=True),
                         reads=["Btm%d" % p_, "xdtd%d" % p_], writes=["st0"])
                    S.op("dve", lambda e: e.tensor_tensor(out=S32.rearrange("p (h d) -> p h d", h=8), in0=S32.rearrange("p (h d) -> p h d", h=8),
                                                          in1=hb(g, cd_o, t), op=ALU.mult), reads=["S32", "cd"], writes=["S32"])
                    S.op("dve", lambda e: e.tensor_tensor(out=Sbf, in0=S32, in1=st0[:, 0:512], op=ALU.add), reads=["S32", "st0"], writes=["Sbf"])
                    S.op("dve", lambda e: e.tensor_tensor(out=S32, in0=S32, in1=st0[:, 0:512], op=ALU.add), reads=["S32", "st0"], writes=["S32"])
                S.op("dve", lambda e: e.tensor_tensor(out=t1.rearrange("p (h d) -> p h d", h=8), in0=AP(nd[1], 0, [[512, 128], [64, 8], [1, 64]]),
                                                      in1=hb(g, ea_o, t), op=ALU.mult), reads=["nd1", "ea"], writes=["t1"])
                S.op("dve", lambda e: e.tensor_tensor(out=yb, in0=nd[0][:, 0:512], in1=t1, op=ALU.add), reads=["nd0", "t1"], writes=["yb"])
                S.op("dve", lambda e: e.tensor_tensor(out=yb, in0=yb, in1=zs4[:, b * 512:(b + 1) * 512], op=ALU.mult), reads=["yb", "zs4_%d" % b], writes=["yb"])
                S.op("act", lambda e: e.activation(out=junk, in_=yb, func=AF.Square, accum_out=ss), reads=["yb", "t1"], writes=["t1", "ss"])
                S.op("act", lambda e: e.activation(out=rs, in_=ss, func=AF.Ln, bias=float(RMS_EPS), scale=1.0 / 512.0), reads=["ss"], writes=["rs"])
                S.op("act", lambda e: e.activation(out=rs, in_=rs, func=AF.Exp, scale=-0.5), reads=["rs"], writes=["rs"])

            def ssd_D2(g, tg, b):
                S.op("dve", lambda e: e.scalar_tensor_tensor(out=gn, in0=yb, scalar=rs[:, 0:1], in1=normw[:, g * 512:(g + 1) * 512],
                                                             op0=ALU.mult, op1=ALU.mult), reads=["yb", "rs", "normw"], writes=["gn"])
                for j in range(4):
                    S.op("pe", lambda e, j=j: e.transpose(tr[1][:, j * 128:(j + 1) * 128], gn[:, j * 128:(j + 1) * 128], identb),
                         reads=["gn", "cb16"], writes=["tr"], inc=(j == 3))
                S.op("act", lambda e: e.activation(out=AP(mix4, b * 128, [[2048, 128], [512, 4], [1, 128]]),
                                                   in_=AP(trT, 512, [[1024, 128], [128, 4], [1, 128]]), func=AF.Copy),
                     reads=["tr"], writes=["mix4"])

            hb = lambda g, o_, t: AP(ph32, o_ + t * 16 + 8 * g, [[PB // 4, 128], [1, 8], [0, 64]])
            UB = lambda c0, n_: AP(ph8, u_o + c0, [[PB // 2, 128], [515, 4], [1, n_]])

            def xs_halo(tg):
                allu = ["ubuf%d" % j for j in range(4)] + ["ubuf"]
                if tg > 0:
                    S.op("dve", lambda e: e.tensor_copy(out=UB(0, 3), in_=UB(512, 3)), reads=allu, writes=allu)
                else:
                    S.op("pool", lambda e: e.memset(UB(0, 3), 0.0), reads=[], writes=allu)

            def xs_inproj_piece(g, tg, j):
                bank, bn = next_pj()
                inproj_fm(0, j * 128, tg, bank, bn)
                S.op("act", lambda e, bank=bank: e.activation(out=ubuf[:, j * 515 + 3: j * 515 + 515], in_=bank[:, 0:512], func=AF.Copy),
                     reads=[bn], writes=["ubuf%d" % j])

            def xs_conv_piece(g, tg, j):
                bank, bn = next_pj()
                items = [(bank[:, 0:512], diag[:, (j * 4 + tap) * 128:(j * 4 + tap + 1) * 128],
                          ubuf[:, j * 515 + tap: j * 515 + tap + 512]) for tap in range(4)]
                mm_group(items, reads=["diag", "ubuf%d" % j], writes=[bn])
                S.op("act", lambda e, bank=bank: e.activation(out=xsT[:, j * 512:(j + 1) * 512], in_=bank[:, 0:512], func=AF.Silu,
                                                              bias=convb[:, 4 * g + j: 4 * g + j + 1]),
                     reads=[bn, "convb"], writes=["xsT"])

            def z_piece(g, tg, b):
                bank, bn = next_pj()
                inproj_tm(1, 0, 512, 4 * tg + b, bank, bn)
                S.op("act", lambda e, bank=bank: e.activation(out=zs4[:, b * 512:(b + 1) * 512], in_=bank[:, 0:512], func=AF.Silu),
                     reads=[bn], writes=["zs4_%d" % b])

            for g in range(2):
                load_w_chunk(l, 6 + 2 * g, 0)
                load_w_chunk(l, 7 + 2 * g, 1)
                load_wo(l, 2 + g)
                build_diag(4 * g)
                S.op("pool", lambda e: e.memset(S32, 0.0), reads=[], writes=["S32"])
                S.op("pool", lambda e: e.memset(Sbf, 0.0), reads=[], writes=["Sbf"])
                xs_halo(0)
                for j in range(4):
                    xs_inproj_piece(g, 0, j)
                for j in range(4):
                    xs_conv_piece(g, 0, j)
                for b in range(4):
                    z_piece(g, 0, b)
                for tg in range(NG):
                    nxt = tg + 1 < NG
                    ssd_I(g, tg, 0)
                    ssd_I(g, tg, 1)
                    ssd_D(g, tg, 0)
                    if nxt:
                        xs_halo(tg + 1)
                    ssd_I(g, tg, 2)
                    ssd_D2(g, tg, 0)
                    ssd_D(g, tg, 1)
                    if nxt:
                        xs_inproj_piece(g, tg + 1, 0)
                        xs_inproj_piece(g, tg + 1, 1)
                    ssd_I(g, tg, 3)
                    ssd_D2(g, tg, 1)
                    ssd_D(g, tg, 2)
                    if nxt:
                        xs_inproj_piece(g, tg + 1, 2)
                        xs_inproj_piece(g, tg + 1, 3)
                        z_piece(g, tg + 1, 0)
                        z_piece(g, tg + 1, 1)
                    ssd_D2(g, tg, 2)
                    ssd_D(g, tg, 3)
                    if nxt:
                        for j in range(4):
                            xs_conv_piece(g, tg + 1, j)
                        z_piece(g, tg + 1, 2)
                    ssd_D2(g, tg, 3)
                    if nxt:
                        z_piece(g, tg + 1, 3)
                    outproj_partial(l, False, tg)
                    if g == 1:
                        for b in range(4):
                            layernorm_tile(l, 4 * tg + b, last)
        for tok in S.all_dma_toks:
            S.wait_tok("sp", tok)
        print("instructions emitted:", S.n_ins)
    return nc


_CACHE = {}


def run(inputs, n_layers=DEPTH):
    x = np.asarray(inputs["x"], np.float32)
    w = prep_weights(inputs, n_layers)
    if n_layers not in _CACHE:
        _CACHE[n_layers] = build_program(n_layers)
    nc = _CACHE[n_layers]
    in_maps = []
    for c in range(8):
        m = dict(w)
        m["x"] = np.ascontiguousarray(x[c])
        in_maps.append(m)
    res = run_bass_kernel_spmd(nc, in_maps, core_ids=list(range(8)))
    return np.stack([np.asarray(r["y"], np.float32) for r in res.results], axis=0)


def kernel(**inputs):
    return run(inputs, DEPTH)
```
